# Optimizing a Trainium2 kernel written in Bass

```python
import math
import jax, jax.numpy as jnp
from jax import lax
import numpy as np

D_MODEL = 4096
BATCH = 2
SEQ = 8192
DEPTH = 2

N_META = 16
PREFIX = 128
CHUNK = 64
SB_BLOCK = 128
EPS = 1e-6

GLA_HEADS = 4
GLA_DK = 128
GLA_DV = 256
GLA_RANK = 16
GLA_TAU = 16.0
GDN_HEADS = 12
GDN_DK = 128
GDN_DV = 128
CONV_K = 4
SB_HEADS = 12
SB_D = 128

GLA_QK = GLA_HEADS * GLA_DK
GLA_W = GLA_HEADS * GLA_DV
GDN_QK = GDN_HEADS * GDN_DK
GDN_W = GDN_HEADS * GDN_DV
GDN_CONV_CH = 2 * GDN_QK + GDN_W
SB_W = SB_HEADS * SB_D
D_MIX = GLA_W + GDN_W + SB_W
IN_SPLITS = (GLA_QK, GLA_QK, GLA_W, GLA_W, GLA_RANK,
             GDN_QK, GDN_QK, GDN_W, GDN_W, GDN_HEADS, GDN_HEADS,
             SB_W, SB_W, SB_W, SB_W)
D_IN = sum(IN_SPLITS)

kernel_name = 'hymba_gla_gdn_stickbreak_hybrid'


def rmsnorm(x, g):
    xf = x.astype(jnp.float32)
    y = xf * lax.rsqrt(jnp.mean(xf * xf, axis=-1, keepdims=True) + EPS)
    return (y * g.astype(jnp.float32)).astype(x.dtype)


def l2norm(x):
    return x * lax.rsqrt(jnp.sum(x * x, axis=-1, keepdims=True) + EPS)


def to_chunks(x, heads):
    b, l, hd = x.shape
    return x.reshape(b, l // CHUNK, CHUNK, heads, hd // heads).transpose(1, 0, 3, 2, 4)


def from_chunks(x):
    n, b, h, c, d = x.shape
    return x.transpose(1, 0, 3, 2, 4).reshape(b, n * c, h, d)


def causal_conv(x, w):
    return lax.conv_general_dilated(
        x, w[:, None, :].astype(x.dtype), window_strides=(1,),
        padding=[(CONV_K - 1, 0)], dimension_numbers=('NWC', 'WIO', 'NWC'),
        feature_group_count=x.shape[-1])


def gla_group(q, k, v, r, lr, w_gate, b_gate, norm_g):
    f32 = jnp.float32
    bsz, seq, _ = q.shape
    log_a = jax.nn.log_sigmoid((lr @ w_gate + b_gate).astype(f32)) / GLA_TAU
    qc = to_chunks(q.astype(f32) * GLA_DK ** -0.5, GLA_HEADS)
    kc = to_chunks(k.astype(f32), GLA_HEADS)
    vc = to_chunks(v.astype(f32), GLA_HEADS)
    gc = to_chunks(log_a, GLA_HEADS)
    idx = jnp.arange(CHUNK)
    incl = (idx[:, None] >= idx[None, :])[:, :, None]

    def step(state, inp):
        qi, ki, vi, gi = inp
        b = jnp.cumsum(gi, axis=2)
        pair = jnp.exp(jnp.where(incl, b[:, :, :, None, :] - b[:, :, None, :, :], -jnp.inf))
        att = jnp.einsum('bhtd,bhsd,bhtsd->bhts', qi, ki, pair)
        out = (jnp.einsum('bhts,bhsv->bhtv', att, vi)
               + jnp.einsum('bhtd,bhdv->bhtv', qi * jnp.exp(b), state))
        b_last = b[:, :, -1:, :]
        state = (state * jnp.exp(b_last)[:, :, 0, :, None]
                 + jnp.einsum('bhsd,bhsv->bhdv', ki * jnp.exp(b_last - b), vi))
        return state, out

    s0 = jnp.zeros((bsz, GLA_HEADS, GLA_DK, GLA_DV), f32)
    _, o = lax.scan(step, s0, (qc, kc, vc, gc))
    o = rmsnorm(from_chunks(o), norm_g)
    gate = jax.nn.silu(r.astype(f32)).reshape(o.shape)
    return (o * gate).reshape(bsz, seq, GLA_W).astype(q.dtype)


def gdn_group(q, k, v, z, b_raw, a_raw, conv_w, a_log, dt_bias, norm_g):
    f32 = jnp.float32
    bsz, seq, _ = q.shape
    qkv = jax.nn.silu(causal_conv(jnp.concatenate([q, k, v], axis=-1), conv_w)).astype(f32)
    q, k, v = jnp.split(qkv, [GDN_QK, 2 * GDN_QK], axis=-1)
    q = l2norm(q.reshape(bsz, seq, GDN_HEADS, GDN_DK)) * GDN_DK ** -0.5
    k = l2norm(k.reshape(bsz, seq, GDN_HEADS, GDN_DK))
    beta = jax.nn.sigmoid(b_raw.astype(f32))
    g = -jnp.exp(a_log.astype(f32)) * jax.nn.softplus(a_raw.astype(f32) + dt_bias.astype(f32))
    qc = to_chunks(q.reshape(bsz, seq, GDN_QK), GDN_HEADS)
    kc = to_chunks(k.reshape(bsz, seq, GDN_QK), GDN_HEADS)
    vc = to_chunks(v, GDN_HEADS)
    bc = to_chunks(beta, GDN_HEADS)[..., 0]
    G = jnp.cumsum(to_chunks(g, GDN_HEADS)[..., 0], axis=-1)
    idx = jnp.arange(CHUNK)
    incl = idx[:, None] >= idx[None, :]
    strict = idx[:, None] > idx[None, :]
    decay = jnp.exp(jnp.where(incl, G[..., :, None] - G[..., None, :], -jnp.inf))
    kb = kc * bc[..., None]
    m = jnp.where(strict, jnp.einsum('nbhtd,nbhsd->nbhts', kb, kc) * decay, 0.0)
    rhs = jnp.concatenate([vc * bc[..., None], kb * jnp.exp(G)[..., None]], axis=-1)
    sol = lax.linalg.triangular_solve(m + jnp.eye(CHUNK, dtype=f32), rhs,
                                      left_side=True, lower=True)
    u, w = sol[..., :GDN_DV], sol[..., GDN_DV:]
    aqk = jnp.einsum('nbhtd,nbhsd->nbhts', qc, kc) * decay

    def step(state, inp):
        qi, ki, ui, wi, gi, ai = inp
        v_new = ui - jnp.einsum('bhtd,bhdv->bhtv', wi, state)
        out = (jnp.einsum('bhtd,bhdv->bhtv', qi * jnp.exp(gi)[..., None], state)
               + jnp.einsum('bhts,bhsv->bhtv', ai, v_new))
        g_last = gi[..., -1:]
        state = (state * jnp.exp(g_last)[..., None]
                 + jnp.einsum('bhsd,bhsv->bhdv', ki * jnp.exp(g_last - gi)[..., None], v_new))
        return state, out

    s0 = jnp.zeros((bsz, GDN_HEADS, GDN_DK, GDN_DV), f32)
    _, o = lax.scan(step, s0, (qc, kc, u, w, G, aqk))
    o = rmsnorm(from_chunks(o), norm_g)
    gate = jax.nn.silu(z.astype(f32)).reshape(o.shape)
    return (o * gate).reshape(bsz, seq, GDN_W).astype(z.dtype)


def sb_group(q, k, v, gate, norm_g, valid):
    f32 = jnp.float32
    bsz, seq, _ = q.shape
    nb = seq // SB_BLOCK
    scale = SB_D ** -0.5
    kh = k.reshape(bsz, seq, SB_HEADS, SB_D).transpose(0, 2, 1, 3)
    vh = v.reshape(bsz, seq, SB_HEADS, SB_D).transpose(0, 2, 1, 3)
    qb = q.reshape(bsz, nb, SB_BLOCK, SB_HEADS, SB_D).transpose(1, 0, 3, 2, 4)
    key_pos = jnp.arange(seq)

    def block(args):
        i, qblk = args
        t = i * SB_BLOCK + jnp.arange(SB_BLOCK)
        vis = (key_pos[None, :] < t[:, None]) & valid[None, :]
        zz = jnp.einsum('bhqd,bhsd->bhqs', qblk, kh, preferred_element_type=f32) * scale
        log_not = jnp.where(vis, jax.nn.log_sigmoid(-zz), 0.0)
        between = lax.cumsum(log_not, axis=3, reverse=True) - log_not
        a = jnp.exp(jnp.where(vis, jax.nn.log_sigmoid(zz) + between, -jnp.inf))
        return jnp.einsum('bhqs,bhsd->bhqd', a.astype(vh.dtype), vh, preferred_element_type=f32)

    o = lax.map(block, (jnp.arange(nb), qb))
    o = o.transpose(1, 0, 3, 2, 4).reshape(bsz, seq, SB_HEADS, SB_D)
    o = rmsnorm(o, norm_g)
    g = jax.nn.silu(gate.astype(f32)).reshape(o.shape)
    return (o * g).reshape(bsz, seq, SB_W).astype(gate.dtype)


def setup_inputs(seed: int = 0) -> dict:
    key = jax.random.key(seed)
    ks = jax.random.split(key, 16)
    f32 = jnp.float32
    x = jax.random.normal(ks[0], (BATCH, SEQ, D_MODEL), f32)
    meta = jax.random.normal(ks[1], (N_META, D_MODEL), f32)
    norm_g = 1.0 + 0.02 * jax.random.normal(ks[2], (DEPTH, D_MODEL), f32)
    w_in = jax.random.normal(ks[3], (DEPTH, D_MODEL, D_IN), f32) * D_MODEL ** -0.5
    gla_w_gate = jax.random.normal(ks[4], (DEPTH, GLA_RANK, GLA_QK), f32) * GLA_RANK ** -0.5
    gla_b_gate = 0.1 * jax.random.normal(ks[5], (DEPTH, GLA_QK), f32)
    gla_norm_g = 1.0 + 0.02 * jax.random.normal(ks[6], (DEPTH, GLA_DV), f32)
    gdn_conv_w = jax.random.normal(ks[7], (DEPTH, CONV_K, GDN_CONV_CH), f32) * CONV_K ** -0.5
    gdn_a_log = jnp.log(jax.random.uniform(ks[8], (DEPTH, GDN_HEADS), f32, 1.0, 16.0))
    dt = jnp.exp(jax.random.uniform(ks[9], (DEPTH, GDN_HEADS), f32, math.log(1e-3), math.log(1e-1)))
    gdn_dt_bias = dt + jnp.log(-jnp.expm1(-dt))
    gdn_norm_g = 1.0 + 0.02 * jax.random.normal(ks[10], (DEPTH, GDN_DV), f32)
    sb_norm_g = 1.0 + 0.02 * jax.random.normal(ks[11], (DEPTH, SB_D), f32)
    w_out = jax.random.normal(ks[12], (DEPTH, D_MIX, D_MODEL), f32) * D_MIX ** -0.5
    final_g = 1.0 + 0.02 * jax.random.normal(ks[13], (D_MODEL,), f32)
    return {'x': x, 'meta': meta, 'norm_g': norm_g, 'w_in': w_in,
            'gla_w_gate': gla_w_gate, 'gla_b_gate': gla_b_gate, 'gla_norm_g': gla_norm_g,
            'gdn_conv_w': gdn_conv_w, 'gdn_a_log': gdn_a_log, 'gdn_dt_bias': gdn_dt_bias,
            'gdn_norm_g': gdn_norm_g, 'sb_norm_g': sb_norm_g, 'w_out': w_out,
            'final_g': final_g}


def reference(x, meta, norm_g, w_in, gla_w_gate, gla_b_gate, gla_norm_g, gdn_conv_w,
              gdn_a_log, gdn_dt_bias, gdn_norm_g, sb_norm_g, w_out, final_g):
    bsz = x.shape[0]
    pad = jnp.zeros((bsz, PREFIX - N_META, D_MODEL), x.dtype)
    metas = jnp.broadcast_to(meta.astype(x.dtype)[None], (bsz, N_META, D_MODEL))
    h = jnp.concatenate([pad, metas, x], axis=1)
    seq = h.shape[1]
    valid = jnp.arange(seq) >= PREFIX - N_META
    offsets = np.cumsum(IN_SPLITS)[:-1].tolist()
    for l in range(DEPTH):
        u = rmsnorm(h, norm_g[l]) @ w_in[l]
        (gq, gk, gv, gr, glr, dq, dk, dv, dz, db, da,
         sq, sk, sv, sg) = jnp.split(u, offsets, axis=-1)
        o_gla = gla_group(gq, gk, gv, gr, glr, gla_w_gate[l], gla_b_gate[l], gla_norm_g[l])
        o_gdn = gdn_group(dq, dk, dv, dz, db, da, gdn_conv_w[l], gdn_a_log[l],
                          gdn_dt_bias[l], gdn_norm_g[l])
        o_sb = sb_group(sq, sk, sv, sg, sb_norm_g[l], valid)
        h = h + jnp.concatenate([o_gla, o_gdn, o_sb], axis=-1) @ w_out[l]
    return rmsnorm(h, final_g)[:, PREFIX:, :]
```

```python
import numpy as np
import concourse.bass as bass
import concourse.mybir as mybir
from concourse.bass_utils import run_bass_kernel_spmd
from contextlib import ExitStack

F32 = mybir.dt.float32
AF = mybir.ActivationFunctionType
ALU = mybir.AluOpType

EPOCH = 20000
DEPOCH = 1000
SAME_ENGINE_SYNC = True
EPS = 1e-6


class V:
    def __init__(self, ap, key):
        self.ap = ap
        self.key = key

    def __getitem__(self, idx):
        return V(self.ap[idx], self.key)

    def re(self, pat, **kw):
        return V(self.ap.rearrange(pat, **kw), self.key)


class Prog:
    ENGS = ("pe", "act", "dve", "pool", "sp")

    def __init__(self, nc, es):
        self.nc = nc
        self.es = es
        self.ops = []
        self.cnt = {e: 0 for e in self.ENGS}
        self.last_w = {}
        self.readers = {}
        self.chan_cnt = {}
        self.sems = {}

    def _deps(self, reads, writes):
        deps = []
        for k in reads:
            t = self.last_w.get(k)
            if t is not None:
                deps.append(t)
        for k in writes:
            t = self.last_w.get(k)
            if t is not None:
                deps.append(t)
            deps.extend(self.readers.get(k, []))
        return deps

    def _commit(self, tok, reads, writes):
        for k in reads:
            self.readers.setdefault(k, []).append(tok)
        for k in writes:
            self.last_w[k] = tok
            self.readers[k] = []

    def op(self, eng, fn, reads=(), writes=()):
        reads = [k for k in reads if k is not None]
        writes = list(writes) + [k for k in reads if k.startswith("ps") and k not in writes]
        deps = self._deps(reads, writes)
        self.cnt[eng] += 1
        tok = ("c", eng, self.cnt[eng])
        self.ops.append((eng, fn, deps, tok))
        self._commit(tok, reads, writes)
        return tok

    def dma(self, eng, fn, reads=(), writes=(), chan=None):
        deps = self._deps(reads, writes)
        c = self.chan_cnt.get(chan, 0) + 1
        self.chan_cnt[chan] = c
        tok = ("d", chan, c)
        self.ops.append((eng, fn, deps, tok))
        self._commit(tok, reads, writes)
        return tok

    def barrier(self):
        deps = [("c", e, n) for e, n in self.cnt.items() if n > 0]
        deps += [("d", ch, n) for ch, n in self.chan_cnt.items()]
        for e in self.ENGS:
            self.ops.append((e, None, list(deps), None))
        self.last_w = {}
        self.readers = {}

    def _tok_sem(self, tok):
        kind, who, n = tok
        if kind == "c":
            return ("c_%s_%d" % (who, (n - 1) // EPOCH), (n - 1) % EPOCH + 1)
        return ("d_%s_%d" % (who, (n - 1) // DEPOCH), (1 if who.startswith("cc") else 16) * ((n - 1) % DEPOCH + 1))

    def emit(self):
        nc = self.nc
        for eng, fn, deps, tok in self.ops:
            if tok is not None:
                s = self._tok_sem(tok)[0]
                if s not in self.sems:
                    self.sems[s] = self.es.enter_context(nc.semaphore(s))
        per_eng = {e: [] for e in self.ENGS}
        for o in self.ops:
            per_eng[o[0]].append(o)
        with nc.Block() as block:
            def make(e):
                def body(engh):
                    waited = {}
                    for eng, fn, deps, tok in per_eng[e]:
                        need = {}
                        for d in deps:
                            if d[0] == "c" and d[1] == e and (e == "pe" or not SAME_ENGINE_SYNC):
                                continue
                            s, v = self._tok_sem(d)
                            if need.get(s, 0) < v:
                                need[s] = v
                        for s, v in need.items():
                            if waited.get(s, 0) >= v:
                                continue
                            base, ep = s.rsplit("_", 1)
                            if any(k.startswith(base + "_") and int(k.rsplit("_", 1)[1]) > int(ep)
                                   for k in waited):
                                continue
                            engh.wait_ge(self.sems[s], v)
                            waited[s] = v
                        if fn is None:
                            continue
                        ins = fn(engh)
                        s, v = self._tok_sem(tok)
                        ins.then_inc(self.sems[s], 16 if (tok[0] == "d" and not tok[1].startswith("cc")) else 1)
                return body
            for e, deco in (("pe", block.tensor), ("act", block.scalar), ("dve", block.vector),
                            ("pool", block.gpsimd), ("sp", block.sync)):
                if per_eng[e]:
                    deco(make(e))


class Cfg:
    def __init__(self, D=4096, NB=65, depth=2, debug=False, mixers=("sb", "gla", "gdn")):
        self.mixers = mixers
        self.stop = 99
        self.D = D
        self.KC = D // 128
        self.NB = NB
        self.NT = NB * 128
        assert self.NT % 4 == 0
        self.SL = self.NT // 4
        self.depth = depth
        self.debug = debug
        self.DMIX = 4096
        self.CC = 32


FM_GLA_Q, FM_GLA_K, FM_GLA_R0, FM_GLA_R1, FM_GLA_LR = 0, 1, 2, 3, 4
FM_GDN_Q, FM_GDN_K, FM_GDN_V, FM_GDN_Z = 5, 8, 11, 14
FM_SB_Q, FM_SB_K, FM_SB_G = 17, 20, 23
NFM = 26
TM_GLA_K, TM_GLA_V0, TM_GLA_V1, TM_GDN_BA, TM_SB_V = 0, 1, 2, 3, 4
NTM = 7
NCT = NFM + NTM

_OFF = {}
_o = 0
for _n, _w in (("gq", 512), ("gk", 512), ("gv", 1024), ("gr", 1024), ("glr", 16),
               ("dq", 1536), ("dk", 1536), ("dv", 1536), ("dz", 1536), ("db", 12), ("da", 12),
               ("sq", 1536), ("sk", 1536), ("sv", 1536), ("sg", 1536)):
    _OFF[_n] = _o
    _o += _w
D_IN = _o


def col_tiles(g):
    def rng(name, start, n):
        a = np.full(128, -1, np.int64)
        a[:n] = _OFF[name] + start + np.arange(n)
        return a
    fm = [None] * NFM
    fm[FM_GLA_Q] = rng("gq", g * 128, 128)
    fm[FM_GLA_K] = rng("gk", g * 128, 128)
    fm[FM_GLA_R0] = rng("gr", g * 256, 128)
    fm[FM_GLA_R1] = rng("gr", g * 256 + 128, 128)
    fm[FM_GLA_LR] = rng("glr", 0, 16)
    for h in range(3):
        hh = 3 * g + h
        fm[FM_GDN_Q + h] = rng("dq", hh * 128, 128)
        fm[FM_GDN_K + h] = rng("dk", hh * 128, 128)
        fm[FM_GDN_V + h] = rng("dv", hh * 128, 128)
        fm[FM_GDN_Z + h] = rng("dz", hh * 128, 128)
        fm[FM_SB_Q + h] = rng("sq", hh * 128, 128)
        fm[FM_SB_K + h] = rng("sk", hh * 128, 128)
        fm[FM_SB_G + h] = rng("sg", hh * 128, 128)
    tm = [None] * NTM
    tm[TM_GLA_K] = rng("gk", g * 128, 128)
    tm[TM_GLA_V0] = rng("gv", g * 256, 128)
    tm[TM_GLA_V1] = rng("gv", g * 256 + 128, 128)
    ba = np.full(128, -1, np.int64)
    ba[0:3] = _OFF["db"] + 3 * g + np.arange(3)
    ba[3:6] = _OFF["da"] + 3 * g + np.arange(3)
    tm[TM_GDN_BA] = ba
    for h in range(3):
        tm[TM_SB_V + h] = rng("sv", (3 * g + h) * 128, 128)
    return fm + tm


C_ID, C_ONES, C_UI, C_TRI, C_SFX, C_MUT, C_MSL, C_TRI1 = 0, 128, 256, 384, 512, 640, 768, 896
C_SBM = 1024
C_VALID = C_SBM + 4 * 512
C_ONE = C_VALID + 1
C_EPS = C_ONE + 1
C_LNS = C_EPS + 1
NCONST = C_LNS + 1


def make_consts():
    c = np.zeros((128, NCONST), np.float32)
    j = np.arange(128)[:, None]
    f = np.arange(128)[None, :]
    c[:, C_ID:C_ID + 128] = (j == f)
    c[:, C_ONES:C_ONES + 128] = 1.0
    c[:, C_UI:C_UI + 128] = (j >= f)
    c[:, C_TRI:C_TRI + 128] = (j <= f) * (-1.0 / 16.0)
    c[:, C_SFX:C_SFX + 128] = (j > f) * (-1.0 / 16.0)
    c[:, C_MUT:C_MUT + 128] = (j <= f)
    c[:, C_MSL:C_MSL + 128] = (f < j)
    c[:, C_TRI1:C_TRI1 + 128] = (j <= f)
    t = np.arange(512)[None, :]
    for m in range(4):
        c[:, C_SBM + m * 512:C_SBM + (m + 1) * 512] = ((m * 128 + j) < t)
    c[:, C_VALID] = (np.arange(128) >= 112)
    c[:, C_ONE] = 1.0
    c[:, C_EPS] = EPS
    c[:, C_LNS] = np.log(128.0 ** -0.5)
    return c


ARENA = 40000


def build(cfg):
    nc = bass.Bass("TRN2", target_bir_lowering=False)
    D, KC, NB, NT, SL, L = cfg.D, cfg.KC, cfg.NB, cfg.NT, cfg.SL, cfg.depth
    KH = KC // 2
    dram_in = lambda name, shape: nc.dram_tensor(name, shape, F32, kind="ExternalInput").ap()
    dram_out = lambda name, shape: nc.dram_tensor(name, shape, F32, kind="ExternalOutput").ap()
    dram = lambda name, shape: nc.dram_tensor(name, shape, F32).ap()

    hT_in = dram_in("hT", [D, SL])
    consts_in = dram_in("consts", [128, NCONST])
    normg_in = dram_in("normg", [128, L + 1, KC])
    win_in = dram_in("win", [L * NCT * 128, KH * 128])
    wout_in = dram_in("wout", [L * KC * 128, 4 * 128])
    NSM = 64
    small_in = dram_in("small", [128, L, NSM])
    glawg_in = dram_in("glawg", [16, L, 128])
    glabg_in = dram_in("glabg", [1, L, 128])
    out_ext = dram_out("outT", [D, SL])

    win_b = dram("win_b", [L * NCT * 128, KH * 128])
    WIN = dram("WIN", [2 * L * NCT * 128, KH * 128])
    wout_b = dram("wout_b", [L * KC * 128, 4 * 128])
    WOUT = dram("WOUT", [2 * L * KC * 128, 4 * 128])
    XNS = dram("XNS", [D, SL])
    XN = dram("XN", [4 * D, SL])
    UFM = dram("UFM", [NFM * 128, NT])
    UTM = dram("UTM", [NTM * NT, 128])
    OL = dram("OL", [1024, NT])
    HB = [dram("HB%d" % i, [D, SL]) for i in range(2)]
    dbg = {}
    if cfg.debug:
        dbg["ufm"] = dram_out("dbg_ufm", [NFM * 128, NT])
        dbg["utm"] = dram_out("dbg_utm", [NTM * NT, 128])
        dbg["ol"] = dram_out("dbg_ol", [1024, NT])

    with ExitStack() as es:
        P = Prog(nc, es)
        arena_t = es.enter_context(nc.sbuf_tensor("arena", [128, ARENA], F32))
        const_t = es.enter_context(nc.sbuf_tensor("constt", [128, NCONST], F32))
        ng_t = es.enter_context(nc.sbuf_tensor("ngt", [128, (L + 1) * KC], F32))
        small_t = es.enter_context(nc.sbuf_tensor("smallt", [128, L * NSM], F32))
        PS = [V(es.enter_context(nc.psum_tensor("ps%d" % i, [128, 512], F32))[:], "ps%d" % i)
              for i in range(8)]
        CONST = V(const_t[:], "const")
        NG = V(ng_t[:], "ng")
        SMALL = V(small_t[:], "small")
        state = {"off": 0, "phase": 0}

        def phase():
            P.barrier()
            state["off"] = 0
            state["phase"] += 1

        def alloc(name, n):
            off = state["off"]
            assert off + n <= ARENA, (name, off, n)
            state["off"] = off + n
            return V(arena_t[:, off:off + n], "%s@%d" % (name, state["phase"]))

        cst = lambda c0, n=128: CONST[:, c0:c0 + n]
        IDENT, ONES = cst(C_ID), cst(C_ONES)
        ONE1, EPS1, LNS1 = cst(C_ONE, 1), cst(C_EPS, 1), cst(C_LNS, 1)

        def ld(out, in_ap, rkey=None, eng="sp"):
            rk = [] if rkey is None else (list(rkey) if isinstance(rkey, (list, tuple)) else [rkey])
            P.dma(eng, lambda e: e.dma_start(out=out.ap, in_=in_ap), reads=rk,
                  writes=[out.key], chan="L" + out.key)

        def st(out_ap, wkey, in_, eng="pool"):
            P.dma(eng, lambda e: e.dma_start(out=out_ap, in_=in_.ap), reads=[in_.key],
                  writes=[wkey] if wkey else [], chan="S" + in_.key)

        def d2d(out_ap, wkey, in_ap, rkey, chan, eng="sp"):
            nrows = out_ap.shape[0]
            for r0 in range(0, nrows, 1024):
                r1 = min(nrows, r0 + 1024)
                P.dma(eng, lambda e, r0=r0, r1=r1: e.dma_start(out=out_ap[r0:r1, :], in_=in_ap[r0:r1, :]),
                      reads=[rkey] if rkey else [], writes=[wkey], chan=chan)

        def allgather(out_ap, wkey, in_ap, rkey, groups, chan):
            P.dma("pool", lambda e: e.collective_compute(
                "AllGather", ALU.bypass, replica_groups=groups, ins=[in_ap.opt()], outs=[out_ap.opt()]),
                reads=[rkey], writes=[wkey], chan=chan)

        def mm(out, lhsT, rhs, start=True, stop=True):
            P.op("pe", lambda e: e.matmul(out.ap, lhsT=lhsT.ap, rhs=rhs.ap, start=start, stop=stop),
                 reads=[lhsT.key, rhs.key], writes=[out.key])

        def tr(out, in_):
            P.op("pe", lambda e: e.transpose(out.ap, in_.ap, IDENT.ap),
                 reads=[in_.key, "const"], writes=[out.key])

        def act(out, in_, func, bias=None, scale=None, eng="act"):
            kw = {}
            rk = [in_.key]
            if bias is not None:
                if isinstance(bias, V):
                    kw["bias"] = bias.ap
                    rk.append(bias.key)
                else:
                    kw["bias"] = bias
            if scale is not None:
                if isinstance(scale, V):
                    kw["scale"] = scale.ap
                    rk.append(scale.key)
                else:
                    kw["scale"] = scale
            P.op("act", lambda e: e.activation(out=out.ap, in_=in_.ap, func=func, **kw),
                 reads=rk, writes=[out.key])

        def tt(out, a, b, op, eng="dve"):
            P.op(eng, lambda e: e.tensor_tensor(out=out.ap, in0=a.ap, in1=b.ap, op=op),
                 reads=[a.key, b.key], writes=[out.key])

        def ts(out, a, s1, op0, s2=None, op1=None, eng="dve"):
            rk = [a.key]
            v1 = s1
            if isinstance(s1, V):
                v1 = s1.ap
                rk.append(s1.key)
            v2 = s2
            if isinstance(s2, V):
                v2 = s2.ap
                rk.append(s2.key)
            if op1 is None:
                P.op(eng, lambda e: e.tensor_scalar(out=out.ap, in0=a.ap, scalar1=v1, scalar2=None, op0=op0),
                     reads=rk, writes=[out.key])
            else:
                P.op(eng, lambda e: e.tensor_scalar(out=out.ap, in0=a.ap, scalar1=v1, scalar2=v2,
                                                    op0=op0, op1=op1), reads=rk, writes=[out.key])

        def stt(out, a, s, b, op0, op1):
            rk = [a.key, b.key]
            sv = s
            if isinstance(s, V):
                sv = s.ap
                rk.append(s.key)
            P.op("dve", lambda e: e.scalar_tensor_tensor(out=out.ap, in0=a.ap, scalar=sv, in1=b.ap,
                                                         op0=op0, op1=op1), reads=rk, writes=[out.key])

        def cp(out, in_, eng="dve"):
            P.op(eng, lambda e: e.tensor_copy(out=out.ap, in_=in_.ap), reads=[in_.key], writes=[out.key])

        def memset(out, val, eng="dve"):
            P.op(eng, lambda e: e.memset(out.ap, val), writes=[out.key])

        def recip(out, in_):
            P.op("dve", lambda e: e.reciprocal(out=out.ap, in_=in_.ap), reads=[in_.key], writes=[out.key])

        def rstd_from_ssq(dst, ssq_ps, inv_n, w):
            act(dst[:, :w], ssq_ps[:, :w], AF.Sqrt, bias=EPS1, scale=inv_n)
            recip(dst[:, :w], dst[:, :w])

        ld(CONST, consts_in)
        ld(NG, normg_in.rearrange("p l k -> p (l k)"))
        ld(SMALL, small_in.rearrange("p l k -> p (l k)"))
        d2d(win_b, "win_b", win_in, None, "c_win")
        d2d(wout_b, "wout_b", wout_in, None, "c_wout")
        pairs = [[0, 4], [1, 5], [2, 6], [3, 7]]
        quads = [[0, 1, 2, 3], [4, 5, 6, 7]]
        for T in range(L * NCT):
            allgather(WIN[T * 256:(T + 1) * 256, :], "WIN#%d" % T, win_b[T * 128:(T + 1) * 128, :], "win_b", pairs, "cc_win")
        for q in range(L * KC // 4):
            allgather(WOUT[q * 1024:(q + 1) * 1024, :], "WOUT#%d" % q, wout_b[q * 512:(q + 1) * 512, :], "wout_b", pairs, "cc_wout")

        def tok_tiles(n, w=512):
            return [(t0, min(w, n - t0)) for t0 in range(0, n, w)]

        def norm_phase(l, src_ap, src_key, add_y, hdst_ap, hdst_key, dst_ap, dst_key):
            phase()
            W_ = 256
            h = alloc("h", KC * W_)
            y = alloc("y", KC * W_)
            sq = [alloc("sq%d" % i, W_) for i in range(2)]
            r = alloc("r", W_)
            xo = [alloc("xo%d" % i, W_) for i in range(2)]
            for (t0, w) in tok_tiles(SL, W_):
                h3 = h[:, :KC * w].re("p (k t) -> p k t", t=w)
                ld_chunks(h3, src_ap[:, t0:t0 + w].rearrange("(k p) t -> p k t", p=128), src_key, KC, 8)
                if add_y:
                    y3 = y[:, :KC * w].re("p (k t) -> p k t", t=w)
                    ysrc = YS[:, t0:t0 + w].rearrange("(k p) t -> p k t", p=128)
                    for k0 in range(0, KC, 8):
                        k1 = min(KC, k0 + 8)
                        ld(y3[:, k0:k1, :], ysrc[:, k0:k1, :], ["YS#%d" % k for k in range(k0, k1)])
                    tt(h[:, :KC * w], h[:, :KC * w], y[:, :KC * w], ALU.add)
                    for k0 in range(0, KC, 8):
                        k1 = min(KC, k0 + 8)
                        st(hdst_ap[k0 * 128:k1 * 128, t0:t0 + w].rearrange("(k p) t -> p k t", p=128), hdst_key,
                           h3[:, k0:k1, :])
                for kc in range(KC):
                    s = sq[kc % 2]
                    act(s[:, :w], h3[:, kc, :], AF.Square)
                    mm(PS[0][:, :w], ONES, s[:, :w], start=(kc == 0), stop=(kc == KC - 1))
                rstd_from_ssq(r, PS[0], 1.0 / D, w)
                for kc in range(KC):
                    x = xo[kc % 2]
                    stt(x[:, :w], h3[:, kc, :], NG[:, l * KC + kc:l * KC + kc + 1], r[:, :w], ALU.mult, ALU.mult)
                    st(dst_ap[kc * 128:(kc + 1) * 128, t0:t0 + w], "%s#%d" % (dst_key, kc), x[:, :w])

        def inproj_phase(l):
            phase()
            xn = alloc("xn", KC * 512)
            wt = [alloc("wt%d" % i, KC * 128) for i in range(2)]
            ev = [alloc("ev%d" % i, 512) for i in range(2)]
            nw = 0
            ne = 0
            for (t0, w) in tok_tiles(NT):
                xn3 = xn[:, :KC * w].re("p (k t) -> p k t", t=w)
                t = t0
                while t < t0 + w:
                    rk = t // SL
                    te = min(t0 + w, (rk + 1) * SL)
                    for ph in range(2):
                        srcv = XN.rearrange("(k ph r pl) t -> ph r pl k t", ph=2, r=4, pl=64)[ph, rk]
                        for k0 in range(0, KC, 8):
                            k1 = min(KC, k0 + 8)
                            ld(xn3[ph * 64:(ph + 1) * 64, k0:k1, t - t0:te - t0],
                               srcv[:, k0:k1, t - rk * SL:te - rk * SL],
                               ["XN#%d" % (2 * k + ph) for k in range(k0, k1)])
                    t = te
                for ct in range(NCT):
                    wtile = wt[nw % 2]
                    nw += 1
                    w3 = wtile.re("p (k c) -> p k c", c=128)
                    for half in range(2):
                        T = l * NCT + ct
                        base = (T * 2 + half) * 128
                        ld(w3[:, half * KH:(half + 1) * KH, :],
                           WIN[base:base + 128, :].rearrange("p (k c) -> p k c", c=128), "WIN#%d" % T)
                    if ct < NFM:
                        ps = PS[ct % 2]
                        for kc in range(KC):
                            mm(ps[:, :w], w3[:, kc, :], xn3[:, kc, :], start=(kc == 0), stop=(kc == KC - 1))
                        e = ev[ne % 2]
                        ne += 1
                        if ne % 2:
                            cp(e[:, :w], ps[:, :w], eng="dve")
                        else:
                            act(e[:, :w], ps[:, :w], AF.Copy)
                        st(UFM[ct * 128:(ct + 1) * 128, t0:t0 + w], "UFM", e[:, :w])
                    else:
                        tmi = ct - NFM
                        nblk = w // 128
                        ps = PS[2 + (tmi % 2)]
                        for bi in range(nblk):
                            for kc in range(KC):
                                mm(ps[:, bi * 128:(bi + 1) * 128], xn3[:, kc, bi * 128:(bi + 1) * 128], w3[:, kc, :],
                                   start=(kc == 0), stop=(kc == KC - 1))
                        e = ev[ne % 2]
                        ne += 1
                        cp(e[:, :w], ps[:, :w], eng="dve")
                        st(UTM[tmi * NT + t0:tmi * NT + t0 + w, :].rearrange("(b p) c -> p b c", p=128),
                           "UTM", e[:, :w].re("p (b c) -> p b c", c=128))


        def ld_chunks(dst3, src3, key, n, step):
            for a in range(0, n, step):
                b = min(n, a + step)
                ld(dst3[:, a:b, :], src3[:, a:b, :], key)

        def st_OL(c0, t0, w, tile):
            st(OL[c0:c0 + 128, t0:t0 + w], "OL", tile[:, :w])

        SBM = [CONST[:, C_SBM + m * 512:C_SBM + (m + 1) * 512] for m in range(4)]
        VALID1 = cst(C_VALID, 1)
        UI, TRI, SFX, MUT, MSL, TRI1 = cst(C_UI), cst(C_TRI), cst(C_SFX), cst(C_MUT), cst(C_MSL), cst(C_TRI1)
        NSM_ = NSM

        def sm(l, idx):
            return SMALL[:, l * NSM_ + idx:l * NSM_ + idx + 1]

        def sb_phase(l):
            import os
            CUT = int(os.environ.get("CUT", "99"))
            phase()
            sc = 128.0 ** -0.5
            kT = alloc("kT", NT)
            vtm = alloc("vtm", NB * 128)
            vtm3 = vtm.re("p (b d) -> p b d", d=128)
            qs, nqs, S = alloc("qs", 512), alloc("nqs", 512), alloc("S", 512)
            E = [alloc("E%d" % i, 512) for i in range(2)]
            SP = [alloc("SP%d" % i, 512) for i in range(2)]
            A = [alloc("A%d" % i, 512) for i in range(2)]
            osb, sq, r, gt = alloc("osb", 512), alloc("sq", 512), alloc("r", 512), alloc("gt", 512)
            for h in range(3):
                ld(kT, UFM[(FM_SB_K + h) * 128:(FM_SB_K + h + 1) * 128, :], "UFM")
                ld_chunks(vtm3, UTM[(TM_SB_V + h) * NT:(TM_SB_V + h + 1) * NT, :].rearrange("(b p) d -> p b d", p=128),
                          "UTM", NB, 4)
                for (q0, w) in tok_tiles(NT):
                    qb0, nqb = q0 // 128, w // 128
                    ld(qs[:, :w], UFM[(FM_SB_Q + h) * 128:(FM_SB_Q + h + 1) * 128, q0:q0 + w], "UFM")
                    ld(gt[:, :w], UFM[(FM_SB_G + h) * 128:(FM_SB_G + h + 1) * 128, q0:q0 + w], "UFM")
                    ts(nqs[:, :w], qs[:, :w], -sc, ALU.mult)
                    ts(qs[:, :w], qs[:, :w], sc, ALU.mult)
                    first = True
                    for i, kb in enumerate(range(qb0 + nqb - 1, -1, -1)):
                        Zb, Tb, e, sp, a = PS[i % 2], PS[2 + i % 2], E[i % 2], SP[i % 2], A[i % 2]
                        kblk = kT[:, kb * 128:(kb + 1) * 128]
                        m = kb - qb0
                        mm(Zb[:, :w], kblk, qs[:, :w])
                        act(e[:, :w], Zb[:, :w], AF.Exp)
                        act(sp[:, :w], e[:, :w], AF.Ln, bias=ONE1)
                        if CUT <= 1:
                            continue
                        if m >= 0:
                            tt(sp[:, :w], sp[:, :w], SBM[m][:, :w], ALU.mult)
                        if kb == 0:
                            ts(sp[:, :w], sp[:, :w], VALID1, ALU.mult)
                        mm(Tb[:, :w], kblk, nqs[:, :w], start=True, stop=False)
                        mm(Tb[:, :w], UI, sp[:, :w], start=False, stop=first)
                        if not first:
                            mm(Tb[:, :w], ONES, S[:, :w], start=False, stop=True)
                        act(a[:, :w], Tb[:, :w], AF.Exp, scale=-1.0)
                        if CUT <= 2:
                            continue
                        if m >= 0:
                            tt(a[:, :w], a[:, :w], SBM[m][:, :w], ALU.mult)
                        if kb == 0:
                            ts(a[:, :w], a[:, :w], VALID1, ALU.mult)
                        if first:
                            cp(S[:, :w], sp[:, :w], eng="pool")
                        else:
                            tt(S[:, :w], S[:, :w], sp[:, :w], ALU.add, eng="pool")
                        if CUT <= 3:
                            first = False
                            continue
                        mm(PS[4][:, :w], vtm3[:, kb, :], a[:, :w], start=first, stop=(kb == 0))
                        first = False
                    if CUT <= 4:
                        continue
                    cp(osb[:, :w], PS[4][:, :w])
                    act(sq[:, :w], PS[4][:, :w], AF.Square)
                    mm(PS[5][:, :w], ONES, sq[:, :w])
                    rstd_from_ssq(r, PS[5], 1.0 / 128, w)
                    act(sq[:, :w], gt[:, :w], AF.Silu)
                    stt(osb[:, :w], osb[:, :w], sm(l, 3), r[:, :w], ALU.mult, ALU.mult)
                    tt(osb[:, :w], osb[:, :w], sq[:, :w], ALU.mult)
                    if CUT <= 5:
                        continue
                    st_OL(5 * 128 + h * 128, q0, w, osb)

        def gla_phase(l):
            phase()
            S = alloc("S", 256)
            WG, BG = alloc("WG", 128), alloc("BG", 128)
            qT, kT, lr = alloc("qT", 512), alloc("kT", 512), alloc("lr", 512)
            rT = [alloc("r0", 512), alloc("r1", 512)]
            ktm, vtm = alloc("ktm", 512), alloc("vtm", 1024)
            ktm3 = ktm.re("p (b d) -> p b d", d=128)
            vtm3 = vtm.re("p (b d) -> p b d", d=256)
            e1, sp, bl, eBl = alloc("e1", 128), alloc("sp", 128), alloc("bl", 1), alloc("eBl", 1)
            E1, E2, E3 = alloc("E1", 128), alloc("E2", 128), alloc("E3", 128)
            qg, kg, kd, att = alloc("qg", 128), alloc("kg", 128), alloc("kd", 128), alloc("att", 128)
            oraw = [alloc("oraw0", 512), alloc("oraw1", 512)]
            sq, r, sg, o = alloc("sq", 512), alloc("r", 512), alloc("sg", 512), alloc("o", 512)
            memset(S, 0.0)
            memset(WG, 0.0)
            memset(BG, 0.0)
            ld(WG[0:16, :], glawg_in[:, l, :])
            ld(BG[0:1, :], glabg_in[:, l, :])
            fmrow = lambda ct, t0, w: UFM[ct * 128:(ct + 1) * 128, t0:t0 + w]
            for (t0, w) in tok_tiles(NT):
                nb = w // 128
                ld(qT[:, :w], fmrow(FM_GLA_Q, t0, w), "UFM")
                ld(kT[:, :w], fmrow(FM_GLA_K, t0, w), "UFM")
                ld(lr[:, :w], fmrow(FM_GLA_LR, t0, w), "UFM")
                ld(rT[0][:, :w], fmrow(FM_GLA_R0, t0, w), "UFM")
                ld(rT[1][:, :w], fmrow(FM_GLA_R1, t0, w), "UFM")
                ld(ktm3[:, :nb, :], UTM[TM_GLA_K * NT + t0:TM_GLA_K * NT + t0 + w, :].rearrange("(b p) d -> p b d", p=128), "UTM")
                for vc in range(2):
                    ld(vtm3[:, :nb, vc * 128:(vc + 1) * 128],
                       UTM[(TM_GLA_V0 + vc) * NT + t0:(TM_GLA_V0 + vc) * NT + t0 + w, :].rearrange("(b p) d -> p b d", p=128), "UTM")
                for bi in range(nb):
                    blk = slice(bi * 128, (bi + 1) * 128)
                    mm(PS[0][:, :128], lr[:, blk], WG, start=True, stop=False)
                    mm(PS[0][:, :128], ONES, BG, start=False, stop=True)
                    act(e1, PS[0][:, :128], AF.Exp, scale=-1.0)
                    act(sp, e1, AF.Ln, bias=ONE1)
                    mm(PS[1][:, :128], sp, TRI)
                    mm(PS[2][:, :128], SFX, sp)
                    cp(bl, PS[1][:, 127:128])
                    act(E1, PS[1][:, :128], AF.Exp, bias=LNS1)
                    act(E2, PS[1][:, :128], AF.Exp, scale=-1.0)
                    act(E3, PS[2][:, :128], AF.Exp)
                    act(eBl, bl, AF.Exp)
                    tt(qg, qT[:, blk], E1, ALU.mult)
                    tt(kg, kT[:, blk], E2, ALU.mult)
                    tt(kd, ktm3[:, bi, :], E3, ALU.mult)
                    mm(PS[3][:, :128], kg, qg)
                    tt(att, PS[3][:, :128], MUT, ALU.mult)
                    for vc in range(2):
                        mm(PS[4 + vc][:, :128], vtm3[:, bi, vc * 128:(vc + 1) * 128], att, start=True, stop=False)
                        mm(PS[4 + vc][:, :128], S[:, vc * 128:(vc + 1) * 128], qg, start=False, stop=True)
                        act(oraw[vc][:, blk], PS[4 + vc][:, :128], AF.Copy)
                    mm(PS[6][:, :256], kd, vtm3[:, bi, :])
                    stt(S, S, eBl, PS[6][:, :256], ALU.mult, ALU.add)
                    import os
                    if os.environ.get("GLARAW") == "2":
                        cp(oraw[0][:, blk], PS[1][:, :128])
                        tr(PS[3][:, :128], sp)
                        cp(oraw[1][:, blk], PS[3][:, :128])
                for vc in range(2):
                    act(sq[:, :w], oraw[vc][:, :w], AF.Square)
                    mm(PS[7][:, :w], ONES, sq[:, :w], start=(vc == 0), stop=(vc == 1))
                rstd_from_ssq(r, PS[7], 1.0 / 256, w)
                for vc in range(2):
                    act(sg[:, :w], rT[vc][:, :w], AF.Silu)
                    stt(o[:, :w], oraw[vc][:, :w], sm(l, vc), r[:, :w], ALU.mult, ALU.mult)
                    tt(o[:, :w], o[:, :w], sg[:, :w], ALU.mult)
                    import os
                    if os.environ.get("GLARAW"):
                        cp(o[:, :w], oraw[vc][:, :w])
                    st_OL(vc * 128, t0, w, o)

        def gdn_phase(l):
            phase()
            sc = 128.0 ** -0.5
            Sst = [alloc("S%d" % h, 128) for h in range(3)]
            ba = alloc("ba", 4 * 6)
            ba3 = ba.re("p (b c) -> p b c", c=6)
            DT4, EA4 = alloc("DT4", 12), alloc("EA4", 12)
            t1, gg, beta, G, GL, eG, eGlG, eGL, bG = [alloc("sc%d" % i, 12) for i in range(9)]
            xin = [alloc("xin%d" % i, 515) for i in range(3)]
            acc = alloc("acc", 512)
            yq, yk, yv = alloc("yq", 512), alloc("yk", 512), alloc("yv", 512)
            sq, rn, zt = alloc("sq", 512), alloc("rn", 512), alloc("zt", 512)
            oraw, o = alloc("oraw", 512), alloc("o", 512)
            names = ["kbg", "Kd", "betaV", "gbc", "d1", "d2", "EGb", "M", "Aqk", "qg", "MT", "Xa", "Xb",
                     "Na", "Nb", "NTa", "NTb", "nWT", "Vnew"]
            B = {n: alloc(n, 128) for n in names}
            for h in range(3):
                memset(Sst[h], 0.0)
            for bi in range(4):
                for h in range(3):
                    cp(DT4[:, bi * 3 + h:bi * 3 + h + 1], sm(l, 7 + h))
                    act(EA4[:, bi * 3 + h:bi * 3 + h + 1], sm(l, 4 + h), AF.Exp)
            ts(EA4, EA4, -1.0, ALU.mult)
            for h in range(3):
                for (t0, w) in tok_tiles(NT):
                    nb = w // 128
                    n3 = nb * 3
                    ld(ba3[:, :nb, :], UTM[TM_GDN_BA * NT + t0:TM_GDN_BA * NT + t0 + w, 0:6].rearrange("(b p) c -> p b c", p=128), "UTM")
                    for bi in range(nb):
                        tt(t1[:, bi * 3:bi * 3 + 3], ba3[:, bi, 3:6], DT4[:, bi * 3:bi * 3 + 3], ALU.add)
                        act(beta[:, bi * 3:bi * 3 + 3], ba3[:, bi, 0:3], AF.Exp, scale=-1.0)
                    act(t1[:, :n3], t1[:, :n3], AF.Exp)
                    act(t1[:, :n3], t1[:, :n3], AF.Ln, bias=ONE1)
                    tt(gg[:, :n3], t1[:, :n3], EA4[:, :n3], ALU.mult)
                    ts(beta[:, :n3], beta[:, :n3], 1.0, ALU.add)
                    recip(beta[:, :n3], beta[:, :n3])
                    for bi in range(nb):
                        c3 = slice(bi * 3, bi * 3 + 3)
                        mm(PS[6][:, bi * 3:bi * 3 + 3], TRI1, gg[:, c3])
                        mm(PS[6][:, 16 + bi * 3:16 + bi * 3 + 3], ONES, gg[:, c3])
                    cp(G[:, :n3], PS[6][:, 0:n3])
                    cp(GL[:, :n3], PS[6][:, 16:16 + n3])
                    act(eG[:, :n3], G[:, :n3], AF.Exp)
                    act(eGL[:, :n3], GL[:, :n3], AF.Exp)
                    tt(eGlG[:, :n3], GL[:, :n3], G[:, :n3], ALU.subtract)
                    act(eGlG[:, :n3], eGlG[:, :n3], AF.Exp)
                    tt(bG[:, :n3], beta[:, :n3], eG[:, :n3], ALU.mult)
                    ys = [yq, yk, yv]
                    for j3, (fm0, y) in enumerate(((FM_GDN_Q, yq), (FM_GDN_K, yk), (FM_GDN_V, yv))):
                        x = xin[j3]
                        ct = fm0 + h
                        if t0 == 0:
                            memset(x[:, 0:3], 0.0)
                            ld(x[:, 3:3 + w], UFM[ct * 128:(ct + 1) * 128, 0:w], "UFM")
                        else:
                            ld(x[:, 0:3 + w], UFM[ct * 128:(ct + 1) * 128, t0 - 3:t0 + w], "UFM")
                        cw = lambda i: sm(l, 10 + (j3 * 3 + h) * 4 + i)
                        ts(acc[:, :w], x[:, 3:3 + w], cw(3), ALU.mult)
                        for i in (2, 1, 0):
                            stt(acc[:, :w], x[:, i:i + w], cw(i), acc[:, :w], ALU.mult, ALU.add)
                        act(y[:, :w], acc[:, :w], AF.Silu)
                        if j3 < 2:
                            act(sq[:, :w], y[:, :w], AF.Square)
                            mm(PS[5][:, :w], ONES, sq[:, :w])
                            rstd_from_ssq(rn, PS[5], 1.0, w)
                            if j3 == 0:
                                stt(y[:, :w], y[:, :w], sc, rn[:, :w], ALU.mult, ALU.mult)
                            else:
                                tt(y[:, :w], y[:, :w], rn[:, :w], ALU.mult)
                    ld(zt[:, :w], UFM[(FM_GDN_Z + h) * 128:(FM_GDN_Z + h + 1) * 128, t0:t0 + w], "UFM")
                    S = Sst[h]
                    for bi in range(nb):
                        blk = slice(bi * 128, (bi + 1) * 128)
                        col = bi * 3 + h
                        c1 = lambda t: t[:, col:col + 1]
                        pa, pb, pc, pd, pe = [PS[i][:, :128] for i in range(5)]
                        tr(pa, yk[:, blk])
                        ts(B["kbg"], pa, c1(bG), ALU.mult)
                        ts(B["Kd"], pa, c1(eGlG), ALU.mult)
                        tr(pb, yv[:, blk])
                        ts(B["betaV"], pb, c1(beta), ALU.mult)
                        mm(pc, yk[:, blk], yk[:, blk])
                        mm(pd, yk[:, blk], yq[:, blk])
                        ts(B["gbc"], ONES, c1(gg), ALU.mult)
                        mm(pe, B["gbc"], TRI1)
                        ts(B["d1"], pe, c1(G), ALU.subtract, 0.0, ALU.max)
                        act(B["d1"], B["d1"], AF.Exp, scale=-1.0)
                        ts(B["d2"], pe, c1(G), ALU.subtract, 0.0, ALU.min)
                        act(B["d2"], B["d2"], AF.Exp)
                        act(B["EGb"], pe, AF.Exp)
                        tt(B["M"], pc, B["d1"], ALU.mult)
                        stt(B["M"], B["M"], c1(beta), MSL, ALU.mult, ALU.mult)
                        tt(B["Aqk"], pd, B["d2"], ALU.mult)
                        tt(B["Aqk"], B["Aqk"], MUT, ALU.mult)
                        tt(B["qg"], yq[:, blk], B["EGb"], ALU.mult)
                        tr(pa, B["M"])
                        act(B["MT"], pa, AF.Copy)
                        tt(B["Xa"], IDENT, pa, ALU.subtract)
                        mm(pb, B["MT"], B["M"])
                        mm(pc, B["M"], B["MT"])
                        cp(B["Na"], pb)
                        act(B["NTa"], pc, AF.Copy)
                        X, X2 = B["Xa"], B["Xb"]
                        N, N2, NTt, NT2 = B["Na"], B["Nb"], B["NTa"], B["NTb"]
                        for it in range(6):
                            mm(pd, N, X)
                            tt(X2, pd, X, ALU.add)
                            X, X2 = X2, X
                            if it < 5:
                                mm(pb, NTt, N)
                                mm(pc, N, NTt)
                                cp(N2, pb)
                                act(NT2, pc, AF.Copy)
                                N, N2, NTt, NT2 = N2, N, NT2, NTt
                        mm(pa, B["kbg"], X)
                        ts(B["nWT"], pa, -1.0, ALU.mult)
                        mm(pb, X, B["betaV"], start=True, stop=False)
                        mm(pb, B["nWT"], S, start=False, stop=True)
                        cp(B["Vnew"], pb)
                        mm(pc, S, B["qg"], start=True, stop=False)
                        mm(pc, B["Vnew"], B["Aqk"], start=False, stop=True)
                        act(oraw[:, blk], pc, AF.Copy)
                        mm(pd, B["Kd"], B["Vnew"])
                        stt(S, S, c1(eGL), pd, ALU.mult, ALU.add)
                    act(sq[:, :w], oraw[:, :w], AF.Square)
                    mm(PS[7][:, :w], ONES, sq[:, :w])
                    rstd_from_ssq(rn, PS[7], 1.0 / 128, w)
                    act(sq[:, :w], zt[:, :w], AF.Silu)
                    stt(o[:, :w], oraw[:, :w], sm(l, 2), rn[:, :w], ALU.mult, ALU.mult)
                    tt(o[:, :w], o[:, :w], sq[:, :w], ALU.mult)
                    st_OL(256 + h * 128, t0, w, o)

        YP = dram("YP", [4 * D, SL])
        YS = dram("YS", [D, SL])

        def outproj_phase(l):
            phase()
            ot = alloc("ot", 8 * 512)
            wt = [alloc("wt%d" % i, 8 * 128) for i in range(2)]
            ev = [alloc("ev%d" % i, 512) for i in range(2)]
            nw = 0
            for (t0, w) in tok_tiles(NT):
                ot3 = ot[:, :8 * w].re("p (k t) -> p k t", t=w)
                ld(ot3, OL[:, t0:t0 + w].rearrange("(k p) t -> p k t", p=128), "OL")
                for j in range(KC):
                    w3 = wt[nw % 2].re("p (k c) -> p k c", c=128)
                    for half in range(2):
                        T = l * KC + j
                        base = (T // 4) * 1024 + half * 512 + (T % 4) * 128
                        ld(w3[:, half * 4:(half + 1) * 4, :],
                           WOUT[base:base + 128, :].rearrange("p (k c) -> p k c", c=128), "WOUT#%d" % (T // 4))
                    e = ev[nw % 2]
                    nw += 1
                    ps = PS[j % 2]
                    for lc in range(8):
                        mm(ps[:, :w], w3[:, lc, :], ot3[:, lc, :], start=(lc == 0), stop=(lc == 7))
                    if nw % 2:
                        cp(e[:, :w], ps[:, :w])
                    else:
                        act(e[:, :w], ps[:, :w], AF.Copy)
                    t = t0
                    while t < t0 + w:
                        s = t // SL
                        te = min(t0 + w, (s + 1) * SL)
                        st(YP[j * 512 + s * 128:j * 512 + (s + 1) * 128, t - s * SL:te - s * SL], "YP#%d" % j, e[:, t - t0:te - t0])
                        t = te
            for j in range(KC):
                P.dma("pool", lambda e, j=j: e.collective_compute(
                    "ReduceScatter", ALU.add, replica_groups=quads, ins=[YP[j * 512:(j + 1) * 512, :].opt()],
                    outs=[YS[j * 128:(j + 1) * 128, :].opt()]),
                    reads=["YP#%d" % j], writes=["YS#%d" % j], chan="cc_rs_y")

        hsrc_ap, hsrc_key = hT_in, None
        for l in range(L + 1):
            if cfg.stop == 0:
                break
            last = (l == L)
            hd = HB[l % 2]
            norm_phase(l, hsrc_ap, hsrc_key, l > 0, hd, "HB%d" % (l % 2),
                       out_ext if last else XNS, "OUT" if last else "XNS")
            if l > 0:
                hsrc_ap, hsrc_key = hd, "HB%d" % (l % 2)
            if last or cfg.stop == 1:
                break
            for i in range(2 * KC):
                allgather(XN[i * 256:(i + 1) * 256, :], "XN#%d" % i, XNS[i * 64:(i + 1) * 64, :], "XNS#%d" % (i // 2),
                          quads, "cc_xn")
            inproj_phase(l)
            if cfg.debug and l == 0:
                d2d(dbg["ufm"], "dbgufm", UFM, "UFM", "c_dbg1")
                d2d(dbg["utm"], "dbgutm", UTM, "UTM", "c_dbg2")
            if cfg.stop == 2:
                break
            if "sb" in cfg.mixers:
                sb_phase(l)
            if "gla" in cfg.mixers:
                gla_phase(l)
            if "gdn" in cfg.mixers:
                gdn_phase(l)
            if cfg.debug and l == 0:
                d2d(dbg["ol"], "dbgol", OL, "OL", "c_dbg3")
            outproj_phase(l)
        P.barrier()
        P.emit()
    return nc


def prep_inputs(cfg, x, meta, norm_g, w_in, gla_w_gate, gla_b_gate, gla_norm_g, gdn_conv_w,
                gdn_a_log, gdn_dt_bias, gdn_norm_g, sb_norm_g, w_out, final_g):
    D, KC, NT, SL, L = cfg.D, cfg.KC, cfg.NT, cfg.SL, cfg.depth
    KH = KC // 2
    NSM = 64
    f32 = np.float32
    consts = make_consts()
    normg = np.zeros((128, L + 1, KC), f32)
    for l in range(L):
        normg[:, l, :] = np.asarray(norm_g[l], f32).reshape(KC, 128).T
    normg[:, L, :] = np.asarray(final_g, f32).reshape(KC, 128).T
    in_maps = []
    wg_cache = {}
    for c in range(8):
        b, g = c // 4, c % 4
        h0 = np.zeros((NT, D), f32)
        h0[112:128] = meta
        h0[128:] = x[b]
        hT = np.ascontiguousarray(h0[g * SL:(g + 1) * SL].T)
        if g not in wg_cache:
            tiles = col_tiles(g)
            cols = np.concatenate(tiles)
            wl = []
            for l in range(L):
                wsel = np.where(cols[None, :] >= 0, np.asarray(w_in[l])[:, np.maximum(cols, 0)], 0.0).astype(f32)
                w4 = wsel.reshape(KC, 128, NCT, 128).transpose(2, 1, 0, 3)
                wl.append(w4)
            wg_cache[g] = np.stack(wl)
        wfull = wg_cache[g]
        win = np.ascontiguousarray(wfull[:, :, :, b * KH:(b + 1) * KH, :]).reshape(L * NCT * 128, KH * 128)
        rows = np.concatenate([g * 256 + np.arange(256), 1024 + 3 * g * 128 + np.arange(384),
                               2560 + 3 * g * 128 + np.arange(384)])
        wo = []
        for l in range(L):
            wp = np.asarray(w_out[l], f32)[rows]
            w4 = wp.reshape(8, 128, KC, 128).transpose(2, 1, 0, 3)
            wo.append(w4[:, :, b * 4:(b + 1) * 4, :])
        wout = np.ascontiguousarray(np.stack(wo)).reshape(L * KC * 128, 4 * 128)
        small = np.zeros((128, L, NSM), f32)
        for l in range(L):
            small[:, l, 0] = gla_norm_g[l][0:128]
            small[:, l, 1] = gla_norm_g[l][128:256]
            small[:, l, 2] = gdn_norm_g[l]
            small[:, l, 3] = sb_norm_g[l]
            for h in range(3):
                small[:, l, 4 + h] = gdn_a_log[l][3 * g + h]
                small[:, l, 7 + h] = gdn_dt_bias[l][3 * g + h]
            for j3 in range(3):
                for h in range(3):
                    ch = j3 * 1536 + (3 * g + h) * 128 + np.arange(128)
                    for i in range(4):
                        small[:, l, 10 + (j3 * 3 + h) * 4 + i] = gdn_conv_w[l][i, ch]
        glawg = np.ascontiguousarray(np.asarray(gla_w_gate, f32)[:L, :, g * 128:(g + 1) * 128].transpose(1, 0, 2))
        glabg = np.ascontiguousarray(np.asarray(gla_b_gate, f32)[:L, g * 128:(g + 1) * 128][None])
        in_maps.append({"hT": hT, "consts": consts, "normg": normg, "win": win, "wout": wout,
                        "small": small, "glawg": glawg, "glabg": glabg})
    return in_maps


def run(cfg, **inputs):
    inputs = {k: np.asarray(v) for k, v in inputs.items()}
    in_maps = prep_inputs(cfg, **inputs)
    nc = build(cfg)
    res = run_bass_kernel_spmd(nc, in_maps, core_ids=list(range(8)))
    D, NT, SL = cfg.D, cfg.NT, cfg.SL
    out = np.zeros((2, NT, D), np.float32)
    for c in range(8):
        b, g = c // 4, c % 4
        out[b, g * SL:(g + 1) * SL, :] = res.results[c]["outT"].T
    return np.ascontiguousarray(out[:, 128:, :]), res


def kernel(**inputs):
    out, _ = run(Cfg(), **inputs)
    return out
```

```python
import numpy as np
import concourse.bass as bass
import concourse.mybir as mybir
from concourse.bass_utils import run_bass_kernel_spmd
from contextlib import ExitStack

F32 = mybir.dt.float32
AF = mybir.ActivationFunctionType
ALU = mybir.AluOpType

EPOCH = 20000
DEPOCH = 1000
SAME_ENGINE_SYNC = True
EPS = 1e-6


class V:
    def __init__(self, ap, key):
        self.ap = ap
        self.key = key

    def __getitem__(self, idx):
        return V(self.ap[idx], self.key)

    def re(self, pat, **kw):
        return V(self.ap.rearrange(pat, **kw), self.key)


class Prog:
    ENGS = ("pe", "act", "dve", "pool", "sp")

    def __init__(self, nc, es):
        self.nc = nc
        self.es = es
        self.ops = []
        self.cnt = {e: 0 for e in self.ENGS}
        self.last_w = {}
        self.readers = {}
        self.chan_cnt = {}
        self.sems = {}

    def _deps(self, reads, writes):
        deps = []
        for k in reads:
            t = self.last_w.get(k)
            if t is not None:
                deps.append((t, True))
        for k in writes:
            t = self.last_w.get(k)
            if t is not None:
                deps.append((t, False))
            deps.extend((r, False) for r in self.readers.get(k, []))
        return deps

    def _commit(self, tok, reads, writes):
        for k in reads:
            self.readers.setdefault(k, []).append(tok)
        for k in writes:
            self.last_w[k] = tok
            self.readers[k] = []

    def op(self, eng, fn, reads=(), writes=()):
        reads = [k for k in reads if k is not None]
        writes = list(writes) + [k for k in reads if k.startswith("ps") and k not in writes]
        deps = self._deps(reads, writes)
        self.cnt[eng] += 1
        tok = ("c", eng, self.cnt[eng])
        self.ops.append((eng, fn, deps, tok))
        self._commit(tok, reads, writes)
        return tok

    def dma(self, eng, fn, reads=(), writes=(), chan=None):
        deps = self._deps(reads, writes)
        c = self.chan_cnt.get(chan, 0) + 1
        self.chan_cnt[chan] = c
        tok = ("d", chan, c)
        self.ops.append((eng, fn, deps, tok))
        self._commit(tok, reads, writes)
        return tok

    def barrier(self):
        deps = [(("c", e, n), True) for e, n in self.cnt.items() if n > 0]
        deps += [(("d", ch, n), True) for ch, n in self.chan_cnt.items()]
        for e in self.ENGS:
            self.ops.append((e, None, list(deps), None))
        self.last_w = {}
        self.readers = {}

    def _tok_sem(self, tok):
        kind, who, n = tok
        if kind == "c":
            return ("c_%s_%d" % (who, (n - 1) // EPOCH), (n - 1) % EPOCH + 1)
        return ("d_%s_%d" % (who, (n - 1) // DEPOCH), (1 if who.startswith("cc") else 16) * ((n - 1) % DEPOCH + 1))

    def emit(self):
        nc = self.nc
        for eng, fn, deps, tok in self.ops:
            if tok is not None:
                s = self._tok_sem(tok)[0]
                if s not in self.sems:
                    self.sems[s] = self.es.enter_context(nc.semaphore(s))
        EI = {e: i for i, e in enumerate(self.ENGS)}
        know_c = {e: [0] * len(self.ENGS) for e in self.ENGS}
        know_d = {e: {} for e in self.ENGS}
        clock = {}
        plan = []
        for eng, fn, deps, tok in self.ops:
            kc, kd = know_c[eng], know_d[eng]
            need = []
            for d, is_raw in deps:
                if d[0] == "c":
                    if d[1] == eng:
                        if eng == "pe" or not SAME_ENGINE_SYNC or not is_raw:
                            continue
                        need.append(d)
                        continue
                    if kc[EI[d[1]]] >= d[2]:
                        continue
                else:
                    if kd.get(d[1], 0) >= d[2]:
                        continue
                need.append(d)
            best = {}
            for d in need:
                key = (d[0], d[1])
                if key not in best or best[key][2] < d[2]:
                    best[key] = d
            waits = []
            newd = None
            for d in best.values():
                waits.append(self._tok_sem(d))
                if d[0] == "c" and d[1] == eng:
                    continue
                ck = clock.get(d)
                if d[0] == "c":
                    if kc[EI[d[1]]] < d[2]:
                        kc[EI[d[1]]] = d[2]
                else:
                    if kd.get(d[1], 0) < d[2]:
                        if newd is None:
                            newd = dict(kd)
                        newd[d[1]] = d[2]
                if ck is not None:
                    cc, cd = ck
                    for i in range(len(cc)):
                        if kc[i] < cc[i]:
                            kc[i] = cc[i]
                    for ch, n in cd.items():
                        if (newd if newd is not None else kd).get(ch, 0) < n:
                            if newd is None:
                                newd = dict(kd)
                            newd[ch] = n
            if newd is not None:
                know_d[eng] = newd
            plan.append(waits)
            if tok is not None:
                clock[tok] = (tuple(kc), know_d[eng])
        per_eng = {e: [] for e in self.ENGS}
        for o, waits in zip(self.ops, plan):
            per_eng[o[0]].append((o, waits))
        with nc.Block() as block:
            def make(e):
                def body(engh):
                    for (eng, fn, deps, tok), waits in per_eng[e]:
                        for s, v in waits:
                            engh.wait_ge(self.sems[s], v)
                        if fn is None:
                            continue
                        ins = fn(engh)
                        s, v = self._tok_sem(tok)
                        ins.then_inc(self.sems[s], 16 if (tok[0] == "d" and not tok[1].startswith("cc")) else 1)
                return body
            for e, deco in (("pe", block.tensor), ("act", block.scalar), ("dve", block.vector),
                            ("pool", block.gpsimd), ("sp", block.sync)):
                if per_eng[e]:
                    deco(make(e))


class Cfg:
    def __init__(self, D=4096, NB=65, depth=2, debug=False, mixers=("sb", "gla", "gdn")):
        self.mixers = mixers
        self.stop = 99
        self.D = D
        self.KC = D // 128
        self.NB = NB
        self.NT = NB * 128
        assert self.NT % 4 == 0
        self.SL = self.NT // 4
        self.depth = depth
        self.debug = debug
        self.DMIX = 4096
        self.CC = 32


FM_GLA_Q, FM_GLA_K, FM_GLA_R0, FM_GLA_R1, FM_GLA_LR = 0, 1, 2, 3, 4
FM_GDN_Q, FM_GDN_K, FM_GDN_V, FM_GDN_Z = 5, 8, 11, 14
FM_SB_Q, FM_SB_K, FM_SB_G = 17, 20, 23
NFM = 26
TM_GLA_K, TM_GLA_V0, TM_GLA_V1, TM_GDN_BA, TM_SB_V = 0, 1, 2, 3, 4
NTM = 7
NCT = NFM + NTM

_OFF = {}
_o = 0
for _n, _w in (("gq", 512), ("gk", 512), ("gv", 1024), ("gr", 1024), ("glr", 16),
               ("dq", 1536), ("dk", 1536), ("dv", 1536), ("dz", 1536), ("db", 12), ("da", 12),
               ("sq", 1536), ("sk", 1536), ("sv", 1536), ("sg", 1536)):
    _OFF[_n] = _o
    _o += _w
D_IN = _o


def col_tiles(g):
    def rng(name, start, n):
        a = np.full(128, -1, np.int64)
        a[:n] = _OFF[name] + start + np.arange(n)
        return a
    fm = [None] * NFM
    fm[FM_GLA_Q] = rng("gq", g * 128, 128)
    fm[FM_GLA_K] = rng("gk", g * 128, 128)
    fm[FM_GLA_R0] = rng("gr", g * 256, 128)
    fm[FM_GLA_R1] = rng("gr", g * 256 + 128, 128)
    fm[FM_GLA_LR] = rng("glr", 0, 16)
    for h in range(3):
        hh = 3 * g + h
        fm[FM_GDN_Q + h] = rng("dq", hh * 128, 128)
        fm[FM_GDN_K + h] = rng("dk", hh * 128, 128)
        fm[FM_GDN_V + h] = rng("dv", hh * 128, 128)
        fm[FM_GDN_Z + h] = rng("dz", hh * 128, 128)
        fm[FM_SB_Q + h] = rng("sq", hh * 128, 128)
        fm[FM_SB_K + h] = rng("sk", hh * 128, 128)
        fm[FM_SB_G + h] = rng("sg", hh * 128, 128)
    tm = [None] * NTM
    tm[TM_GLA_K] = rng("gk", g * 128, 128)
    tm[TM_GLA_V0] = rng("gv", g * 256, 128)
    tm[TM_GLA_V1] = rng("gv", g * 256 + 128, 128)
    ba = np.full(128, -1, np.int64)
    ba[0:3] = _OFF["db"] + 3 * g + np.arange(3)
    ba[3:6] = _OFF["da"] + 3 * g + np.arange(3)
    tm[TM_GDN_BA] = ba
    for h in range(3):
        tm[TM_SB_V + h] = rng("sv", (3 * g + h) * 128, 128)
    return fm + tm


C_ID, C_ONES, C_UI, C_TRI, C_SFX, C_MUT, C_MSL, C_TRI1 = 0, 128, 256, 384, 512, 640, 768, 896
C_SBM = 1024
C_VALID = C_SBM + 4 * 512
C_ONE = C_VALID + 1
C_EPS = C_ONE + 1
C_LNS = C_EPS + 1
NCONST = C_LNS + 1


def make_consts():
    c = np.zeros((128, NCONST), np.float32)
    j = np.arange(128)[:, None]
    f = np.arange(128)[None, :]
    c[:, C_ID:C_ID + 128] = (j == f)
    c[:, C_ONES:C_ONES + 128] = 1.0
    c[:, C_UI:C_UI + 128] = (j >= f)
    c[:, C_TRI:C_TRI + 128] = (j <= f) * (-1.0 / 16.0)
    c[:, C_SFX:C_SFX + 128] = (j > f) * (-1.0 / 16.0)
    c[:, C_MUT:C_MUT + 128] = (j <= f)
    c[:, C_MSL:C_MSL + 128] = (f < j)
    c[:, C_TRI1:C_TRI1 + 128] = (j <= f)
    t = np.arange(512)[None, :]
    for m in range(4):
        c[:, C_SBM + m * 512:C_SBM + (m + 1) * 512] = ((m * 128 + j) < t)
    c[:, C_VALID] = (np.arange(128) >= 112)
    c[:, C_ONE] = 1.0
    c[:, C_EPS] = EPS
    c[:, C_LNS] = np.log(128.0 ** -0.5)
    return c


ARENA = 40000


def build(cfg):
    nc = bass.Bass("TRN2", target_bir_lowering=False)
    D, KC, NB, NT, SL, L = cfg.D, cfg.KC, cfg.NB, cfg.NT, cfg.SL, cfg.depth
    KH = KC // 2
    dram_in = lambda name, shape: nc.dram_tensor(name, shape, F32, kind="ExternalInput").ap()
    dram_out = lambda name, shape: nc.dram_tensor(name, shape, F32, kind="ExternalOutput").ap()
    dram = lambda name, shape: nc.dram_tensor(name, shape, F32).ap()

    hT_in = dram_in("hT", [D, SL])
    consts_in = dram_in("consts", [128, NCONST])
    normg_in = dram_in("normg", [128, L + 1, KC])
    win_in = dram_in("win", [L * NCT * 128, KH * 128])
    wout_in = dram_in("wout", [L * KC * 128, 4 * 128])
    NSM = 64
    small_in = dram_in("small", [128, L, NSM])
    glawg_in = dram_in("glawg", [16, L, 128])
    glabg_in = dram_in("glabg", [1, L, 128])
    out_ext = dram_out("outT", [D, SL])

    win_b = dram("win_b", [L * NCT * 128, KH * 128])
    WIN = dram("WIN", [2 * L * NCT * 128, KH * 128])
    wout_b = dram("wout_b", [L * KC * 128, 4 * 128])
    WOUT = dram("WOUT", [2 * L * KC * 128, 4 * 128])
    XNS = dram("XNS", [D, SL])
    XN = dram("XN", [4 * D, SL])
    UFM = dram("UFM", [NFM * 128, NT])
    UTM = dram("UTM", [NTM * NT, 128])
    OL = dram("OL", [1024, NT])
    HB = [dram("HB%d" % i, [D, SL]) for i in range(2)]
    dbg = {}
    if cfg.debug:
        dbg["ufm"] = dram_out("dbg_ufm", [NFM * 128, NT])
        dbg["utm"] = dram_out("dbg_utm", [NTM * NT, 128])
        dbg["ol"] = dram_out("dbg_ol", [1024, NT])

    with ExitStack() as es:
        P = Prog(nc, es)
        arena_t = es.enter_context(nc.sbuf_tensor("arena", [128, ARENA], F32))
        const_t = es.enter_context(nc.sbuf_tensor("constt", [128, NCONST], F32))
        ng_t = es.enter_context(nc.sbuf_tensor("ngt", [128, (L + 1) * KC], F32))
        small_t = es.enter_context(nc.sbuf_tensor("smallt", [128, L * NSM], F32))
        PS = [V(es.enter_context(nc.psum_tensor("ps%d" % i, [128, 512], F32))[:], "ps%d" % i)
              for i in range(8)]
        CONST = V(const_t[:], "const")
        NG = V(ng_t[:], "ng")
        SMALL = V(small_t[:], "small")
        state = {"off": 0, "phase": 0}

        def phase():
            P.barrier()
            state["off"] = 0
            state["phase"] += 1

        def alloc(name, n):
            off = state["off"]
            assert off + n <= ARENA, (name, off, n)
            state["off"] = off + n
            return V(arena_t[:, off:off + n], "%s@%d" % (name, state["phase"]))

        cst = lambda c0, n=128: CONST[:, c0:c0 + n]
        IDENT, ONES = cst(C_ID), cst(C_ONES)
        ONE1, EPS1, LNS1 = cst(C_ONE, 1), cst(C_EPS, 1), cst(C_LNS, 1)

        def ld(out, in_ap, rkey=None, eng="sp"):
            rk = [] if rkey is None else (list(rkey) if isinstance(rkey, (list, tuple)) else [rkey])
            P.dma(eng, lambda e: e.dma_start(out=out.ap, in_=in_ap), reads=rk,
                  writes=[out.key], chan="L" + out.key)

        def st(out_ap, wkey, in_, eng="pool"):
            P.dma(eng, lambda e: e.dma_start(out=out_ap, in_=in_.ap), reads=[in_.key],
                  writes=[wkey] if wkey else [], chan="S" + in_.key)

        def d2d(out_ap, wkey, in_ap, rkey, chan, eng="sp"):
            nrows = out_ap.shape[0]
            for r0 in range(0, nrows, 1024):
                r1 = min(nrows, r0 + 1024)
                P.dma(eng, lambda e, r0=r0, r1=r1: e.dma_start(out=out_ap[r0:r1, :], in_=in_ap[r0:r1, :]),
                      reads=[rkey] if rkey else [], writes=[wkey], chan=chan)

        def allgather(out_ap, wkey, in_ap, rkey, groups, chan):
            P.dma("pool", lambda e: e.collective_compute(
                "AllGather", ALU.bypass, replica_groups=groups, ins=[in_ap.opt()], outs=[out_ap.opt()]),
                reads=[rkey], writes=[wkey], chan=chan)

        def mm(out, lhsT, rhs, start=True, stop=True):
            P.op("pe", lambda e: e.matmul(out.ap, lhsT=lhsT.ap, rhs=rhs.ap, start=start, stop=stop),
                 reads=[lhsT.key, rhs.key], writes=[out.key])

        def tr(out, in_):
            P.op("pe", lambda e: e.transpose(out.ap, in_.ap, IDENT.ap),
                 reads=[in_.key, "const"], writes=[out.key])

        def act(out, in_, func, bias=None, scale=None, eng="act"):
            kw = {}
            rk = [in_.key]
            if bias is not None:
                if isinstance(bias, V):
                    kw["bias"] = bias.ap
                    rk.append(bias.key)
                else:
                    kw["bias"] = bias
            if scale is not None:
                if isinstance(scale, V):
                    kw["scale"] = scale.ap
                    rk.append(scale.key)
                else:
                    kw["scale"] = scale
            P.op("act", lambda e: e.activation(out=out.ap, in_=in_.ap, func=func, **kw),
                 reads=rk, writes=[out.key])

        def tt(out, a, b, op, eng="dve"):
            P.op(eng, lambda e: e.tensor_tensor(out=out.ap, in0=a.ap, in1=b.ap, op=op),
                 reads=[a.key, b.key], writes=[out.key])

        def ts(out, a, s1, op0, s2=None, op1=None, eng="dve"):
            rk = [a.key]
            v1 = s1
            if isinstance(s1, V):
                v1 = s1.ap
                rk.append(s1.key)
            v2 = s2
            if isinstance(s2, V):
                v2 = s2.ap
                rk.append(s2.key)
            if op1 is None:
                P.op(eng, lambda e: e.tensor_scalar(out=out.ap, in0=a.ap, scalar1=v1, scalar2=None, op0=op0),
                     reads=rk, writes=[out.key])
            else:
                P.op(eng, lambda e: e.tensor_scalar(out=out.ap, in0=a.ap, scalar1=v1, scalar2=v2,
                                                    op0=op0, op1=op1), reads=rk, writes=[out.key])

        def stt(out, a, s, b, op0, op1):
            rk = [a.key, b.key]
            sv = s
            if isinstance(s, V):
                sv = s.ap
                rk.append(s.key)
            P.op("dve", lambda e: e.scalar_tensor_tensor(out=out.ap, in0=a.ap, scalar=sv, in1=b.ap,
                                                         op0=op0, op1=op1), reads=rk, writes=[out.key])

        def cp(out, in_, eng="dve"):
            P.op(eng, lambda e: e.tensor_copy(out=out.ap, in_=in_.ap), reads=[in_.key], writes=[out.key])

        def memset(out, val, eng="dve"):
            P.op(eng, lambda e: e.memset(out.ap, val), writes=[out.key])

        def recip(out, in_):
            P.op("dve", lambda e: e.reciprocal(out=out.ap, in_=in_.ap), reads=[in_.key], writes=[out.key])

        def rstd_from_ssq(dst, ssq_ps, inv_n, w):
            act(dst[:, :w], ssq_ps[:, :w], AF.Sqrt, bias=EPS1, scale=inv_n)
            recip(dst[:, :w], dst[:, :w])

        ld(CONST, consts_in)
        ld(NG, normg_in.rearrange("p l k -> p (l k)"))
        ld(SMALL, small_in.rearrange("p l k -> p (l k)"))
        d2d(win_b, "win_b", win_in, None, "c_win")
        d2d(wout_b, "wout_b", wout_in, None, "c_wout")
        pairs = [[0, 4], [1, 5], [2, 6], [3, 7]]
        quads = [[0, 1, 2, 3], [4, 5, 6, 7]]
        for T in range(L * NCT):
            allgather(WIN[T * 256:(T + 1) * 256, :], "WIN#%d" % T, win_b[T * 128:(T + 1) * 128, :], "win_b", pairs, "cc_win")
        for q in range(L * KC // 4):
            allgather(WOUT[q * 1024:(q + 1) * 1024, :], "WOUT#%d" % q, wout_b[q * 512:(q + 1) * 512, :], "wout_b", pairs, "cc_wout")

        def tok_tiles(n, w=512):
            return [(t0, min(w, n - t0)) for t0 in range(0, n, w)]

        def norm_phase(l, src_ap, src_key, add_y, hdst_ap, hdst_key, dst_ap, dst_key):
            phase()
            W_ = 256
            h = alloc("h", KC * W_)
            y = alloc("y", KC * W_)
            sq = [alloc("sq%d" % i, W_) for i in range(2)]
            r = alloc("r", W_)
            xo = [alloc("xo%d" % i, W_) for i in range(2)]
            for (t0, w) in tok_tiles(SL, W_):
                h3 = h[:, :KC * w].re("p (k t) -> p k t", t=w)
                ld_chunks(h3, src_ap[:, t0:t0 + w].rearrange("(k p) t -> p k t", p=128), src_key, KC, 8)
                if add_y:
                    y3 = y[:, :KC * w].re("p (k t) -> p k t", t=w)
                    ysrc = YS[:, t0:t0 + w].rearrange("(k p) t -> p k t", p=128)
                    for k0 in range(0, KC, 8):
                        k1 = min(KC, k0 + 8)
                        ld(y3[:, k0:k1, :], ysrc[:, k0:k1, :], ["YS#%d" % k for k in range(k0, k1)])
                    tt(h[:, :KC * w], h[:, :KC * w], y[:, :KC * w], ALU.add)
                    for k0 in range(0, KC, 8):
                        k1 = min(KC, k0 + 8)
                        st(hdst_ap[k0 * 128:k1 * 128, t0:t0 + w].rearrange("(k p) t -> p k t", p=128), hdst_key,
                           h3[:, k0:k1, :])
                for kc in range(KC):
                    s = sq[kc % 2]
                    act(s[:, :w], h3[:, kc, :], AF.Square)
                    mm(PS[0][:, :w], ONES, s[:, :w], start=(kc == 0), stop=(kc == KC - 1))
                rstd_from_ssq(r, PS[0], 1.0 / D, w)
                for kc in range(KC):
                    x = xo[kc % 2]
                    stt(x[:, :w], h3[:, kc, :], NG[:, l * KC + kc:l * KC + kc + 1], r[:, :w], ALU.mult, ALU.mult)
                    st(dst_ap[kc * 128:(kc + 1) * 128, t0:t0 + w], "%s#%d" % (dst_key, kc), x[:, :w])

        def inproj_phase(l):
            phase()
            xn = alloc("xn", KC * 512)
            wt = [alloc("wt%d" % i, KC * 128) for i in range(2)]
            ev = [alloc("ev%d" % i, 512) for i in range(2)]
            ev2 = [alloc("ev2%d" % i, 512) for i in range(2)]
            nw = 0
            ne = 0
            for (t0, w) in tok_tiles(NT):
                xn3 = xn[:, :KC * w].re("p (k t) -> p k t", t=w)
                t = t0
                while t < t0 + w:
                    rk = t // SL
                    te = min(t0 + w, (rk + 1) * SL)
                    for ph in range(2):
                        srcv = XN.rearrange("(k ph r pl) t -> ph r pl k t", ph=2, r=4, pl=64)[ph, rk]
                        for k0 in range(0, KC, 8):
                            k1 = min(KC, k0 + 8)
                            ld(xn3[ph * 64:(ph + 1) * 64, k0:k1, t - t0:te - t0],
                               srcv[:, k0:k1, t - rk * SL:te - rk * SL],
                               ["XN#%d" % (2 * k + ph) for k in range(k0, k1)])
                    t = te
                for ct in range(NCT):
                    wtile = wt[nw % 2]
                    nw += 1
                    w3 = wtile.re("p (k c) -> p k c", c=128)
                    for half in range(2):
                        T = l * NCT + ct
                        base = (T * 2 + half) * 128
                        ld(w3[:, half * KH:(half + 1) * KH, :],
                           WIN[base:base + 128, :].rearrange("p (k c) -> p k c", c=128), "WIN#%d" % T)
                    if ct < NFM:
                        ps = PS[ct % 2]
                        for kc in range(KC):
                            mm(ps[:, :w], w3[:, kc, :], xn3[:, kc, :], start=(kc == 0), stop=(kc == KC - 1))
                        e = ev[ne % 2]
                        ne += 1
                        if ne % 2:
                            cp(e[:, :w], ps[:, :w], eng="dve")
                        else:
                            act(e[:, :w], ps[:, :w], AF.Copy)
                        st(UFM[ct * 128:(ct + 1) * 128, t0:t0 + w], "UFM", e[:, :w])
                    else:
                        tmi = ct - NFM
                        nblk = w // 128
                        ps = PS[ct % 2]
                        for kc in range(KC):
                            mm(ps[:, :w], w3[:, kc, :], xn3[:, kc, :], start=(kc == 0), stop=(kc == KC - 1))
                        e = ev[ne % 2]
                        ne += 1
                        act(e[:, :w], ps[:, :w], AF.Copy)
                        ps2 = PS[2 + (tmi % 2)]
                        for bi in range(nblk):
                            tr(ps2[:, bi * 128:(bi + 1) * 128], e[:, bi * 128:(bi + 1) * 128])
                        e2 = ev2[tmi % 2]
                        cp(e2[:, :w], ps2[:, :w], eng="dve")
                        st(UTM[tmi * NT + t0:tmi * NT + t0 + w, :].rearrange("(b p) c -> p b c", p=128),
                           "UTM", e2[:, :w].re("p (b c) -> p b c", c=128))

        def ld_chunks(dst3, src3, key, n, step):
            for a in range(0, n, step):
                b = min(n, a + step)
                ld(dst3[:, a:b, :], src3[:, a:b, :], key)

        def st_OL(c0, t0, w, tile):
            st(OL[c0:c0 + 128, t0:t0 + w], "OL", tile[:, :w])

        SBM = [CONST[:, C_SBM + m * 512:C_SBM + (m + 1) * 512] for m in range(4)]
        VALID1 = cst(C_VALID, 1)
        UI, TRI, SFX, MUT, MSL, TRI1 = cst(C_UI), cst(C_TRI), cst(C_SFX), cst(C_MUT), cst(C_MSL), cst(C_TRI1)
        NSM_ = NSM

        def sm(l, idx):
            return SMALL[:, l * NSM_ + idx:l * NSM_ + idx + 1]

        def sb_phase(l):
            import os
            CUT = int(os.environ.get("CUT", "99"))
            phase()
            sc = 128.0 ** -0.5
            kT = alloc("kT", NT)
            vtm = alloc("vtm", NB * 128)
            vtm3 = vtm.re("p (b d) -> p b d", d=128)
            qs, nqs, S = alloc("qs", 512), alloc("nqs", 512), alloc("S", 512)
            E = [alloc("E%d" % i, 512) for i in range(2)]
            SP = [alloc("SP%d" % i, 512) for i in range(2)]
            A = [alloc("A%d" % i, 512) for i in range(2)]
            osb, sq, r, gt = alloc("osb", 512), alloc("sq", 512), alloc("r", 512), alloc("gt", 512)
            for h in range(3):
                ld(kT, UFM[(FM_SB_K + h) * 128:(FM_SB_K + h + 1) * 128, :], "UFM")
                ld_chunks(vtm3, UTM[(TM_SB_V + h) * NT:(TM_SB_V + h + 1) * NT, :].rearrange("(b p) d -> p b d", p=128),
                          "UTM", NB, 4)
                for (q0, w) in tok_tiles(NT):
                    qb0, nqb = q0 // 128, w // 128
                    ld(qs[:, :w], UFM[(FM_SB_Q + h) * 128:(FM_SB_Q + h + 1) * 128, q0:q0 + w], "UFM")
                    ld(gt[:, :w], UFM[(FM_SB_G + h) * 128:(FM_SB_G + h + 1) * 128, q0:q0 + w], "UFM")
                    ts(nqs[:, :w], qs[:, :w], -sc, ALU.mult)
                    ts(qs[:, :w], qs[:, :w], sc, ALU.mult)
                    first = True
                    for i, kb in enumerate(range(qb0 + nqb - 1, -1, -1)):
                        Zb, Tb, e, sp, a = PS[i % 2], PS[2 + i % 2], E[i % 2], SP[i % 2], A[i % 2]
                        kblk = kT[:, kb * 128:(kb + 1) * 128]
                        m = kb - qb0
                        mm(Zb[:, :w], kblk, qs[:, :w])
                        act(e[:, :w], Zb[:, :w], AF.Exp)
                        act(sp[:, :w], e[:, :w], AF.Ln, bias=ONE1)
                        if CUT <= 1:
                            continue
                        if m >= 0:
                            tt(sp[:, :w], sp[:, :w], SBM[m][:, :w], ALU.mult)
                        if kb == 0:
                            ts(sp[:, :w], sp[:, :w], VALID1, ALU.mult)
                        mm(Tb[:, :w], kblk, nqs[:, :w], start=True, stop=False)
                        mm(Tb[:, :w], UI, sp[:, :w], start=False, stop=first)
                        if not first:
                            mm(Tb[:, :w], ONES, S[:, :w], start=False, stop=True)
                        act(a[:, :w], Tb[:, :w], AF.Exp, scale=-1.0)
                        if CUT <= 2:
                            continue
                        if m >= 0:
                            tt(a[:, :w], a[:, :w], SBM[m][:, :w], ALU.mult)
                        if kb == 0:
                            ts(a[:, :w], a[:, :w], VALID1, ALU.mult)
                        if first:
                            cp(S[:, :w], sp[:, :w], eng="pool")
                        else:
                            tt(S[:, :w], S[:, :w], sp[:, :w], ALU.add, eng="pool")
                        if CUT <= 3:
                            first = False
                            continue
                        mm(PS[4][:, :w], vtm3[:, kb, :], a[:, :w], start=first, stop=(kb == 0))
                        first = False
                    if CUT <= 4:
                        continue
                    cp(osb[:, :w], PS[4][:, :w])
                    act(sq[:, :w], PS[4][:, :w], AF.Square)
                    mm(PS[5][:, :w], ONES, sq[:, :w])
                    rstd_from_ssq(r, PS[5], 1.0 / 128, w)
                    act(sq[:, :w], gt[:, :w], AF.Silu)
                    stt(osb[:, :w], osb[:, :w], sm(l, 3), r[:, :w], ALU.mult, ALU.mult)
                    tt(osb[:, :w], osb[:, :w], sq[:, :w], ALU.mult)
                    if CUT <= 5:
                        continue
                    st_OL(5 * 128 + h * 128, q0, w, osb)

        def gla_phase(l):
            phase()
            S = alloc("S", 256)
            WG, BG = alloc("WG", 128), alloc("BG", 128)
            qT, kT, lr = alloc("qT", 512), alloc("kT", 512), alloc("lr", 512)
            rT = [alloc("r0", 512), alloc("r1", 512)]
            ktm, vtm = alloc("ktm", 512), alloc("vtm", 1024)
            ktm3 = ktm.re("p (b d) -> p b d", d=128)
            vtm3 = vtm.re("p (b d) -> p b d", d=256)
            e1, sp, bl, eBl = alloc("e1", 128), alloc("sp", 128), alloc("bl", 1), alloc("eBl", 1)
            E1, E2, E3 = alloc("E1", 128), alloc("E2", 128), alloc("E3", 128)
            qg, kg, kd, att = alloc("qg", 128), alloc("kg", 128), alloc("kd", 128), alloc("att", 128)
            oraw = [alloc("oraw0", 512), alloc("oraw1", 512)]
            sq, r, sg, o = alloc("sq", 512), alloc("r", 512), alloc("sg", 512), alloc("o", 512)
            memset(S, 0.0)
            memset(WG, 0.0)
            memset(BG, 0.0)
            ld(WG[0:16, :], glawg_in[:, l, :])
            ld(BG[0:1, :], glabg_in[:, l, :])
            fmrow = lambda ct, t0, w: UFM[ct * 128:(ct + 1) * 128, t0:t0 + w]
            for (t0, w) in tok_tiles(NT):
                nb = w // 128
                ld(qT[:, :w], fmrow(FM_GLA_Q, t0, w), "UFM")
                ld(kT[:, :w], fmrow(FM_GLA_K, t0, w), "UFM")
                ld(lr[:, :w], fmrow(FM_GLA_LR, t0, w), "UFM")
                ld(rT[0][:, :w], fmrow(FM_GLA_R0, t0, w), "UFM")
                ld(rT[1][:, :w], fmrow(FM_GLA_R1, t0, w), "UFM")
                ld(ktm3[:, :nb, :], UTM[TM_GLA_K * NT + t0:TM_GLA_K * NT + t0 + w, :].rearrange("(b p) d -> p b d", p=128), "UTM")
                for vc in range(2):
                    ld(vtm3[:, :nb, vc * 128:(vc + 1) * 128],
                       UTM[(TM_GLA_V0 + vc) * NT + t0:(TM_GLA_V0 + vc) * NT + t0 + w, :].rearrange("(b p) d -> p b d", p=128), "UTM")
                for bi in range(nb):
                    blk = slice(bi * 128, (bi + 1) * 128)
                    mm(PS[0][:, :128], lr[:, blk], WG, start=True, stop=False)
                    mm(PS[0][:, :128], ONES, BG, start=False, stop=True)
                    act(e1, PS[0][:, :128], AF.Exp, scale=-1.0)
                    act(sp, e1, AF.Ln, bias=ONE1)
                    mm(PS[1][:, :128], sp, TRI)
                    mm(PS[2][:, :128], SFX, sp)
                    cp(bl, PS[1][:, 127:128])
                    act(E1, PS[1][:, :128], AF.Exp, bias=LNS1)
                    act(E2, PS[1][:, :128], AF.Exp, scale=-1.0)
                    act(E3, PS[2][:, :128], AF.Exp)
                    act(eBl, bl, AF.Exp)
                    tt(qg, qT[:, blk], E1, ALU.mult)
                    tt(kg, kT[:, blk], E2, ALU.mult)
                    tt(kd, ktm3[:, bi, :], E3, ALU.mult)
                    mm(PS[3][:, :128], kg, qg)
                    tt(att, PS[3][:, :128], MUT, ALU.mult)
                    for vc in range(2):
                        mm(PS[4 + vc][:, :128], vtm3[:, bi, vc * 128:(vc + 1) * 128], att, start=True, stop=False)
                        mm(PS[4 + vc][:, :128], S[:, vc * 128:(vc + 1) * 128], qg, start=False, stop=True)
                        act(oraw[vc][:, blk], PS[4 + vc][:, :128], AF.Copy)
                    mm(PS[6][:, :256], kd, vtm3[:, bi, :])
                    stt(S, S, eBl, PS[6][:, :256], ALU.mult, ALU.add)
                    import os
                    if os.environ.get("GLARAW") == "2":
                        cp(oraw[0][:, blk], PS[1][:, :128])
                        tr(PS[3][:, :128], sp)
                        cp(oraw[1][:, blk], PS[3][:, :128])
                for vc in range(2):
                    act(sq[:, :w], oraw[vc][:, :w], AF.Square)
                    mm(PS[7][:, :w], ONES, sq[:, :w], start=(vc == 0), stop=(vc == 1))
                rstd_from_ssq(r, PS[7], 1.0 / 256, w)
                for vc in range(2):
                    act(sg[:, :w], rT[vc][:, :w], AF.Silu)
                    stt(o[:, :w], oraw[vc][:, :w], sm(l, vc), r[:, :w], ALU.mult, ALU.mult)
                    tt(o[:, :w], o[:, :w], sg[:, :w], ALU.mult)
                    import os
                    if os.environ.get("GLARAW"):
                        cp(o[:, :w], oraw[vc][:, :w])
                    st_OL(vc * 128, t0, w, o)

        def gdn_phase(l):
            phase()
            sc = 128.0 ** -0.5
            Sst = [alloc("S%d" % h, 128) for h in range(3)]
            ba = alloc("ba", 4 * 6)
            ba3 = ba.re("p (b c) -> p b c", c=6)
            DT4, EA4 = alloc("DT4", 12), alloc("EA4", 12)
            t1, gg, beta, G, GL, eG, eGlG, eGL, bG = [alloc("sc%d" % i, NB * 3) for i in range(9)]
            xin = [alloc("xin%d" % i, 515) for i in range(3)]
            acc = alloc("acc", 512)
            yq, yk, yv = alloc("yq", 512), alloc("yk", 512), alloc("yv", 512)
            sq, rn, zt = alloc("sq", 512), alloc("rn", 512), alloc("zt", 512)
            oraw, o = alloc("oraw", 512), alloc("o", 512)
            names = ["kbg", "Kd", "betaV", "gbc", "d1", "d2", "EGb", "M", "Aqk", "qg", "MT", "Xa", "Xb",
                     "Na", "Nb", "NTa", "NTb", "nWT", "Vnew"]
            B = {n: alloc(n, 128) for n in names}
            for h in range(3):
                memset(Sst[h], 0.0)
            for bi in range(4):
                for h in range(3):
                    cp(DT4[:, bi * 3 + h:bi * 3 + h + 1], sm(l, 7 + h))
                    act(EA4[:, bi * 3 + h:bi * 3 + h + 1], sm(l, 4 + h), AF.Exp)
            ts(EA4, EA4, -1.0, ALU.mult)
            for (t0, w) in tok_tiles(NT):
                nb = w // 128
                n3 = nb * 3
                c0 = (t0 // 128) * 3
                cs = slice(c0, c0 + n3)
                ld(ba3[:, :nb, :], UTM[TM_GDN_BA * NT + t0:TM_GDN_BA * NT + t0 + w, 0:6].rearrange("(b p) c -> p b c", p=128), "UTM")
                for bi in range(nb):
                    tt(t1[:, bi * 3:bi * 3 + 3], ba3[:, bi, 3:6], DT4[:, bi * 3:bi * 3 + 3], ALU.add)
                    act(beta[:, c0 + bi * 3:c0 + bi * 3 + 3], ba3[:, bi, 0:3], AF.Exp, scale=-1.0)
                act(t1[:, :n3], t1[:, :n3], AF.Exp)
                act(t1[:, :n3], t1[:, :n3], AF.Ln, bias=ONE1)
                tt(gg[:, cs], t1[:, :n3], EA4[:, :n3], ALU.mult)
                ts(beta[:, cs], beta[:, cs], 1.0, ALU.add)
                recip(beta[:, cs], beta[:, cs])
                for bi in range(nb):
                    c3 = slice(c0 + bi * 3, c0 + bi * 3 + 3)
                    mm(PS[6][:, bi * 3:bi * 3 + 3], TRI1, gg[:, c3])
                    mm(PS[6][:, 16 + bi * 3:16 + bi * 3 + 3], ONES, gg[:, c3])
                cp(G[:, cs], PS[6][:, 0:n3])
                cp(GL[:, cs], PS[6][:, 16:16 + n3])
                act(eG[:, cs], G[:, cs], AF.Exp)
                act(eGL[:, cs], GL[:, cs], AF.Exp)
                tt(eGlG[:, cs], GL[:, cs], G[:, cs], ALU.subtract)
                act(eGlG[:, cs], eGlG[:, cs], AF.Exp)
                tt(bG[:, cs], beta[:, cs], eG[:, cs], ALU.mult)
            for h in range(3):
                for (t0, w) in tok_tiles(NT):
                    nb = w // 128
                    ys = [yq, yk, yv]
                    for j3, (fm0, y) in enumerate(((FM_GDN_Q, yq), (FM_GDN_K, yk), (FM_GDN_V, yv))):
                        x = xin[j3]
                        ct = fm0 + h
                        if t0 == 0:
                            memset(x[:, 0:3], 0.0)
                            ld(x[:, 3:3 + w], UFM[ct * 128:(ct + 1) * 128, 0:w], "UFM")
                        else:
                            ld(x[:, 0:3 + w], UFM[ct * 128:(ct + 1) * 128, t0 - 3:t0 + w], "UFM")
                        cw = lambda i: sm(l, 10 + (j3 * 3 + h) * 4 + i)
                        ts(acc[:, :w], x[:, 3:3 + w], cw(3), ALU.mult)
                        for i in (2, 1, 0):
                            stt(acc[:, :w], x[:, i:i + w], cw(i), acc[:, :w], ALU.mult, ALU.add)
                        act(y[:, :w], acc[:, :w], AF.Silu)
                        if j3 < 2:
                            act(sq[:, :w], y[:, :w], AF.Square)
                            mm(PS[5][:, :w], ONES, sq[:, :w])
                            rstd_from_ssq(rn, PS[5], 1.0, w)
                            if j3 == 0:
                                stt(y[:, :w], y[:, :w], sc, rn[:, :w], ALU.mult, ALU.mult)
                            else:
                                tt(y[:, :w], y[:, :w], rn[:, :w], ALU.mult)
                    ld(zt[:, :w], UFM[(FM_GDN_Z + h) * 128:(FM_GDN_Z + h + 1) * 128, t0:t0 + w], "UFM")
                    S = Sst[h]
                    for bi in range(nb):
                        blk = slice(bi * 128, (bi + 1) * 128)
                        col = (t0 // 128 + bi) * 3 + h
                        c1 = lambda t: t[:, col:col + 1]
                        pa, pb, pc, pd, pe = [PS[i][:, :128] for i in range(5)]
                        tr(pa, yk[:, blk])
                        ts(B["kbg"], pa, c1(bG), ALU.mult)
                        ts(B["Kd"], pa, c1(eGlG), ALU.mult)
                        tr(pb, yv[:, blk])
                        ts(B["betaV"], pb, c1(beta), ALU.mult)
                        mm(pc, yk[:, blk], yk[:, blk])
                        mm(pd, yk[:, blk], yq[:, blk])
                        ts(B["gbc"], ONES, c1(gg), ALU.mult)
                        mm(pe, B["gbc"], TRI1)
                        ts(B["d1"], pe, c1(G), ALU.subtract, 0.0, ALU.max)
                        act(B["d1"], B["d1"], AF.Exp, scale=-1.0)
                        ts(B["d2"], pe, c1(G), ALU.subtract, 0.0, ALU.min)
                        act(B["d2"], B["d2"], AF.Exp)
                        act(B["EGb"], pe, AF.Exp)
                        tt(B["M"], pc, B["d1"], ALU.mult)
                        stt(B["M"], B["M"], c1(beta), MSL, ALU.mult, ALU.mult)
                        tt(B["Aqk"], pd, B["d2"], ALU.mult)
                        tt(B["Aqk"], B["Aqk"], MUT, ALU.mult)
                        tt(B["qg"], yq[:, blk], B["EGb"], ALU.mult)
                        tr(pa, B["M"])
                        act(B["MT"], pa, AF.Copy)
                        tt(B["Xa"], IDENT, pa, ALU.subtract)
                        mm(pb, B["MT"], B["M"])
                        mm(pc, B["M"], B["MT"])
                        cp(B["Na"], pb)
                        act(B["NTa"], pc, AF.Copy)
                        X, X2 = B["Xa"], B["Xb"]
                        N, N2, NTt, NT2 = B["Na"], B["Nb"], B["NTa"], B["NTb"]
                        for it in range(6):
                            mm(pd, N, X)
                            tt(X2, pd, X, ALU.add)
                            X, X2 = X2, X
                            if it < 5:
                                mm(pb, NTt, N)
                                mm(pc, N, NTt)
                                cp(N2, pb)
                                act(NT2, pc, AF.Copy)
                                N, N2, NTt, NT2 = N2, N, NT2, NTt
                        mm(pa, B["kbg"], X)
                        ts(B["nWT"], pa, -1.0, ALU.mult)
                        mm(pb, X, B["betaV"], start=True, stop=False)
                        mm(pb, B["nWT"], S, start=False, stop=True)
                        cp(B["Vnew"], pb)
                        mm(pc, S, B["qg"], start=True, stop=False)
                        mm(pc, B["Vnew"], B["Aqk"], start=False, stop=True)
                        act(oraw[:, blk], pc, AF.Copy)
                        mm(pd, B["Kd"], B["Vnew"])
                        stt(S, S, c1(eGL), pd, ALU.mult, ALU.add)
                    act(sq[:, :w], oraw[:, :w], AF.Square)
                    mm(PS[7][:, :w], ONES, sq[:, :w])
                    rstd_from_ssq(rn, PS[7], 1.0 / 128, w)
                    act(sq[:, :w], zt[:, :w], AF.Silu)
                    stt(o[:, :w], oraw[:, :w], sm(l, 2), rn[:, :w], ALU.mult, ALU.mult)
                    tt(o[:, :w], o[:, :w], sq[:, :w], ALU.mult)
                    st_OL(256 + h * 128, t0, w, o)

        YP = dram("YP", [4 * D, SL])
        YS = dram("YS", [D, SL])

        def outproj_phase(l):
            phase()
            ot = alloc("ot", 8 * 512)
            wt = [alloc("wt%d" % i, 8 * 128) for i in range(2)]
            ev = [alloc("ev%d" % i, 512) for i in range(2)]
            nw = 0
            for (t0, w) in tok_tiles(NT):
                ot3 = ot[:, :8 * w].re("p (k t) -> p k t", t=w)
                ld(ot3, OL[:, t0:t0 + w].rearrange("(k p) t -> p k t", p=128), "OL")
                for j in range(KC):
                    w3 = wt[nw % 2].re("p (k c) -> p k c", c=128)
                    for half in range(2):
                        T = l * KC + j
                        base = (T // 4) * 1024 + half * 512 + (T % 4) * 128
                        ld(w3[:, half * 4:(half + 1) * 4, :],
                           WOUT[base:base + 128, :].rearrange("p (k c) -> p k c", c=128), "WOUT#%d" % (T // 4))
                    e = ev[nw % 2]
                    nw += 1
                    ps = PS[j % 2]
                    for lc in range(8):
                        mm(ps[:, :w], w3[:, lc, :], ot3[:, lc, :], start=(lc == 0), stop=(lc == 7))
                    if nw % 2:
                        cp(e[:, :w], ps[:, :w])
                    else:
                        act(e[:, :w], ps[:, :w], AF.Copy)
                    t = t0
                    while t < t0 + w:
                        s = t // SL
                        te = min(t0 + w, (s + 1) * SL)
                        st(YP[j * 512 + s * 128:j * 512 + (s + 1) * 128, t - s * SL:te - s * SL], "YP#%d" % j, e[:, t - t0:te - t0])
                        t = te
            for j in range(KC):
                P.dma("pool", lambda e, j=j: e.collective_compute(
                    "ReduceScatter", ALU.add, replica_groups=quads, ins=[YP[j * 512:(j + 1) * 512, :].opt()],
                    outs=[YS[j * 128:(j + 1) * 128, :].opt()]),
                    reads=["YP#%d" % j], writes=["YS#%d" % j], chan="cc_rs_y")

        hsrc_ap, hsrc_key = hT_in, None
        for l in range(L + 1):
            if cfg.stop == 0:
                break
            last = (l == L)
            hd = HB[l % 2]
            norm_phase(l, hsrc_ap, hsrc_key, l > 0, hd, "HB%d" % (l % 2),
                       out_ext if last else XNS, "OUT" if last else "XNS")
            if l > 0:
                hsrc_ap, hsrc_key = hd, "HB%d" % (l % 2)
            if last or cfg.stop == 1:
                break
            for i in range(2 * KC):
                allgather(XN[i * 256:(i + 1) * 256, :], "XN#%d" % i, XNS[i * 64:(i + 1) * 64, :], "XNS#%d" % (i // 2),
                          quads, "cc_xn")
            inproj_phase(l)
            if cfg.debug and l == 0:
                d2d(dbg["ufm"], "dbgufm", UFM, "UFM", "c_dbg1")
                d2d(dbg["utm"], "dbgutm", UTM, "UTM", "c_dbg2")
            if cfg.stop == 2:
                break
            if "sb" in cfg.mixers:
                sb_phase(l)
            if "gla" in cfg.mixers:
                gla_phase(l)
            if "gdn" in cfg.mixers:
                gdn_phase(l)
            if cfg.debug and l == 0:
                d2d(dbg["ol"], "dbgol", OL, "OL", "c_dbg3")
            outproj_phase(l)
        P.barrier()
        P.emit()
    return nc


def prep_inputs(cfg, x, meta, norm_g, w_in, gla_w_gate, gla_b_gate, gla_norm_g, gdn_conv_w,
                gdn_a_log, gdn_dt_bias, gdn_norm_g, sb_norm_g, w_out, final_g):
    D, KC, NT, SL, L = cfg.D, cfg.KC, cfg.NT, cfg.SL, cfg.depth
    KH = KC // 2
    NSM = 64
    f32 = np.float32
    consts = make_consts()
    normg = np.zeros((128, L + 1, KC), f32)
    for l in range(L):
        normg[:, l, :] = np.asarray(norm_g[l], f32).reshape(KC, 128).T
    normg[:, L, :] = np.asarray(final_g, f32).reshape(KC, 128).T
    in_maps = []
    wg_cache = {}
    for c in range(8):
        b, g = c // 4, c % 4
        h0 = np.zeros((NT, D), f32)
        h0[112:128] = meta
        h0[128:] = x[b]
        hT = np.ascontiguousarray(h0[g * SL:(g + 1) * SL].T)
        if g not in wg_cache:
            tiles = col_tiles(g)
            cols = np.concatenate(tiles)
            wl = []
            for l in range(L):
                wsel = np.where(cols[None, :] >= 0, np.asarray(w_in[l])[:, np.maximum(cols, 0)], 0.0).astype(f32)
                w4 = wsel.reshape(KC, 128, NCT, 128).transpose(2, 1, 0, 3)
                wl.append(w4)
            wg_cache[g] = np.stack(wl)
        wfull = wg_cache[g]
        win = np.ascontiguousarray(wfull[:, :, :, b * KH:(b + 1) * KH, :]).reshape(L * NCT * 128, KH * 128)
        rows = np.concatenate([g * 256 + np.arange(256), 1024 + 3 * g * 128 + np.arange(384),
                               2560 + 3 * g * 128 + np.arange(384)])
        wo = []
        for l in range(L):
            wp = np.asarray(w_out[l], f32)[rows]
            w4 = wp.reshape(8, 128, KC, 128).transpose(2, 1, 0, 3)
            wo.append(w4[:, :, b * 4:(b + 1) * 4, :])
        wout = np.ascontiguousarray(np.stack(wo)).reshape(L * KC * 128, 4 * 128)
        small = np.zeros((128, L, NSM), f32)
        for l in range(L):
            small[:, l, 0] = gla_norm_g[l][0:128]
            small[:, l, 1] = gla_norm_g[l][128:256]
            small[:, l, 2] = gdn_norm_g[l]
            small[:, l, 3] = sb_norm_g[l]
            for h in range(3):
                small[:, l, 4 + h] = gdn_a_log[l][3 * g + h]
                small[:, l, 7 + h] = gdn_dt_bias[l][3 * g + h]
            for j3 in range(3):
                for h in range(3):
                    ch = j3 * 1536 + (3 * g + h) * 128 + np.arange(128)
                    for i in range(4):
                        small[:, l, 10 + (j3 * 3 + h) * 4 + i] = gdn_conv_w[l][i, ch]
        glawg = np.ascontiguousarray(np.asarray(gla_w_gate, f32)[:L, :, g * 128:(g + 1) * 128].transpose(1, 0, 2))
        glabg = np.ascontiguousarray(np.asarray(gla_b_gate, f32)[:L, g * 128:(g + 1) * 128][None])
        in_maps.append({"hT": hT, "consts": consts, "normg": normg, "win": win, "wout": wout,
                        "small": small, "glawg": glawg, "glabg": glabg})
    return in_maps


def run(cfg, **inputs):
    inputs = {k: np.asarray(v) for k, v in inputs.items()}
    in_maps = prep_inputs(cfg, **inputs)
    nc = build(cfg)
    res = run_bass_kernel_spmd(nc, in_maps, core_ids=list(range(8)))
    D, NT, SL = cfg.D, cfg.NT, cfg.SL
    out = np.zeros((2, NT, D), np.float32)
    for c in range(8):
        b, g = c // 4, c % 4
        out[b, g * SL:(g + 1) * SL, :] = res.results[c]["outT"].T
    return np.ascontiguousarray(out[:, 128:, :]), res


def kernel(**inputs):
    out, _ = run(Cfg(), **inputs)
    return out
```

```python
import numpy as np
import concourse.bass as bass
import concourse.mybir as mybir
from concourse.bass_utils import run_bass_kernel_spmd
from contextlib import ExitStack

F32 = mybir.dt.float32
AF = mybir.ActivationFunctionType
ALU = mybir.AluOpType

EPOCH = 20000
DEPOCH = 1000
SAME_ENGINE_SYNC = True
EPS = 1e-6


class V:
    def __init__(self, ap, key):
        self.ap = ap
        self.key = key

    def __getitem__(self, idx):
        return V(self.ap[idx], self.key)

    def re(self, pat, **kw):
        return V(self.ap.rearrange(pat, **kw), self.key)


class Prog:
    ENGS = ("pe", "act", "dve", "pool", "sp")

    def __init__(self, nc, es):
        self.nc = nc
        self.es = es
        self.ops = []
        self.cnt = {e: 0 for e in self.ENGS}
        self.last_w = {}
        self.readers = {}
        self.chan_cnt = {}
        self.sems = {}

    def _deps(self, reads, writes):
        deps = []
        for k in reads:
            t = self.last_w.get(k)
            if t is not None:
                deps.append((t, True))
        for k in writes:
            t = self.last_w.get(k)
            if t is not None:
                deps.append((t, False))
            deps.extend((r, False) for r in self.readers.get(k, []))
        return deps

    def _commit(self, tok, reads, writes):
        for k in reads:
            self.readers.setdefault(k, []).append(tok)
        for k in writes:
            self.last_w[k] = tok
            self.readers[k] = []

    def op(self, eng, fn, reads=(), writes=()):
        reads = [k for k in reads if k is not None]
        writes = list(writes) + [k for k in reads if k.startswith("ps") and k not in writes]
        deps = self._deps(reads, writes)
        self.cnt[eng] += 1
        tok = ("c", eng, self.cnt[eng])
        self.ops.append((eng, fn, deps, tok))
        self._commit(tok, reads, writes)
        return tok

    def dma(self, eng, fn, reads=(), writes=(), chan=None):
        deps = self._deps(reads, writes)
        c = self.chan_cnt.get(chan, 0) + 1
        self.chan_cnt[chan] = c
        tok = ("d", chan, c)
        self.ops.append((eng, fn, deps, tok))
        self._commit(tok, reads, writes)
        return tok

    def barrier(self):
        deps = [(("c", e, n), True) for e, n in self.cnt.items() if n > 0]
        deps += [(("d", ch, n), True) for ch, n in self.chan_cnt.items()]
        for e in self.ENGS:
            self.ops.append((e, None, list(deps), None))
        self.last_w = {}
        self.readers = {}

    def _tok_sem(self, tok):
        kind, who, n = tok
        if kind == "c":
            return ("c_%s_%d" % (who, (n - 1) // EPOCH), (n - 1) % EPOCH + 1)
        return ("d_%s_%d" % (who, (n - 1) // DEPOCH), (1 if who.startswith("cc") else 16) * ((n - 1) % DEPOCH + 1))

    def emit(self):
        nc = self.nc
        for eng, fn, deps, tok in self.ops:
            if tok is not None:
                s = self._tok_sem(tok)[0]
                if s not in self.sems:
                    self.sems[s] = self.es.enter_context(nc.semaphore(s))
        EI = {e: i for i, e in enumerate(self.ENGS)}
        know_c = {e: [0] * len(self.ENGS) for e in self.ENGS}
        know_d = {e: {} for e in self.ENGS}
        clock = {}
        plan = []
        for eng, fn, deps, tok in self.ops:
            kc, kd = know_c[eng], know_d[eng]
            need = []
            for d, is_raw in deps:
                if d[0] == "c":
                    if d[1] == eng:
                        if eng == "pe" or not SAME_ENGINE_SYNC or not is_raw:
                            continue
                        need.append(d)
                        continue
                    if kc[EI[d[1]]] >= d[2]:
                        continue
                else:
                    if kd.get(d[1], 0) >= d[2]:
                        continue
                need.append(d)
            best = {}
            for d in need:
                key = (d[0], d[1])
                if key not in best or best[key][2] < d[2]:
                    best[key] = d
            waits = []
            newd = None
            for d in best.values():
                waits.append(self._tok_sem(d))
                if d[0] == "c" and d[1] == eng:
                    continue
                ck = clock.get(d)
                if d[0] == "c":
                    if kc[EI[d[1]]] < d[2]:
                        kc[EI[d[1]]] = d[2]
                else:
                    if kd.get(d[1], 0) < d[2]:
                        if newd is None:
                            newd = dict(kd)
                        newd[d[1]] = d[2]
                if ck is not None:
                    cc, cd = ck
                    for i in range(len(cc)):
                        if kc[i] < cc[i]:
                            kc[i] = cc[i]
                    for ch, n in cd.items():
                        if (newd if newd is not None else kd).get(ch, 0) < n:
                            if newd is None:
                                newd = dict(kd)
                            newd[ch] = n
            if newd is not None:
                know_d[eng] = newd
            plan.append(waits)
            if tok is not None:
                clock[tok] = (tuple(kc), know_d[eng])
        per_eng = {e: [] for e in self.ENGS}
        for o, waits in zip(self.ops, plan):
            per_eng[o[0]].append((o, waits))
        with nc.Block() as block:
            def make(e):
                def body(engh):
                    for (eng, fn, deps, tok), waits in per_eng[e]:
                        for s, v in waits:
                            engh.wait_ge(self.sems[s], v)
                        if fn is None:
                            continue
                        ins = fn(engh)
                        s, v = self._tok_sem(tok)
                        ins.then_inc(self.sems[s], 16 if (tok[0] == "d" and not tok[1].startswith("cc")) else 1)
                return body
            for e, deco in (("pe", block.tensor), ("act", block.scalar), ("dve", block.vector),
                            ("pool", block.gpsimd), ("sp", block.sync)):
                if per_eng[e]:
                    deco(make(e))


class Cfg:
    def __init__(self, D=4096, NB=65, depth=2, debug=False, mixers=("sb", "gla", "gdn")):
        self.mixers = mixers
        self.stop = 99
        self.D = D
        self.KC = D // 128
        self.NB = NB
        self.NT = NB * 128
        assert self.NT % 4 == 0
        self.SL = self.NT // 4
        self.depth = depth
        self.debug = debug
        self.DMIX = 4096
        self.CC = 32


FM_GLA_Q, FM_GLA_K, FM_GLA_R0, FM_GLA_R1, FM_GLA_LR = 0, 1, 2, 3, 4
FM_GDN_Q, FM_GDN_K, FM_GDN_V, FM_GDN_Z = 5, 8, 11, 14
FM_SB_Q, FM_SB_K, FM_SB_G = 17, 20, 23
NFM = 26
TM_GLA_K, TM_GLA_V0, TM_GLA_V1, TM_GDN_BA, TM_SB_V = 0, 1, 2, 3, 4
NTM = 7
NCT = NFM + NTM

_OFF = {}
_o = 0
for _n, _w in (("gq", 512), ("gk", 512), ("gv", 1024), ("gr", 1024), ("glr", 16),
               ("dq", 1536), ("dk", 1536), ("dv", 1536), ("dz", 1536), ("db", 12), ("da", 12),
               ("sq", 1536), ("sk", 1536), ("sv", 1536), ("sg", 1536)):
    _OFF[_n] = _o
    _o += _w
D_IN = _o


def col_tiles(g):
    def rng(name, start, n):
        a = np.full(128, -1, np.int64)
        a[:n] = _OFF[name] + start + np.arange(n)
        return a
    fm = [None] * NFM
    fm[FM_GLA_Q] = rng("gq", g * 128, 128)
    fm[FM_GLA_K] = rng("gk", g * 128, 128)
    fm[FM_GLA_R0] = rng("gr", g * 256, 128)
    fm[FM_GLA_R1] = rng("gr", g * 256 + 128, 128)
    fm[FM_GLA_LR] = rng("glr", 0, 16)
    for h in range(3):
        hh = 3 * g + h
        fm[FM_GDN_Q + h] = rng("dq", hh * 128, 128)
        fm[FM_GDN_K + h] = rng("dk", hh * 128, 128)
        fm[FM_GDN_V + h] = rng("dv", hh * 128, 128)
        fm[FM_GDN_Z + h] = rng("dz", hh * 128, 128)
        fm[FM_SB_Q + h] = rng("sq", hh * 128, 128)
        fm[FM_SB_K + h] = rng("sk", hh * 128, 128)
        fm[FM_SB_G + h] = rng("sg", hh * 128, 128)
    tm = [None] * NTM
    tm[TM_GLA_K] = rng("gk", g * 128, 128)
    tm[TM_GLA_V0] = rng("gv", g * 256, 128)
    tm[TM_GLA_V1] = rng("gv", g * 256 + 128, 128)
    ba = np.full(128, -1, np.int64)
    ba[0:3] = _OFF["db"] + 3 * g + np.arange(3)
    ba[3:6] = _OFF["da"] + 3 * g + np.arange(3)
    tm[TM_GDN_BA] = ba
    for h in range(3):
        tm[TM_SB_V + h] = rng("sv", (3 * g + h) * 128, 128)
    return fm + tm


C_ID, C_ONES, C_UI, C_TRI, C_SFX, C_MUT, C_MSL, C_TRI1 = 0, 128, 256, 384, 512, 640, 768, 896
C_SBM = 1024
C_VALID = C_SBM + 4 * 512
C_ONE = C_VALID + 1
C_EPS = C_ONE + 1
C_LNS = C_EPS + 1
NCONST = C_LNS + 1


def make_consts():
    c = np.zeros((128, NCONST), np.float32)
    j = np.arange(128)[:, None]
    f = np.arange(128)[None, :]
    c[:, C_ID:C_ID + 128] = (j == f)
    c[:, C_ONES:C_ONES + 128] = 1.0
    c[:, C_UI:C_UI + 128] = (j >= f)
    c[:, C_TRI:C_TRI + 128] = (j <= f) * (-1.0 / 16.0)
    c[:, C_SFX:C_SFX + 128] = (j > f) * (-1.0 / 16.0)
    c[:, C_MUT:C_MUT + 128] = (j <= f)
    c[:, C_MSL:C_MSL + 128] = (f < j)
    c[:, C_TRI1:C_TRI1 + 128] = (j <= f)
    t = np.arange(512)[None, :]
    for m in range(4):
        c[:, C_SBM + m * 512:C_SBM + (m + 1) * 512] = ((m * 128 + j) < t)
    c[:, C_VALID] = (np.arange(128) >= 112)
    c[:, C_ONE] = 1.0
    c[:, C_EPS] = EPS
    c[:, C_LNS] = np.log(128.0 ** -0.5)
    return c


ARENA = 40000


def build(cfg):
    nc = bass.Bass("TRN2", target_bir_lowering=False)
    D, KC, NB, NT, SL, L = cfg.D, cfg.KC, cfg.NB, cfg.NT, cfg.SL, cfg.depth
    KH = KC // 2
    dram_in = lambda name, shape: nc.dram_tensor(name, shape, F32, kind="ExternalInput").ap()
    dram_out = lambda name, shape: nc.dram_tensor(name, shape, F32, kind="ExternalOutput").ap()
    dram = lambda name, shape: nc.dram_tensor(name, shape, F32).ap()

    hT_in = dram_in("hT", [D, SL])
    consts_in = dram_in("consts", [128, NCONST])
    normg_in = dram_in("normg", [128, L + 1, KC])
    win_in = dram_in("win", [L * NCT * 128, KH * 128])
    wout_in = dram_in("wout", [L * KC * 128, 4 * 128])
    NSM = 64
    small_in = dram_in("small", [128, L, NSM])
    glawg_in = dram_in("glawg", [16, L, 128])
    glabg_in = dram_in("glabg", [1, L, 128])
    out_ext = dram_out("outT", [D, SL])

    win_b = dram("win_b", [L * NCT * 128, KH * 128])
    WIN = dram("WIN", [2 * L * NCT * 128, KH * 128])
    wout_b = dram("wout_b", [L * KC * 128, 4 * 128])
    WOUT = dram("WOUT", [2 * L * KC * 128, 4 * 128])
    XNS = dram("XNS", [D, SL])
    XN = dram("XN", [4 * D, SL])
    UFM = dram("UFM", [NFM * 128, NT])
    UTM = dram("UTM", [NTM * NT, 128])
    OL = dram("OL", [1024, NT])
    HB = [dram("HB%d" % i, [D, SL]) for i in range(2)]
    dbg = {}
    if cfg.debug:
        dbg["ufm"] = dram_out("dbg_ufm", [NFM * 128, NT])
        dbg["utm"] = dram_out("dbg_utm", [NTM * NT, 128])
        dbg["ol"] = dram_out("dbg_ol", [1024, NT])

    with ExitStack() as es:
        P = Prog(nc, es)
        arena_t = es.enter_context(nc.sbuf_tensor("arena", [128, ARENA], F32))
        const_t = es.enter_context(nc.sbuf_tensor("constt", [128, NCONST], F32))
        ng_t = es.enter_context(nc.sbuf_tensor("ngt", [128, (L + 1) * KC], F32))
        small_t = es.enter_context(nc.sbuf_tensor("smallt", [128, L * NSM], F32))
        PS = [V(es.enter_context(nc.psum_tensor("ps%d" % i, [128, 512], F32))[:], "ps%d" % i)
              for i in range(8)]
        CONST = V(const_t[:], "const")
        NG = V(ng_t[:], "ng")
        SMALL = V(small_t[:], "small")
        state = {"off": 0, "phase": 0}

        def phase():
            P.barrier()
            state["off"] = 0
            state["phase"] += 1

        def alloc(name, n):
            off = state["off"]
            assert off + n <= ARENA, (name, off, n)
            state["off"] = off + n
            return V(arena_t[:, off:off + n], "%s@%d" % (name, state["phase"]))

        cst = lambda c0, n=128: CONST[:, c0:c0 + n]
        IDENT, ONES = cst(C_ID), cst(C_ONES)
        ONE1, EPS1, LNS1 = cst(C_ONE, 1), cst(C_EPS, 1), cst(C_LNS, 1)

        def ld(out, in_ap, rkey=None, eng="sp"):
            rk = [] if rkey is None else (list(rkey) if isinstance(rkey, (list, tuple)) else [rkey])
            P.dma(eng, lambda e: e.dma_start(out=out.ap, in_=in_ap), reads=rk,
                  writes=[out.key], chan="L" + out.key)

        def st(out_ap, wkey, in_, eng="pool"):
            P.dma(eng, lambda e: e.dma_start(out=out_ap, in_=in_.ap), reads=[in_.key],
                  writes=[wkey] if wkey else [], chan="S" + in_.key)

        def d2d(out_ap, wkey, in_ap, rkey, chan, eng="sp"):
            nrows = out_ap.shape[0]
            for r0 in range(0, nrows, 1024):
                r1 = min(nrows, r0 + 1024)
                P.dma(eng, lambda e, r0=r0, r1=r1: e.dma_start(out=out_ap[r0:r1, :], in_=in_ap[r0:r1, :]),
                      reads=[rkey] if rkey else [], writes=[wkey], chan=chan)

        def allgather(out_ap, wkey, in_ap, rkey, groups, chan):
            P.dma("pool", lambda e: e.collective_compute(
                "AllGather", ALU.bypass, replica_groups=groups, ins=[in_ap.opt()], outs=[out_ap.opt()]),
                reads=[rkey], writes=[wkey], chan=chan)

        def mm(out, lhsT, rhs, start=True, stop=True):
            P.op("pe", lambda e: e.matmul(out.ap, lhsT=lhsT.ap, rhs=rhs.ap, start=start, stop=stop),
                 reads=[lhsT.key, rhs.key], writes=[out.key])

        def tr(out, in_):
            P.op("pe", lambda e: e.transpose(out.ap, in_.ap, IDENT.ap),
                 reads=[in_.key, "const"], writes=[out.key])

        def act(out, in_, func, bias=None, scale=None, eng="act"):
            kw = {}
            rk = [in_.key]
            if bias is not None:
                if isinstance(bias, V):
                    kw["bias"] = bias.ap
                    rk.append(bias.key)
                else:
                    kw["bias"] = bias
            if scale is not None:
                if isinstance(scale, V):
                    kw["scale"] = scale.ap
                    rk.append(scale.key)
                else:
                    kw["scale"] = scale
            P.op("act", lambda e: e.activation(out=out.ap, in_=in_.ap, func=func, **kw),
                 reads=rk, writes=[out.key])

        def tt(out, a, b, op, eng="dve"):
            P.op(eng, lambda e: e.tensor_tensor(out=out.ap, in0=a.ap, in1=b.ap, op=op),
                 reads=[a.key, b.key], writes=[out.key])

        def ts(out, a, s1, op0, s2=None, op1=None, eng="dve"):
            rk = [a.key]
            v1 = s1
            if isinstance(s1, V):
                v1 = s1.ap
                rk.append(s1.key)
            v2 = s2
            if isinstance(s2, V):
                v2 = s2.ap
                rk.append(s2.key)
            if op1 is None:
                P.op(eng, lambda e: e.tensor_scalar(out=out.ap, in0=a.ap, scalar1=v1, scalar2=None, op0=op0),
                     reads=rk, writes=[out.key])
            else:
                P.op(eng, lambda e: e.tensor_scalar(out=out.ap, in0=a.ap, scalar1=v1, scalar2=v2,
                                                    op0=op0, op1=op1), reads=rk, writes=[out.key])

        def stt(out, a, s, b, op0, op1):
            rk = [a.key, b.key]
            sv = s
            if isinstance(s, V):
                sv = s.ap
                rk.append(s.key)
            P.op("dve", lambda e: e.scalar_tensor_tensor(out=out.ap, in0=a.ap, scalar=sv, in1=b.ap,
                                                         op0=op0, op1=op1), reads=rk, writes=[out.key])

        def cp(out, in_, eng="dve"):
            P.op(eng, lambda e: e.tensor_copy(out=out.ap, in_=in_.ap), reads=[in_.key], writes=[out.key])

        def memset(out, val, eng="dve"):
            P.op(eng, lambda e: e.memset(out.ap, val), writes=[out.key])

        def recip(out, in_):
            P.op("dve", lambda e: e.reciprocal(out=out.ap, in_=in_.ap), reads=[in_.key], writes=[out.key])

        def rstd_from_ssq(dst, ssq_ps, inv_n, w):
            act(dst[:, :w], ssq_ps[:, :w], AF.Sqrt, bias=EPS1, scale=inv_n)
            recip(dst[:, :w], dst[:, :w])

        ld(CONST, consts_in)
        ld(NG, normg_in.rearrange("p l k -> p (l k)"))
        ld(SMALL, small_in.rearrange("p l k -> p (l k)"))
        d2d(win_b, "win_b", win_in, None, "c_win")
        d2d(wout_b, "wout_b", wout_in, None, "c_wout")
        pairs = [[0, 4], [1, 5], [2, 6], [3, 7]]
        quads = [[0, 1, 2, 3], [4, 5, 6, 7]]
        for T in range(L * NCT):
            allgather(WIN[T * 256:(T + 1) * 256, :], "WIN#%d" % T, win_b[T * 128:(T + 1) * 128, :], "win_b", pairs, "cc_win")
        for q in range(L * KC // 4):
            allgather(WOUT[q * 1024:(q + 1) * 1024, :], "WOUT#%d" % q, wout_b[q * 512:(q + 1) * 512, :], "wout_b", pairs, "cc_wout")

        def tok_tiles(n, w=512):
            return [(t0, min(w, n - t0)) for t0 in range(0, n, w)]

        def norm_phase(l, src_ap, src_key, add_y, hdst_ap, hdst_key, dst_ap, dst_key):
            phase()
            W_ = 256
            h = alloc("h", KC * W_)
            y = alloc("y", KC * W_)
            sq = [alloc("sq%d" % i, W_) for i in range(2)]
            r = alloc("r", W_)
            xo = [alloc("xo%d" % i, W_) for i in range(2)]
            for (t0, w) in tok_tiles(SL, W_):
                h3 = h[:, :KC * w].re("p (k t) -> p k t", t=w)
                ld_chunks(h3, src_ap[:, t0:t0 + w].rearrange("(k p) t -> p k t", p=128), src_key, KC, 8)
                if add_y:
                    y3 = y[:, :KC * w].re("p (k t) -> p k t", t=w)
                    ysrc = YS[:, t0:t0 + w].rearrange("(k p) t -> p k t", p=128)
                    for k0 in range(0, KC, 8):
                        k1 = min(KC, k0 + 8)
                        ld(y3[:, k0:k1, :], ysrc[:, k0:k1, :], ["YS#%d" % k for k in range(k0, k1)])
                    tt(h[:, :KC * w], h[:, :KC * w], y[:, :KC * w], ALU.add)
                    for k0 in range(0, KC, 8):
                        k1 = min(KC, k0 + 8)
                        st(hdst_ap[k0 * 128:k1 * 128, t0:t0 + w].rearrange("(k p) t -> p k t", p=128), hdst_key,
                           h3[:, k0:k1, :])
                for kc in range(KC):
                    s = sq[kc % 2]
                    act(s[:, :w], h3[:, kc, :], AF.Square)
                    mm(PS[0][:, :w], ONES, s[:, :w], start=(kc == 0), stop=(kc == KC - 1))
                rstd_from_ssq(r, PS[0], 1.0 / D, w)
                for kc in range(KC):
                    x = xo[kc % 2]
                    stt(x[:, :w], h3[:, kc, :], NG[:, l * KC + kc:l * KC + kc + 1], r[:, :w], ALU.mult, ALU.mult)
                    st(dst_ap[kc * 128:(kc + 1) * 128, t0:t0 + w], "%s#%d" % (dst_key, kc), x[:, :w])

        def inproj_phase(l):
            phase()
            xn = alloc("xn", KC * 512)
            wt = [alloc("wt%d" % i, KC * 128) for i in range(2)]
            ev = [alloc("ev%d" % i, 512) for i in range(2)]
            ev2 = [alloc("ev2%d" % i, 512) for i in range(2)]
            nw = 0
            ne = 0
            for (t0, w) in tok_tiles(NT):
                xn3 = xn[:, :KC * w].re("p (k t) -> p k t", t=w)
                t = t0
                while t < t0 + w:
                    rk = t // SL
                    te = min(t0 + w, (rk + 1) * SL)
                    for ph in range(2):
                        srcv = XN.rearrange("(k ph r pl) t -> ph r pl k t", ph=2, r=4, pl=64)[ph, rk]
                        for k0 in range(0, KC, 8):
                            k1 = min(KC, k0 + 8)
                            ld(xn3[ph * 64:(ph + 1) * 64, k0:k1, t - t0:te - t0],
                               srcv[:, k0:k1, t - rk * SL:te - rk * SL],
                               ["XN#%d" % (2 * k + ph) for k in range(k0, k1)])
                    t = te
                for ct in range(NCT):
                    wtile = wt[nw % 2]
                    nw += 1
                    w3 = wtile.re("p (k c) -> p k c", c=128)
                    for half in range(2):
                        T = l * NCT + ct
                        base = (T * 2 + half) * 128
                        ld(w3[:, half * KH:(half + 1) * KH, :],
                           WIN[base:base + 128, :].rearrange("p (k c) -> p k c", c=128), "WIN#%d" % T)
                    if ct < NFM:
                        ps = PS[ct % 2]
                        for kc in range(KC):
                            mm(ps[:, :w], w3[:, kc, :], xn3[:, kc, :], start=(kc == 0), stop=(kc == KC - 1))
                        e = ev[ne % 2]
                        ne += 1
                        if ne % 2:
                            cp(e[:, :w], ps[:, :w], eng="dve")
                        else:
                            act(e[:, :w], ps[:, :w], AF.Copy)
                        st(UFM[ct * 128:(ct + 1) * 128, t0:t0 + w], "UFM", e[:, :w])
                    else:
                        tmi = ct - NFM
                        nblk = w // 128
                        ps = PS[ct % 2]
                        for kc in range(KC):
                            mm(ps[:, :w], w3[:, kc, :], xn3[:, kc, :], start=(kc == 0), stop=(kc == KC - 1))
                        e = ev[ne % 2]
                        ne += 1
                        act(e[:, :w], ps[:, :w], AF.Copy)
                        ps2 = PS[2 + (tmi % 2)]
                        for bi in range(nblk):
                            tr(ps2[:, bi * 128:(bi + 1) * 128], e[:, bi * 128:(bi + 1) * 128])
                        e2 = ev2[tmi % 2]
                        cp(e2[:, :w], ps2[:, :w], eng="dve")
                        st(UTM[tmi * NT + t0:tmi * NT + t0 + w, :].rearrange("(b p) c -> p b c", p=128),
                           "UTM", e2[:, :w].re("p (b c) -> p b c", c=128))

        def ld_chunks(dst3, src3, key, n, step):
            for a in range(0, n, step):
                b = min(n, a + step)
                ld(dst3[:, a:b, :], src3[:, a:b, :], key)

        def st_OL(c0, t0, w, tile):
            st(OL[c0:c0 + 128, t0:t0 + w], "OL", tile[:, :w])

        SBM = [CONST[:, C_SBM + m * 512:C_SBM + (m + 1) * 512] for m in range(4)]
        VALID1 = cst(C_VALID, 1)
        UI, TRI, SFX, MUT, MSL, TRI1 = cst(C_UI), cst(C_TRI), cst(C_SFX), cst(C_MUT), cst(C_MSL), cst(C_TRI1)
        NSM_ = NSM

        def sm(l, idx):
            return SMALL[:, l * NSM_ + idx:l * NSM_ + idx + 1]

        def sb_phase(l):
            import os
            CUT = int(os.environ.get("CUT", "99"))
            phase()
            sc = 128.0 ** -0.5
            kT = alloc("kT", NT)
            vtm = alloc("vtm", NB * 128)
            vtm3 = vtm.re("p (b d) -> p b d", d=128)
            qs, nqs, S = alloc("qs", 512), alloc("nqs", 512), alloc("S", 512)
            E = [alloc("E%d" % i, 512) for i in range(2)]
            SP = [alloc("SP%d" % i, 512) for i in range(2)]
            A = [alloc("A%d" % i, 512) for i in range(2)]
            osb, sq, r, gt = alloc("osb", 512), alloc("sq", 512), alloc("r", 512), alloc("gt", 512)
            for h in range(3):
                ld(kT, UFM[(FM_SB_K + h) * 128:(FM_SB_K + h + 1) * 128, :], "UFM")
                ld_chunks(vtm3, UTM[(TM_SB_V + h) * NT:(TM_SB_V + h + 1) * NT, :].rearrange("(b p) d -> p b d", p=128),
                          "UTM", NB, 4)
                for (q0, w) in tok_tiles(NT):
                    qb0, nqb = q0 // 128, w // 128
                    ld(qs[:, :w], UFM[(FM_SB_Q + h) * 128:(FM_SB_Q + h + 1) * 128, q0:q0 + w], "UFM")
                    ld(gt[:, :w], UFM[(FM_SB_G + h) * 128:(FM_SB_G + h + 1) * 128, q0:q0 + w], "UFM")
                    ts(nqs[:, :w], qs[:, :w], -sc, ALU.mult)
                    ts(qs[:, :w], qs[:, :w], sc, ALU.mult)
                    kbs = list(range(qb0 + nqb - 1, -1, -1))
                    nu = len(kbs)

                    def stageA(i):
                        kb = kbs[i]
                        Zb, e, sp = PS[i % 2], E[i % 2], SP[i % 2]
                        kblk = kT[:, kb * 128:(kb + 1) * 128]
                        m = kb - qb0
                        mm(Zb[:, :w], kblk, qs[:, :w])
                        act(e[:, :w], Zb[:, :w], AF.Exp)
                        act(sp[:, :w], e[:, :w], AF.Ln, bias=ONE1)
                        if m >= 0:
                            tt(sp[:, :w], sp[:, :w], SBM[m][:, :w], ALU.mult)
                        if kb == 0:
                            ts(sp[:, :w], sp[:, :w], VALID1, ALU.mult)

                    def stageB(i):
                        kb = kbs[i]
                        Tb, sp, a, e = PS[2 + i % 2], SP[i % 2], A[i % 2], E[i % 2]
                        m = kb - qb0
                        first = (i == 0)
                        mm(Tb[:, :w], UI, sp[:, :w], start=True, stop=first)
                        if not first:
                            mm(Tb[:, :w], ONES, S[:, :w], start=False, stop=True)
                        act(a[:, :w], Tb[:, :w], AF.Exp, scale=-1.0)
                        tt(a[:, :w], a[:, :w], e[:, :w], ALU.mult)
                        if m >= 0:
                            tt(a[:, :w], a[:, :w], SBM[m][:, :w], ALU.mult)
                        if kb == 0:
                            ts(a[:, :w], a[:, :w], VALID1, ALU.mult)
                        if first:
                            cp(S[:, :w], sp[:, :w], eng="pool")
                        elif i < nu - 1:
                            tt(S[:, :w], S[:, :w], sp[:, :w], ALU.add, eng="pool")

                    def stageC(i):
                        kb = kbs[i]
                        mm(PS[4][:, :w], vtm3[:, kb, :], A[i % 2][:, :w], start=(i == 0), stop=(i == nu - 1))

                    stageA(0)
                    for i in range(nu):
                        if i + 1 < nu:
                            stageA(i + 1)
                        stageB(i)
                        if i >= 1:
                            stageC(i - 1)
                    stageC(nu - 1)
                    if CUT <= 4:
                        continue
                    cp(osb[:, :w], PS[4][:, :w])
                    act(sq[:, :w], PS[4][:, :w], AF.Square)
                    mm(PS[5][:, :w], ONES, sq[:, :w])
                    rstd_from_ssq(r, PS[5], 1.0 / 128, w)
                    act(sq[:, :w], gt[:, :w], AF.Silu)
                    stt(osb[:, :w], osb[:, :w], sm(l, 3), r[:, :w], ALU.mult, ALU.mult)
                    tt(osb[:, :w], osb[:, :w], sq[:, :w], ALU.mult)
                    if CUT <= 5:
                        continue
                    st_OL(5 * 128 + h * 128, q0, w, osb)

        def gla_phase(l):
            phase()
            S = alloc("S", 256)
            WG, BG = alloc("WG", 128), alloc("BG", 128)
            qT, kT, lr = alloc("qT", 512), alloc("kT", 512), alloc("lr", 512)
            rT = [alloc("r0", 512), alloc("r1", 512)]
            ktm, vtm = alloc("ktm", 512), alloc("vtm", 1024)
            ktm3 = ktm.re("p (b d) -> p b d", d=128)
            vtm3 = vtm.re("p (b d) -> p b d", d=256)
            e1, sp, bl, eBl = alloc("e1", 128), alloc("sp", 128), alloc("bl", 1), alloc("eBl", 1)
            E1, E2, E3 = alloc("E1", 128), alloc("E2", 128), alloc("E3", 128)
            qg, kg, kd, att = alloc("qg", 128), alloc("kg", 128), alloc("kd", 128), alloc("att", 128)
            oraw = [alloc("oraw0", 512), alloc("oraw1", 512)]
            sq, r, sg, o = alloc("sq", 512), alloc("r", 512), alloc("sg", 512), alloc("o", 512)
            memset(S, 0.0)
            memset(WG, 0.0)
            memset(BG, 0.0)
            ld(WG[0:16, :], glawg_in[:, l, :])
            ld(BG[0:1, :], glabg_in[:, l, :])
            fmrow = lambda ct, t0, w: UFM[ct * 128:(ct + 1) * 128, t0:t0 + w]
            for (t0, w) in tok_tiles(NT):
                nb = w // 128
                ld(qT[:, :w], fmrow(FM_GLA_Q, t0, w), "UFM")
                ld(kT[:, :w], fmrow(FM_GLA_K, t0, w), "UFM")
                ld(lr[:, :w], fmrow(FM_GLA_LR, t0, w), "UFM")
                ld(rT[0][:, :w], fmrow(FM_GLA_R0, t0, w), "UFM")
                ld(rT[1][:, :w], fmrow(FM_GLA_R1, t0, w), "UFM")
                ld(ktm3[:, :nb, :], UTM[TM_GLA_K * NT + t0:TM_GLA_K * NT + t0 + w, :].rearrange("(b p) d -> p b d", p=128), "UTM")
                for vc in range(2):
                    ld(vtm3[:, :nb, vc * 128:(vc + 1) * 128],
                       UTM[(TM_GLA_V0 + vc) * NT + t0:(TM_GLA_V0 + vc) * NT + t0 + w, :].rearrange("(b p) d -> p b d", p=128), "UTM")
                for bi in range(nb):
                    blk = slice(bi * 128, (bi + 1) * 128)
                    mm(PS[0][:, :128], lr[:, blk], WG, start=True, stop=False)
                    mm(PS[0][:, :128], ONES, BG, start=False, stop=True)
                    act(e1, PS[0][:, :128], AF.Exp, scale=-1.0)
                    act(sp, e1, AF.Ln, bias=ONE1)
                    mm(PS[1][:, :128], sp, TRI)
                    mm(PS[2][:, :128], SFX, sp)
                    cp(bl, PS[1][:, 127:128])
                    act(E1, PS[1][:, :128], AF.Exp, bias=LNS1)
                    act(E2, PS[1][:, :128], AF.Exp, scale=-1.0)
                    act(E3, PS[2][:, :128], AF.Exp)
                    act(eBl, bl, AF.Exp)
                    tt(qg, qT[:, blk], E1, ALU.mult)
                    tt(kg, kT[:, blk], E2, ALU.mult)
                    tt(kd, ktm3[:, bi, :], E3, ALU.mult)
                    mm(PS[3][:, :128], kg, qg)
                    tt(att, PS[3][:, :128], MUT, ALU.mult)
                    for vc in range(2):
                        mm(PS[4 + vc][:, :128], vtm3[:, bi, vc * 128:(vc + 1) * 128], att, start=True, stop=False)
                        mm(PS[4 + vc][:, :128], S[:, vc * 128:(vc + 1) * 128], qg, start=False, stop=True)
                        act(oraw[vc][:, blk], PS[4 + vc][:, :128], AF.Copy)
                    mm(PS[6][:, :256], kd, vtm3[:, bi, :])
                    stt(S, S, eBl, PS[6][:, :256], ALU.mult, ALU.add)
                    import os
                    if os.environ.get("GLARAW") == "2":
                        cp(oraw[0][:, blk], PS[1][:, :128])
                        tr(PS[3][:, :128], sp)
                        cp(oraw[1][:, blk], PS[3][:, :128])
                for vc in range(2):
                    act(sq[:, :w], oraw[vc][:, :w], AF.Square)
                    mm(PS[7][:, :w], ONES, sq[:, :w], start=(vc == 0), stop=(vc == 1))
                rstd_from_ssq(r, PS[7], 1.0 / 256, w)
                for vc in range(2):
                    act(sg[:, :w], rT[vc][:, :w], AF.Silu)
                    stt(o[:, :w], oraw[vc][:, :w], sm(l, vc), r[:, :w], ALU.mult, ALU.mult)
                    tt(o[:, :w], o[:, :w], sg[:, :w], ALU.mult)
                    import os
                    if os.environ.get("GLARAW"):
                        cp(o[:, :w], oraw[vc][:, :w])
                    st_OL(vc * 128, t0, w, o)

        def gdn_phase(l):
            phase()
            sc = 128.0 ** -0.5
            Sst = [alloc("S%d" % h, 128) for h in range(3)]
            ba = alloc("ba", 4 * 6)
            ba3 = ba.re("p (b c) -> p b c", c=6)
            DT4, EA4 = alloc("DT4", 12), alloc("EA4", 12)
            t1, gg, beta, G, GL, eG, eGlG, eGL, bG = [alloc("sc%d" % i, NB * 3) for i in range(9)]
            xin = [alloc("xin%d" % i, 515) for i in range(3)]
            acc = alloc("acc", 512)
            yq, yk, yv = alloc("yq", 512), alloc("yk", 512), alloc("yv", 512)
            sq, rn, zt = alloc("sq", 512), alloc("rn", 512), alloc("zt", 512)
            oraw, o = alloc("oraw", 512), alloc("o", 512)
            names = ["kbg", "Kd", "betaV", "gbc", "d1", "d2", "EGb", "M", "Aqk", "qg", "MT", "Xa", "Xb",
                     "Na", "Nb", "NTa", "NTb", "nWT", "Vnew"]
            B = {n: alloc(n, 128) for n in names}
            for h in range(3):
                memset(Sst[h], 0.0)
            for bi in range(4):
                for h in range(3):
                    cp(DT4[:, bi * 3 + h:bi * 3 + h + 1], sm(l, 7 + h))
                    act(EA4[:, bi * 3 + h:bi * 3 + h + 1], sm(l, 4 + h), AF.Exp)
            ts(EA4, EA4, -1.0, ALU.mult)
            for (t0, w) in tok_tiles(NT):
                nb = w // 128
                n3 = nb * 3
                c0 = (t0 // 128) * 3
                cs = slice(c0, c0 + n3)
                ld(ba3[:, :nb, :], UTM[TM_GDN_BA * NT + t0:TM_GDN_BA * NT + t0 + w, 0:6].rearrange("(b p) c -> p b c", p=128), "UTM")
                for bi in range(nb):
                    tt(t1[:, bi * 3:bi * 3 + 3], ba3[:, bi, 3:6], DT4[:, bi * 3:bi * 3 + 3], ALU.add)
                    act(beta[:, c0 + bi * 3:c0 + bi * 3 + 3], ba3[:, bi, 0:3], AF.Exp, scale=-1.0)
                act(t1[:, :n3], t1[:, :n3], AF.Exp)
                act(t1[:, :n3], t1[:, :n3], AF.Ln, bias=ONE1)
                tt(gg[:, cs], t1[:, :n3], EA4[:, :n3], ALU.mult)
                ts(beta[:, cs], beta[:, cs], 1.0, ALU.add)
                recip(beta[:, cs], beta[:, cs])
                for bi in range(nb):
                    c3 = slice(c0 + bi * 3, c0 + bi * 3 + 3)
                    mm(PS[6][:, bi * 3:bi * 3 + 3], TRI1, gg[:, c3])
                    mm(PS[6][:, 16 + bi * 3:16 + bi * 3 + 3], ONES, gg[:, c3])
                cp(G[:, cs], PS[6][:, 0:n3])
                cp(GL[:, cs], PS[6][:, 16:16 + n3])
                act(eG[:, cs], G[:, cs], AF.Exp)
                act(eGL[:, cs], GL[:, cs], AF.Exp)
                tt(eGlG[:, cs], GL[:, cs], G[:, cs], ALU.subtract)
                act(eGlG[:, cs], eGlG[:, cs], AF.Exp)
                tt(bG[:, cs], beta[:, cs], eG[:, cs], ALU.mult)
            for h in range(3):
                for (t0, w) in tok_tiles(NT):
                    nb = w // 128
                    ys = [yq, yk, yv]
                    for j3, (fm0, y) in enumerate(((FM_GDN_Q, yq), (FM_GDN_K, yk), (FM_GDN_V, yv))):
                        x = xin[j3]
                        ct = fm0 + h
                        if t0 == 0:
                            memset(x[:, 0:3], 0.0)
                            ld(x[:, 3:3 + w], UFM[ct * 128:(ct + 1) * 128, 0:w], "UFM")
                        else:
                            ld(x[:, 0:3 + w], UFM[ct * 128:(ct + 1) * 128, t0 - 3:t0 + w], "UFM")
                        cw = lambda i: sm(l, 10 + (j3 * 3 + h) * 4 + i)
                        ts(acc[:, :w], x[:, 3:3 + w], cw(3), ALU.mult)
                        for i in (2, 1, 0):
                            stt(acc[:, :w], x[:, i:i + w], cw(i), acc[:, :w], ALU.mult, ALU.add)
                        act(y[:, :w], acc[:, :w], AF.Silu)
                        if j3 < 2:
                            act(sq[:, :w], y[:, :w], AF.Square)
                            mm(PS[5][:, :w], ONES, sq[:, :w])
                            rstd_from_ssq(rn, PS[5], 1.0, w)
                            if j3 == 0:
                                stt(y[:, :w], y[:, :w], sc, rn[:, :w], ALU.mult, ALU.mult)
                            else:
                                tt(y[:, :w], y[:, :w], rn[:, :w], ALU.mult)
                    ld(zt[:, :w], UFM[(FM_GDN_Z + h) * 128:(FM_GDN_Z + h + 1) * 128, t0:t0 + w], "UFM")
                    S = Sst[h]
                    for bi in range(nb):
                        blk = slice(bi * 128, (bi + 1) * 128)
                        col = (t0 // 128 + bi) * 3 + h
                        c1 = lambda t: t[:, col:col + 1]
                        pa, pb, pc, pd, pe = [PS[i][:, :128] for i in range(5)]
                        tr(pa, yk[:, blk])
                        ts(B["kbg"], pa, c1(bG), ALU.mult)
                        ts(B["Kd"], pa, c1(eGlG), ALU.mult)
                        tr(pb, yv[:, blk])
                        ts(B["betaV"], pb, c1(beta), ALU.mult)
                        mm(pc, yk[:, blk], yk[:, blk])
                        mm(pd, yk[:, blk], yq[:, blk])
                        ts(B["gbc"], ONES, c1(gg), ALU.mult)
                        mm(pe, B["gbc"], TRI1)
                        ts(B["d1"], pe, c1(G), ALU.subtract, 0.0, ALU.max)
                        act(B["d1"], B["d1"], AF.Exp, scale=-1.0)
                        ts(B["d2"], pe, c1(G), ALU.subtract, 0.0, ALU.min)
                        act(B["d2"], B["d2"], AF.Exp)
                        act(B["EGb"], pe, AF.Exp)
                        tt(B["M"], pc, B["d1"], ALU.mult)
                        stt(B["M"], B["M"], c1(beta), MSL, ALU.mult, ALU.mult)
                        tt(B["Aqk"], pd, B["d2"], ALU.mult)
                        tt(B["Aqk"], B["Aqk"], MUT, ALU.mult)
                        tt(B["qg"], yq[:, blk], B["EGb"], ALU.mult)
                        tr(pa, B["M"])
                        act(B["MT"], pa, AF.Copy)
                        tt(B["Xa"], IDENT, pa, ALU.subtract)
                        mm(pb, B["MT"], B["M"])
                        mm(pc, B["M"], B["MT"])
                        cp(B["Na"], pb)
                        act(B["NTa"], pc, AF.Copy)
                        X, X2 = B["Xa"], B["Xb"]
                        N, N2, NTt, NT2 = B["Na"], B["Nb"], B["NTa"], B["NTb"]
                        for it in range(6):
                            mm(pd, N, X)
                            tt(X2, pd, X, ALU.add)
                            X, X2 = X2, X
                            if it < 5:
                                mm(pb, NTt, N)
                                mm(pc, N, NTt)
                                cp(N2, pb)
                                act(NT2, pc, AF.Copy)
                                N, N2, NTt, NT2 = N2, N, NT2, NTt
                        mm(pa, B["kbg"], X)
                        ts(B["nWT"], pa, -1.0, ALU.mult)
                        mm(pb, X, B["betaV"], start=True, stop=False)
                        mm(pb, B["nWT"], S, start=False, stop=True)
                        cp(B["Vnew"], pb)
                        mm(pc, S, B["qg"], start=True, stop=False)
                        mm(pc, B["Vnew"], B["Aqk"], start=False, stop=True)
                        act(oraw[:, blk], pc, AF.Copy)
                        mm(pd, B["Kd"], B["Vnew"])
                        stt(S, S, c1(eGL), pd, ALU.mult, ALU.add)
                    act(sq[:, :w], oraw[:, :w], AF.Square)
                    mm(PS[7][:, :w], ONES, sq[:, :w])
                    rstd_from_ssq(rn, PS[7], 1.0 / 128, w)
                    act(sq[:, :w], zt[:, :w], AF.Silu)
                    stt(o[:, :w], oraw[:, :w], sm(l, 2), rn[:, :w], ALU.mult, ALU.mult)
                    tt(o[:, :w], o[:, :w], sq[:, :w], ALU.mult)
                    st_OL(256 + h * 128, t0, w, o)

        YP = dram("YP", [4 * D, SL])
        YS = dram("YS", [D, SL])

        def outproj_phase(l):
            phase()
            ot = alloc("ot", 8 * 512)
            wt = [alloc("wt%d" % i, 8 * 128) for i in range(2)]
            ev = [alloc("ev%d" % i, 512) for i in range(2)]
            nw = 0
            for (t0, w) in tok_tiles(NT):
                ot3 = ot[:, :8 * w].re("p (k t) -> p k t", t=w)
                ld(ot3, OL[:, t0:t0 + w].rearrange("(k p) t -> p k t", p=128), "OL")
                for j in range(KC):
                    w3 = wt[nw % 2].re("p (k c) -> p k c", c=128)
                    for half in range(2):
                        T = l * KC + j
                        base = (T // 4) * 1024 + half * 512 + (T % 4) * 128
                        ld(w3[:, half * 4:(half + 1) * 4, :],
                           WOUT[base:base + 128, :].rearrange("p (k c) -> p k c", c=128), "WOUT#%d" % (T // 4))
                    e = ev[nw % 2]
                    nw += 1
                    ps = PS[j % 2]
                    for lc in range(8):
                        mm(ps[:, :w], w3[:, lc, :], ot3[:, lc, :], start=(lc == 0), stop=(lc == 7))
                    if nw % 2:
                        cp(e[:, :w], ps[:, :w])
                    else:
                        act(e[:, :w], ps[:, :w], AF.Copy)
                    t = t0
                    while t < t0 + w:
                        s = t // SL
                        te = min(t0 + w, (s + 1) * SL)
                        st(YP[j * 512 + s * 128:j * 512 + (s + 1) * 128, t - s * SL:te - s * SL], "YP#%d" % j, e[:, t - t0:te - t0])
                        t = te
            for j in range(KC):
                P.dma("pool", lambda e, j=j: e.collective_compute(
                    "ReduceScatter", ALU.add, replica_groups=quads, ins=[YP[j * 512:(j + 1) * 512, :].opt()],
                    outs=[YS[j * 128:(j + 1) * 128, :].opt()]),
                    reads=["YP#%d" % j], writes=["YS#%d" % j], chan="cc_rs_y")

        hsrc_ap, hsrc_key = hT_in, None
        for l in range(L + 1):
            if cfg.stop == 0:
                break
            last = (l == L)
            hd = HB[l % 2]
            norm_phase(l, hsrc_ap, hsrc_key, l > 0, hd, "HB%d" % (l % 2),
                       out_ext if last else XNS, "OUT" if last else "XNS")
            if l > 0:
                hsrc_ap, hsrc_key = hd, "HB%d" % (l % 2)
            if last or cfg.stop == 1:
                break
            for i in range(2 * KC):
                allgather(XN[i * 256:(i + 1) * 256, :], "XN#%d" % i, XNS[i * 64:(i + 1) * 64, :], "XNS#%d" % (i // 2),
                          quads, "cc_xn")
            inproj_phase(l)
            if cfg.debug and l == 0:
                d2d(dbg["ufm"], "dbgufm", UFM, "UFM", "c_dbg1")
                d2d(dbg["utm"], "dbgutm", UTM, "UTM", "c_dbg2")
            if cfg.stop == 2:
                break
            if "sb" in cfg.mixers:
                sb_phase(l)
            if "gla" in cfg.mixers:
                gla_phase(l)
            if "gdn" in cfg.mixers:
                gdn_phase(l)
            if cfg.debug and l == 0:
                d2d(dbg["ol"], "dbgol", OL, "OL", "c_dbg3")
            outproj_phase(l)
        P.barrier()
        P.emit()
    return nc


def prep_inputs(cfg, x, meta, norm_g, w_in, gla_w_gate, gla_b_gate, gla_norm_g, gdn_conv_w,
                gdn_a_log, gdn_dt_bias, gdn_norm_g, sb_norm_g, w_out, final_g):
    D, KC, NT, SL, L = cfg.D, cfg.KC, cfg.NT, cfg.SL, cfg.depth
    KH = KC // 2
    NSM = 64
    f32 = np.float32
    consts = make_consts()
    normg = np.zeros((128, L + 1, KC), f32)
    for l in range(L):
        normg[:, l, :] = np.asarray(norm_g[l], f32).reshape(KC, 128).T
    normg[:, L, :] = np.asarray(final_g, f32).reshape(KC, 128).T
    in_maps = []
    wg_cache = {}
    for c in range(8):
        b, g = c // 4, c % 4
        h0 = np.zeros((NT, D), f32)
        h0[112:128] = meta
        h0[128:] = x[b]
        hT = np.ascontiguousarray(h0[g * SL:(g + 1) * SL].T)
        if g not in wg_cache:
            tiles = col_tiles(g)
            cols = np.concatenate(tiles)
            wl = []
            for l in range(L):
                wsel = np.where(cols[None, :] >= 0, np.asarray(w_in[l])[:, np.maximum(cols, 0)], 0.0).astype(f32)
                w4 = wsel.reshape(KC, 128, NCT, 128).transpose(2, 1, 0, 3)
                wl.append(w4)
            wg_cache[g] = np.stack(wl)
        wfull = wg_cache[g]
        win = np.ascontiguousarray(wfull[:, :, :, b * KH:(b + 1) * KH, :]).reshape(L * NCT * 128, KH * 128)
        rows = np.concatenate([g * 256 + np.arange(256), 1024 + 3 * g * 128 + np.arange(384),
                               2560 + 3 * g * 128 + np.arange(384)])
        wo = []
        for l in range(L):
            wp = np.asarray(w_out[l], f32)[rows]
            w4 = wp.reshape(8, 128, KC, 128).transpose(2, 1, 0, 3)
            wo.append(w4[:, :, b * 4:(b + 1) * 4, :])
        wout = np.ascontiguousarray(np.stack(wo)).reshape(L * KC * 128, 4 * 128)
        small = np.zeros((128, L, NSM), f32)
        for l in range(L):
            small[:, l, 0] = gla_norm_g[l][0:128]
            small[:, l, 1] = gla_norm_g[l][128:256]
            small[:, l, 2] = gdn_norm_g[l]
            small[:, l, 3] = sb_norm_g[l]
            for h in range(3):
                small[:, l, 4 + h] = gdn_a_log[l][3 * g + h]
                small[:, l, 7 + h] = gdn_dt_bias[l][3 * g + h]
            for j3 in range(3):
                for h in range(3):
                    ch = j3 * 1536 + (3 * g + h) * 128 + np.arange(128)
                    for i in range(4):
                        small[:, l, 10 + (j3 * 3 + h) * 4 + i] = gdn_conv_w[l][i, ch]
        glawg = np.ascontiguousarray(np.asarray(gla_w_gate, f32)[:L, :, g * 128:(g + 1) * 128].transpose(1, 0, 2))
        glabg = np.ascontiguousarray(np.asarray(gla_b_gate, f32)[:L, g * 128:(g + 1) * 128][None])
        in_maps.append({"hT": hT, "consts": consts, "normg": normg, "win": win, "wout": wout,
                        "small": small, "glawg": glawg, "glabg": glabg})
    return in_maps


def run(cfg, **inputs):
    inputs = {k: np.asarray(v) for k, v in inputs.items()}
    in_maps = prep_inputs(cfg, **inputs)
    nc = build(cfg)
    res = run_bass_kernel_spmd(nc, in_maps, core_ids=list(range(8)))
    D, NT, SL = cfg.D, cfg.NT, cfg.SL
    out = np.zeros((2, NT, D), np.float32)
    for c in range(8):
        b, g = c // 4, c % 4
        out[b, g * SL:(g + 1) * SL, :] = res.results[c]["outT"].T
    return np.ascontiguousarray(out[:, 128:, :]), res


def kernel(**inputs):
    out, _ = run(Cfg(), **inputs)
    return out
```

```python
import numpy as np
import concourse.bass as bass
import concourse.mybir as mybir
from concourse.bass_utils import run_bass_kernel_spmd
from contextlib import ExitStack

F32 = mybir.dt.float32
AF = mybir.ActivationFunctionType
ALU = mybir.AluOpType

EPOCH = 20000
DEPOCH = 1000
SAME_ENGINE_SYNC = True
EPS = 1e-6


class V:
    def __init__(self, ap, key):
        self.ap = ap
        self.key = key

    def __getitem__(self, idx):
        return V(self.ap[idx], self.key)

    def re(self, pat, **kw):
        return V(self.ap.rearrange(pat, **kw), self.key)


class Prog:
    ENGS = ("pe", "act", "dve", "pool", "sp")

    def __init__(self, nc, es):
        self.nc = nc
        self.es = es
        self.ops = []
        self.cnt = {e: 0 for e in self.ENGS}
        self.last_w = {}
        self.readers = {}
        self.chan_cnt = {}
        self.sems = {}

    def _deps(self, reads, writes):
        deps = []
        for k in reads:
            t = self.last_w.get(k)
            if t is not None:
                deps.append((t, True))
        for k in writes:
            t = self.last_w.get(k)
            if t is not None:
                deps.append((t, False))
            deps.extend((r, False) for r in self.readers.get(k, []))
        return deps

    def _commit(self, tok, reads, writes):
        for k in reads:
            self.readers.setdefault(k, []).append(tok)
        for k in writes:
            self.last_w[k] = tok
            self.readers[k] = []

    def op(self, eng, fn, reads=(), writes=()):
        reads = [k for k in reads if k is not None]
        writes = list(writes) + [k for k in reads if k.startswith("ps") and k not in writes]
        deps = self._deps(reads, writes)
        self.cnt[eng] += 1
        tok = ("c", eng, self.cnt[eng])
        self.ops.append((eng, fn, deps, tok))
        self._commit(tok, reads, writes)
        return tok

    def dma(self, eng, fn, reads=(), writes=(), chan=None):
        deps = self._deps(reads, writes)
        c = self.chan_cnt.get(chan, 0) + 1
        self.chan_cnt[chan] = c
        tok = ("d", chan, c)
        self.ops.append((eng, fn, deps, tok))
        self._commit(tok, reads, writes)
        return tok

    def barrier(self):
        keep = lambda k: k.startswith("WIN#") or k.startswith("WOUT#")
        deps = [(("c", e, n), True) for e, n in self.cnt.items() if n > 0]
        deps += [(("d", ch, n), True) for ch, n in self.chan_cnt.items() if not ch.startswith("cc_w")]
        for e in self.ENGS:
            self.ops.append((e, None, list(deps), None))
        self.last_w = {k: v for k, v in self.last_w.items() if keep(k)}
        self.readers = {k: v for k, v in self.readers.items() if keep(k)}

    def _tok_sem(self, tok):
        kind, who, n = tok
        if kind == "c":
            return ("c_%s_%d" % (who, (n - 1) // EPOCH), (n - 1) % EPOCH + 1)
        return ("d_%s_%d" % (who, (n - 1) // DEPOCH), (1 if who.startswith("cc") else 16) * ((n - 1) % DEPOCH + 1))

    def emit(self):
        nc = self.nc
        for eng, fn, deps, tok in self.ops:
            if tok is not None:
                s = self._tok_sem(tok)[0]
                if s not in self.sems:
                    self.sems[s] = self.es.enter_context(nc.semaphore(s))
        EI = {e: i for i, e in enumerate(self.ENGS)}
        know_c = {e: [0] * len(self.ENGS) for e in self.ENGS}
        know_d = {e: {} for e in self.ENGS}
        clock = {}
        plan = []
        for eng, fn, deps, tok in self.ops:
            kc, kd = know_c[eng], know_d[eng]
            need = []
            for d, is_raw in deps:
                if d[0] == "c":
                    if d[1] == eng:
                        if eng == "pe" or not SAME_ENGINE_SYNC or not is_raw:
                            continue
                        need.append(d)
                        continue
                    if kc[EI[d[1]]] >= d[2]:
                        continue
                else:
                    if kd.get(d[1], 0) >= d[2]:
                        continue
                need.append(d)
            best = {}
            for d in need:
                key = (d[0], d[1])
                if key not in best or best[key][2] < d[2]:
                    best[key] = d
            waits = []
            newd = None
            for d in best.values():
                waits.append(self._tok_sem(d))
                if d[0] == "c" and d[1] == eng:
                    continue
                ck = clock.get(d)
                if d[0] == "c":
                    if kc[EI[d[1]]] < d[2]:
                        kc[EI[d[1]]] = d[2]
                else:
                    if kd.get(d[1], 0) < d[2]:
                        if newd is None:
                            newd = dict(kd)
                        newd[d[1]] = d[2]
                if ck is not None:
                    cc, cd = ck
                    for i in range(len(cc)):
                        if kc[i] < cc[i]:
                            kc[i] = cc[i]
                    for ch, n in cd.items():
                        if (newd if newd is not None else kd).get(ch, 0) < n:
                            if newd is None:
                                newd = dict(kd)
                            newd[ch] = n
            if newd is not None:
                know_d[eng] = newd
            plan.append(waits)
            if tok is not None:
                clock[tok] = (tuple(kc), know_d[eng])
        per_eng = {e: [] for e in self.ENGS}
        for o, waits in zip(self.ops, plan):
            per_eng[o[0]].append((o, waits))
        with nc.Block() as block:
            def make(e):
                def body(engh):
                    for (eng, fn, deps, tok), waits in per_eng[e]:
                        for s, v in waits:
                            engh.wait_ge(self.sems[s], v)
                        if fn is None:
                            continue
                        ins = fn(engh)
                        s, v = self._tok_sem(tok)
                        ins.then_inc(self.sems[s], 16 if (tok[0] == "d" and not tok[1].startswith("cc")) else 1)
                return body
            for e, deco in (("pe", block.tensor), ("act", block.scalar), ("dve", block.vector),
                            ("pool", block.gpsimd), ("sp", block.sync)):
                if per_eng[e]:
                    deco(make(e))


class Cfg:
    def __init__(self, D=4096, NB=65, depth=2, debug=False, mixers=("sb", "gla", "gdn")):
        self.mixers = mixers
        self.stop = 99
        self.D = D
        self.KC = D // 128
        self.NB = NB
        self.NT = NB * 128
        assert self.NT % 4 == 0
        self.SL = self.NT // 4
        self.depth = depth
        self.debug = debug
        self.DMIX = 4096
        self.CC = 32


FM_GLA_Q, FM_GLA_K, FM_GLA_R0, FM_GLA_R1, FM_GLA_LR = 0, 1, 2, 3, 4
FM_GDN_Q, FM_GDN_K, FM_GDN_V, FM_GDN_Z = 5, 8, 11, 14
FM_SB_Q, FM_SB_K, FM_SB_G = 17, 20, 23
NFM = 26
TM_GLA_K, TM_GLA_V0, TM_GLA_V1, TM_GDN_BA, TM_SB_V = 0, 1, 2, 3, 4
NTM = 7
NCT = NFM + NTM

_OFF = {}
_o = 0
for _n, _w in (("gq", 512), ("gk", 512), ("gv", 1024), ("gr", 1024), ("glr", 16),
               ("dq", 1536), ("dk", 1536), ("dv", 1536), ("dz", 1536), ("db", 12), ("da", 12),
               ("sq", 1536), ("sk", 1536), ("sv", 1536), ("sg", 1536)):
    _OFF[_n] = _o
    _o += _w
D_IN = _o


def col_tiles(g):
    def rng(name, start, n):
        a = np.full(128, -1, np.int64)
        a[:n] = _OFF[name] + start + np.arange(n)
        return a
    fm = [None] * NFM
    fm[FM_GLA_Q] = rng("gq", g * 128, 128)
    fm[FM_GLA_K] = rng("gk", g * 128, 128)
    fm[FM_GLA_R0] = rng("gr", g * 256, 128)
    fm[FM_GLA_R1] = rng("gr", g * 256 + 128, 128)
    fm[FM_GLA_LR] = rng("glr", 0, 16)
    for h in range(3):
        hh = 3 * g + h
        fm[FM_GDN_Q + h] = rng("dq", hh * 128, 128)
        fm[FM_GDN_K + h] = rng("dk", hh * 128, 128)
        fm[FM_GDN_V + h] = rng("dv", hh * 128, 128)
        fm[FM_GDN_Z + h] = rng("dz", hh * 128, 128)
        fm[FM_SB_Q + h] = rng("sq", hh * 128, 128)
        fm[FM_SB_K + h] = rng("sk", hh * 128, 128)
        fm[FM_SB_G + h] = rng("sg", hh * 128, 128)
    tm = [None] * NTM
    tm[TM_GLA_K] = rng("gk", g * 128, 128)
    tm[TM_GLA_V0] = rng("gv", g * 256, 128)
    tm[TM_GLA_V1] = rng("gv", g * 256 + 128, 128)
    ba = np.full(128, -1, np.int64)
    ba[0:3] = _OFF["db"] + 3 * g + np.arange(3)
    ba[3:6] = _OFF["da"] + 3 * g + np.arange(3)
    tm[TM_GDN_BA] = ba
    for h in range(3):
        tm[TM_SB_V + h] = rng("sv", (3 * g + h) * 128, 128)
    return fm + tm


C_ID, C_ONES, C_UI, C_TRI, C_SFX, C_MUT, C_MSL, C_TRI1 = 0, 128, 256, 384, 512, 640, 768, 896
C_SBM = 1024
C_VALID = C_SBM + 4 * 512
C_ONE = C_VALID + 1
C_EPS = C_ONE + 1
C_LNS = C_EPS + 1
NCONST = C_LNS + 1


def make_consts():
    c = np.zeros((128, NCONST), np.float32)
    j = np.arange(128)[:, None]
    f = np.arange(128)[None, :]
    c[:, C_ID:C_ID + 128] = (j == f)
    c[:, C_ONES:C_ONES + 128] = 1.0
    c[:, C_UI:C_UI + 128] = (j >= f)
    c[:, C_TRI:C_TRI + 128] = (j <= f) * (-1.0 / 16.0)
    c[:, C_SFX:C_SFX + 128] = (j > f) * (-1.0 / 16.0)
    c[:, C_MUT:C_MUT + 128] = (j <= f)
    c[:, C_MSL:C_MSL + 128] = (f < j)
    c[:, C_TRI1:C_TRI1 + 128] = (j <= f)
    t = np.arange(512)[None, :]
    for m in range(4):
        c[:, C_SBM + m * 512:C_SBM + (m + 1) * 512] = ((m * 128 + j) < t)
    c[:, C_VALID] = (np.arange(128) >= 112)
    c[:, C_ONE] = 1.0
    c[:, C_EPS] = EPS
    c[:, C_LNS] = np.log(128.0 ** -0.5)
    return c


ARENA = 47000


def build(cfg):
    nc = bass.Bass("TRN2", target_bir_lowering=False)
    D, KC, NB, NT, SL, L = cfg.D, cfg.KC, cfg.NB, cfg.NT, cfg.SL, cfg.depth
    KH = KC // 2
    dram_in = lambda name, shape: nc.dram_tensor(name, shape, F32, kind="ExternalInput").ap()
    dram_out = lambda name, shape: nc.dram_tensor(name, shape, F32, kind="ExternalOutput").ap()
    dram = lambda name, shape: nc.dram_tensor(name, shape, F32).ap()

    hT_in = dram_in("hT", [D, SL])
    consts_in = dram_in("consts", [128, NCONST])
    normg_in = dram_in("normg", [128, L + 1, KC])
    win_in = dram_in("win", [L * NCT * 128, KH * 128])
    wout_in = dram_in("wout", [L * KC * 128, 4 * 128])
    NSM = 64
    small_in = dram_in("small", [128, L, NSM])
    glawg_in = dram_in("glawg", [16, L, 128])
    glabg_in = dram_in("glabg", [1, L, 128])
    out_ext = dram_out("outT", [D, SL])

    win_b = dram("win_b", [L * NCT * 128, KH * 128])
    WIN = dram("WIN", [2 * L * NCT * 128, KH * 128])
    wout_b = dram("wout_b", [L * KC * 128, 4 * 128])
    WOUT = dram("WOUT", [2 * L * KC * 128, 4 * 128])
    XNS = dram("XNS", [D, SL])
    XN = dram("XN", [4 * D, SL])
    UFM = dram("UFM", [NFM * 128, NT])
    UTM = dram("UTM", [NTM * NT, 128])
    OL = dram("OL", [1024, NT])
    HB = [dram("HB%d" % i, [D, SL]) for i in range(2)]
    dbg = {}
    if cfg.debug:
        dbg["ufm"] = dram_out("dbg_ufm", [NFM * 128, NT])
        dbg["utm"] = dram_out("dbg_utm", [NTM * NT, 128])
        dbg["ol"] = dram_out("dbg_ol", [1024, NT])

    with ExitStack() as es:
        P = Prog(nc, es)
        arena_t = es.enter_context(nc.sbuf_tensor("arena", [128, ARENA], F32))
        const_t = es.enter_context(nc.sbuf_tensor("constt", [128, NCONST], F32))
        ng_t = es.enter_context(nc.sbuf_tensor("ngt", [128, (L + 1) * KC], F32))
        small_t = es.enter_context(nc.sbuf_tensor("smallt", [128, L * NSM], F32))
        PS = [V(es.enter_context(nc.psum_tensor("ps%d" % i, [128, 512], F32))[:], "ps%d" % i)
              for i in range(8)]
        CONST = V(const_t[:], "const")
        NG = V(ng_t[:], "ng")
        SMALL = V(small_t[:], "small")
        state = {"off": 0, "phase": 0}

        def phase():
            P.barrier()
            state["off"] = 0
            state["phase"] += 1

        def alloc(name, n):
            off = state["off"]
            assert off + n <= ARENA, (name, off, n)
            state["off"] = off + n
            return V(arena_t[:, off:off + n], "%s@%d" % (name, state["phase"]))

        cst = lambda c0, n=128: CONST[:, c0:c0 + n]
        IDENT, ONES = cst(C_ID), cst(C_ONES)
        ONE1, EPS1, LNS1 = cst(C_ONE, 1), cst(C_EPS, 1), cst(C_LNS, 1)

        def ld(out, in_ap, rkey=None, eng="sp"):
            rk = [] if rkey is None else (list(rkey) if isinstance(rkey, (list, tuple)) else [rkey])
            P.dma(eng, lambda e: e.dma_start(out=out.ap, in_=in_ap), reads=rk,
                  writes=[out.key], chan="L" + out.key)

        def st(out_ap, wkey, in_, eng="pool"):
            P.dma(eng, lambda e: e.dma_start(out=out_ap, in_=in_.ap), reads=[in_.key],
                  writes=[wkey] if wkey else [], chan="S" + in_.key)

        def d2d(out_ap, wkey, in_ap, rkey, chan, eng="sp"):
            nrows = out_ap.shape[0]
            for r0 in range(0, nrows, 1024):
                r1 = min(nrows, r0 + 1024)
                P.dma(eng, lambda e, r0=r0, r1=r1: e.dma_start(out=out_ap[r0:r1, :], in_=in_ap[r0:r1, :]),
                      reads=[rkey] if rkey else [], writes=[wkey], chan=chan)

        def allgather(out_ap, wkey, in_ap, rkey, groups, chan):
            P.dma("pool", lambda e: e.collective_compute(
                "AllGather", ALU.bypass, replica_groups=groups, ins=[in_ap.opt()], outs=[out_ap.opt()]),
                reads=[rkey], writes=[wkey], chan=chan)

        def mm(out, lhsT, rhs, start=True, stop=True):
            P.op("pe", lambda e: e.matmul(out.ap, lhsT=lhsT.ap, rhs=rhs.ap, start=start, stop=stop),
                 reads=[lhsT.key, rhs.key], writes=[out.key])

        def tr(out, in_):
            P.op("pe", lambda e: e.transpose(out.ap, in_.ap, IDENT.ap),
                 reads=[in_.key, "const"], writes=[out.key])

        def act(out, in_, func, bias=None, scale=None, eng="act"):
            kw = {}
            rk = [in_.key]
            if bias is not None:
                if isinstance(bias, V):
                    kw["bias"] = bias.ap
                    rk.append(bias.key)
                else:
                    kw["bias"] = bias
            if scale is not None:
                if isinstance(scale, V):
                    kw["scale"] = scale.ap
                    rk.append(scale.key)
                else:
                    kw["scale"] = scale
            P.op("act", lambda e: e.activation(out=out.ap, in_=in_.ap, func=func, **kw),
                 reads=rk, writes=[out.key])

        def tt(out, a, b, op, eng="dve"):
            P.op(eng, lambda e: e.tensor_tensor(out=out.ap, in0=a.ap, in1=b.ap, op=op),
                 reads=[a.key, b.key], writes=[out.key])

        def ts(out, a, s1, op0, s2=None, op1=None, eng="dve"):
            rk = [a.key]
            v1 = s1
            if isinstance(s1, V):
                v1 = s1.ap
                rk.append(s1.key)
            v2 = s2
            if isinstance(s2, V):
                v2 = s2.ap
                rk.append(s2.key)
            if op1 is None:
                P.op(eng, lambda e: e.tensor_scalar(out=out.ap, in0=a.ap, scalar1=v1, scalar2=None, op0=op0),
                     reads=rk, writes=[out.key])
            else:
                P.op(eng, lambda e: e.tensor_scalar(out=out.ap, in0=a.ap, scalar1=v1, scalar2=v2,
                                                    op0=op0, op1=op1), reads=rk, writes=[out.key])

        def stt(out, a, s, b, op0, op1):
            rk = [a.key, b.key]
            sv = s
            if isinstance(s, V):
                sv = s.ap
                rk.append(s.key)
            P.op("dve", lambda e: e.scalar_tensor_tensor(out=out.ap, in0=a.ap, scalar=sv, in1=b.ap,
                                                         op0=op0, op1=op1), reads=rk, writes=[out.key])

        def cp(out, in_, eng="dve"):
            P.op(eng, lambda e: e.tensor_copy(out=out.ap, in_=in_.ap), reads=[in_.key], writes=[out.key])

        def memset(out, val, eng="dve"):
            P.op(eng, lambda e: e.memset(out.ap, val), writes=[out.key])

        def recip(out, in_):
            P.op("dve", lambda e: e.reciprocal(out=out.ap, in_=in_.ap), reads=[in_.key], writes=[out.key])

        def rstd_from_ssq(dst, ssq_ps, inv_n, w):
            act(dst[:, :w], ssq_ps[:, :w], AF.Sqrt, bias=EPS1, scale=inv_n)
            recip(dst[:, :w], dst[:, :w])

        ld(CONST, consts_in)
        ld(NG, normg_in.rearrange("p l k -> p (l k)"))
        ld(SMALL, small_in.rearrange("p l k -> p (l k)"))
        d2d(win_b, "win_b", win_in, None, "c_win")
        d2d(wout_b, "wout_b", wout_in, None, "c_wout")
        pairs = [[0, 4], [1, 5], [2, 6], [3, 7]]
        quads = [[0, 1, 2, 3], [4, 5, 6, 7]]
        for T in range(L * NCT):
            allgather(WIN[T * 256:(T + 1) * 256, :], "WIN#%d" % T, win_b[T * 128:(T + 1) * 128, :], "win_b", pairs, "cc_win")
        for q in range(L * KC // 4):
            allgather(WOUT[q * 1024:(q + 1) * 1024, :], "WOUT#%d" % q, wout_b[q * 512:(q + 1) * 512, :], "wout_b", pairs, "cc_wout")

        def tok_tiles(n, w=512):
            return [(t0, min(w, n - t0)) for t0 in range(0, n, w)]

        def norm_phase(l, src_ap, src_key, add_y, hdst_ap, hdst_key, dst_ap, dst_key):
            phase()
            W_ = 256
            h = alloc("h", KC * W_)
            y = alloc("y", KC * W_)
            sq = [alloc("sq%d" % i, W_) for i in range(2)]
            r = alloc("r", W_)
            xo = [alloc("xo%d" % i, W_) for i in range(2)]
            for (t0, w) in tok_tiles(SL, W_):
                h3 = h[:, :KC * w].re("p (k t) -> p k t", t=w)
                ld_chunks(h3, src_ap[:, t0:t0 + w].rearrange("(k p) t -> p k t", p=128), src_key, KC, 8)
                if add_y:
                    y3 = y[:, :KC * w].re("p (k t) -> p k t", t=w)
                    ysrc = YS[:, t0:t0 + w].rearrange("(k p) t -> p k t", p=128)
                    for k0 in range(0, KC, 8):
                        k1 = min(KC, k0 + 8)
                        ld(y3[:, k0:k1, :], ysrc[:, k0:k1, :], ["YS#%d" % k for k in range(k0, k1)])
                    tt(h[:, :KC * w], h[:, :KC * w], y[:, :KC * w], ALU.add)
                    for k0 in range(0, KC, 8):
                        k1 = min(KC, k0 + 8)
                        st(hdst_ap[k0 * 128:k1 * 128, t0:t0 + w].rearrange("(k p) t -> p k t", p=128), hdst_key,
                           h3[:, k0:k1, :])
                for kc in range(KC):
                    s = sq[kc % 2]
                    act(s[:, :w], h3[:, kc, :], AF.Square)
                    mm(PS[0][:, :w], ONES, s[:, :w], start=(kc == 0), stop=(kc == KC - 1))
                rstd_from_ssq(r, PS[0], 1.0 / D, w)
                for kc in range(KC):
                    x = xo[kc % 2]
                    stt(x[:, :w], h3[:, kc, :], NG[:, l * KC + kc:l * KC + kc + 1], r[:, :w], ALU.mult, ALU.mult)
                    st(dst_ap[kc * 128:(kc + 1) * 128, t0:t0 + w], "%s#%d" % (dst_key, kc), x[:, :w])

        def inproj_phase(l):
            phase()
            xnb = [alloc("xn%d" % i, KC * 512) for i in range(2)]
            wt = [alloc("wt%d" % i, KC * 128) for i in range(2)]
            ev = [alloc("ev%d" % i, 512) for i in range(2)]
            ev2 = [alloc("ev2%d" % i, 512) for i in range(2)]
            nw = 0
            ne = 0
            tiles = tok_tiles(NT)

            def load_xn(ti):
                t0, w = tiles[ti]
                xn3 = xnb[ti % 2][:, :KC * w].re("p (k t) -> p k t", t=w)
                t = t0
                while t < t0 + w:
                    rk = t // SL
                    te = min(t0 + w, (rk + 1) * SL)
                    for ph in range(2):
                        srcv = XN.rearrange("(k ph r pl) t -> ph r pl k t", ph=2, r=4, pl=64)[ph, rk]
                        for k0 in range(0, KC, 8):
                            k1 = min(KC, k0 + 8)
                            ld(xn3[ph * 64:(ph + 1) * 64, k0:k1, t - t0:te - t0],
                               srcv[:, k0:k1, t - rk * SL:te - rk * SL],
                               ["XN#%d" % (2 * k + ph) for k in range(k0, k1)])
                    t = te

            load_xn(0)
            for ti, (t0, w) in enumerate(tiles):
                xn = xnb[ti % 2]
                xn3 = xn[:, :KC * w].re("p (k t) -> p k t", t=w)
                for ct in range(NCT):
                    if ct == 3 and ti + 1 < len(tiles):
                        load_xn(ti + 1)
                    wtile = wt[nw % 2]
                    nw += 1
                    w3 = wtile.re("p (k c) -> p k c", c=128)
                    for half in range(2):
                        T = l * NCT + ct
                        base = (T * 2 + half) * 128
                        ld(w3[:, half * KH:(half + 1) * KH, :],
                           WIN[base:base + 128, :].rearrange("p (k c) -> p k c", c=128), "WIN#%d" % T)
                    if ct < NFM:
                        ps = PS[ct % 2]
                        for kc in range(KC):
                            mm(ps[:, :w], w3[:, kc, :], xn3[:, kc, :], start=(kc == 0), stop=(kc == KC - 1))
                        e = ev[ne % 2]
                        ne += 1
                        if ne % 2:
                            cp(e[:, :w], ps[:, :w], eng="dve")
                        else:
                            act(e[:, :w], ps[:, :w], AF.Copy)
                        st(UFM[ct * 128:(ct + 1) * 128, t0:t0 + w], "UFM", e[:, :w])
                    else:
                        tmi = ct - NFM
                        nblk = w // 128
                        ps = PS[ct % 2]
                        for kc in range(KC):
                            mm(ps[:, :w], w3[:, kc, :], xn3[:, kc, :], start=(kc == 0), stop=(kc == KC - 1))
                        e = ev[ne % 2]
                        ne += 1
                        act(e[:, :w], ps[:, :w], AF.Copy)
                        ps2 = PS[2 + (tmi % 2)]
                        for bi in range(nblk):
                            tr(ps2[:, bi * 128:(bi + 1) * 128], e[:, bi * 128:(bi + 1) * 128])
                        e2 = ev2[tmi % 2]
                        cp(e2[:, :w], ps2[:, :w], eng="dve")
                        st(UTM[tmi * NT + t0:tmi * NT + t0 + w, :].rearrange("(b p) c -> p b c", p=128),
                           "UTM", e2[:, :w].re("p (b c) -> p b c", c=128))

        def ld_chunks(dst3, src3, key, n, step):
            for a in range(0, n, step):
                b = min(n, a + step)
                ld(dst3[:, a:b, :], src3[:, a:b, :], key)

        def st_OL(c0, t0, w, tile):
            st(OL[c0:c0 + 128, t0:t0 + w], "OL", tile[:, :w])

        SBM = [CONST[:, C_SBM + m * 512:C_SBM + (m + 1) * 512] for m in range(4)]
        VALID1 = cst(C_VALID, 1)
        UI, TRI, SFX, MUT, MSL, TRI1 = cst(C_UI), cst(C_TRI), cst(C_SFX), cst(C_MUT), cst(C_MSL), cst(C_TRI1)
        NSM_ = NSM

        def sm(l, idx):
            return SMALL[:, l * NSM_ + idx:l * NSM_ + idx + 1]

        def sb_phase(l):
            import os
            CUT = int(os.environ.get("CUT", "99"))
            phase()
            sc = 128.0 ** -0.5
            kT = alloc("kT", NT)
            vtm = alloc("vtm", NB * 128)
            vtm3 = vtm.re("p (b d) -> p b d", d=128)
            qs, nqs, S = alloc("qs", 512), alloc("nqs", 512), alloc("S", 512)
            E = [alloc("E%d" % i, 512) for i in range(2)]
            SP = [alloc("SP%d" % i, 512) for i in range(2)]
            A = [alloc("A%d" % i, 512) for i in range(2)]
            osb, sq, r, gt = alloc("osb", 512), alloc("sq", 512), alloc("r", 512), alloc("gt", 512)
            for h in range(3):
                ld(kT, UFM[(FM_SB_K + h) * 128:(FM_SB_K + h + 1) * 128, :], "UFM")
                ld_chunks(vtm3, UTM[(TM_SB_V + h) * NT:(TM_SB_V + h + 1) * NT, :].rearrange("(b p) d -> p b d", p=128),
                          "UTM", NB, 4)
                for (q0, w) in tok_tiles(NT):
                    qb0, nqb = q0 // 128, w // 128
                    ld(qs[:, :w], UFM[(FM_SB_Q + h) * 128:(FM_SB_Q + h + 1) * 128, q0:q0 + w], "UFM")
                    ld(gt[:, :w], UFM[(FM_SB_G + h) * 128:(FM_SB_G + h + 1) * 128, q0:q0 + w], "UFM")
                    ts(nqs[:, :w], qs[:, :w], -sc, ALU.mult)
                    ts(qs[:, :w], qs[:, :w], sc, ALU.mult)
                    kbs = list(range(qb0 + nqb - 1, -1, -1))
                    nu = len(kbs)

                    def stageA(i):
                        kb = kbs[i]
                        Zb, e, sp = PS[i % 2], E[i % 2], SP[i % 2]
                        kblk = kT[:, kb * 128:(kb + 1) * 128]
                        m = kb - qb0
                        mm(Zb[:, :w], kblk, qs[:, :w])
                        act(e[:, :w], Zb[:, :w], AF.Exp)
                        act(sp[:, :w], e[:, :w], AF.Ln, bias=ONE1)
                        if m >= 0:
                            tt(sp[:, :w], sp[:, :w], SBM[m][:, :w], ALU.mult)
                        if kb == 0:
                            ts(sp[:, :w], sp[:, :w], VALID1, ALU.mult)

                    def stageB(i):
                        kb = kbs[i]
                        Tb, sp, a, e = PS[2 + i % 2], SP[i % 2], A[i % 2], E[i % 2]
                        m = kb - qb0
                        first = (i == 0)
                        mm(Tb[:, :w], UI, sp[:, :w], start=True, stop=first)
                        if not first:
                            mm(Tb[:, :w], ONES, S[:, :w], start=False, stop=True)
                        act(a[:, :w], Tb[:, :w], AF.Exp, scale=-1.0)
                        tt(a[:, :w], a[:, :w], e[:, :w], ALU.mult)
                        if m >= 0:
                            tt(a[:, :w], a[:, :w], SBM[m][:, :w], ALU.mult)
                        if kb == 0:
                            ts(a[:, :w], a[:, :w], VALID1, ALU.mult)
                        if first:
                            cp(S[:, :w], sp[:, :w], eng="pool")
                        elif i < nu - 1:
                            tt(S[:, :w], S[:, :w], sp[:, :w], ALU.add, eng="pool")

                    def stageC(i):
                        kb = kbs[i]
                        mm(PS[4][:, :w], vtm3[:, kb, :], A[i % 2][:, :w], start=(i == 0), stop=(i == nu - 1))

                    stageA(0)
                    for i in range(nu):
                        if i + 1 < nu:
                            stageA(i + 1)
                        stageB(i)
                        if i >= 1:
                            stageC(i - 1)
                    stageC(nu - 1)
                    if CUT <= 4:
                        continue
                    cp(osb[:, :w], PS[4][:, :w])
                    act(sq[:, :w], PS[4][:, :w], AF.Square)
                    mm(PS[5][:, :w], ONES, sq[:, :w])
                    rstd_from_ssq(r, PS[5], 1.0 / 128, w)
                    act(sq[:, :w], gt[:, :w], AF.Silu)
                    stt(osb[:, :w], osb[:, :w], sm(l, 3), r[:, :w], ALU.mult, ALU.mult)
                    tt(osb[:, :w], osb[:, :w], sq[:, :w], ALU.mult)
                    if CUT <= 5:
                        continue
                    st_OL(5 * 128 + h * 128, q0, w, osb)

        def gla_phase(l):
            phase()
            S = alloc("S", 256)
            WG, BG = alloc("WG", 128), alloc("BG", 128)
            qT, kT, lr = alloc("qT", 512), alloc("kT", 512), alloc("lr", 512)
            rT = [alloc("r0", 512), alloc("r1", 512)]
            ktm, vtm = alloc("ktm", 512), alloc("vtm", 1024)
            ktm3 = ktm.re("p (b d) -> p b d", d=128)
            vtm3 = vtm.re("p (b d) -> p b d", d=256)
            e1, sp, bl, eBl = alloc("e1", 128), alloc("sp", 128), alloc("bl", 1), alloc("eBl", 1)
            E1, E2, E3 = alloc("E1", 128), alloc("E2", 128), alloc("E3", 128)
            qg, kg, kd, att = alloc("qg", 128), alloc("kg", 128), alloc("kd", 128), alloc("att", 128)
            oraw = [alloc("oraw0", 512), alloc("oraw1", 512)]
            sq, r, sg, o = alloc("sq", 512), alloc("r", 512), alloc("sg", 512), alloc("o", 512)
            memset(S, 0.0)
            memset(WG, 0.0)
            memset(BG, 0.0)
            ld(WG[0:16, :], glawg_in[:, l, :])
            ld(BG[0:1, :], glabg_in[:, l, :])
            fmrow = lambda ct, t0, w: UFM[ct * 128:(ct + 1) * 128, t0:t0 + w]
            for (t0, w) in tok_tiles(NT):
                nb = w // 128
                ld(qT[:, :w], fmrow(FM_GLA_Q, t0, w), "UFM")
                ld(kT[:, :w], fmrow(FM_GLA_K, t0, w), "UFM")
                ld(lr[:, :w], fmrow(FM_GLA_LR, t0, w), "UFM")
                ld(rT[0][:, :w], fmrow(FM_GLA_R0, t0, w), "UFM")
                ld(rT[1][:, :w], fmrow(FM_GLA_R1, t0, w), "UFM")
                ld(ktm3[:, :nb, :], UTM[TM_GLA_K * NT + t0:TM_GLA_K * NT + t0 + w, :].rearrange("(b p) d -> p b d", p=128), "UTM")
                for vc in range(2):
                    ld(vtm3[:, :nb, vc * 128:(vc + 1) * 128],
                       UTM[(TM_GLA_V0 + vc) * NT + t0:(TM_GLA_V0 + vc) * NT + t0 + w, :].rearrange("(b p) d -> p b d", p=128), "UTM")
                for bi in range(nb):
                    blk = slice(bi * 128, (bi + 1) * 128)
                    mm(PS[0][:, :128], lr[:, blk], WG, start=True, stop=False)
                    mm(PS[0][:, :128], ONES, BG, start=False, stop=True)
                    act(e1, PS[0][:, :128], AF.Exp, scale=-1.0)
                    act(sp, e1, AF.Ln, bias=ONE1)
                    mm(PS[1][:, :128], sp, TRI)
                    mm(PS[2][:, :128], SFX, sp)
                    cp(bl, PS[1][:, 127:128])
                    act(E1, PS[1][:, :128], AF.Exp, bias=LNS1)
                    act(E2, PS[1][:, :128], AF.Exp, scale=-1.0)
                    act(E3, PS[2][:, :128], AF.Exp)
                    act(eBl, bl, AF.Exp)
                    tt(qg, qT[:, blk], E1, ALU.mult)
                    tt(kg, kT[:, blk], E2, ALU.mult)
                    tt(kd, ktm3[:, bi, :], E3, ALU.mult)
                    mm(PS[3][:, :128], kg, qg)
                    tt(att, PS[3][:, :128], MUT, ALU.mult)
                    for vc in range(2):
                        mm(PS[4 + vc][:, :128], vtm3[:, bi, vc * 128:(vc + 1) * 128], att, start=True, stop=False)
                        mm(PS[4 + vc][:, :128], S[:, vc * 128:(vc + 1) * 128], qg, start=False, stop=True)
                        act(oraw[vc][:, blk], PS[4 + vc][:, :128], AF.Copy)
                    mm(PS[6][:, :256], kd, vtm3[:, bi, :])
                    stt(S, S, eBl, PS[6][:, :256], ALU.mult, ALU.add)
                    import os
                    if os.environ.get("GLARAW") == "2":
                        cp(oraw[0][:, blk], PS[1][:, :128])
                        tr(PS[3][:, :128], sp)
                        cp(oraw[1][:, blk], PS[3][:, :128])
                for vc in range(2):
                    act(sq[:, :w], oraw[vc][:, :w], AF.Square)
                    mm(PS[7][:, :w], ONES, sq[:, :w], start=(vc == 0), stop=(vc == 1))
                rstd_from_ssq(r, PS[7], 1.0 / 256, w)
                for vc in range(2):
                    act(sg[:, :w], rT[vc][:, :w], AF.Silu)
                    stt(o[:, :w], oraw[vc][:, :w], sm(l, vc), r[:, :w], ALU.mult, ALU.mult)
                    tt(o[:, :w], o[:, :w], sg[:, :w], ALU.mult)
                    import os
                    if os.environ.get("GLARAW"):
                        cp(o[:, :w], oraw[vc][:, :w])
                    st_OL(vc * 128, t0, w, o)

        def gdn_phase(l):
            phase()
            sc = 128.0 ** -0.5
            Sst = [alloc("S%d" % h, 128) for h in range(3)]
            ba = alloc("ba", 4 * 6)
            ba3 = ba.re("p (b c) -> p b c", c=6)
            DT4, EA4 = alloc("DT4", 12), alloc("EA4", 12)
            t1, gg, beta, G, GL, eG, eGlG, eGL, bG = [alloc("sc%d" % i, NB * 3) for i in range(9)]
            xin = [alloc("xin%d" % i, 515) for i in range(3)]
            acc = alloc("acc", 512)
            yq, yk, yv = alloc("yq", 512), alloc("yk", 512), alloc("yv", 512)
            sq, rn, zt = alloc("sq", 512), alloc("rn", 512), alloc("zt", 512)
            oraw, o = alloc("oraw", 512), alloc("o", 512)
            names = ["kbg", "Kd", "betaV", "gbc", "d1", "d2", "EGb", "M", "Aqk", "qg", "MT", "Xa", "Xb",
                     "Na", "Nb", "NTa", "NTb", "nWT", "Vnew"]
            B = {n: alloc(n, 128) for n in names}
            for h in range(3):
                memset(Sst[h], 0.0)
            for bi in range(4):
                for h in range(3):
                    cp(DT4[:, bi * 3 + h:bi * 3 + h + 1], sm(l, 7 + h))
                    act(EA4[:, bi * 3 + h:bi * 3 + h + 1], sm(l, 4 + h), AF.Exp)
            ts(EA4, EA4, -1.0, ALU.mult)
            for (t0, w) in tok_tiles(NT):
                nb = w // 128
                n3 = nb * 3
                c0 = (t0 // 128) * 3
                cs = slice(c0, c0 + n3)
                ld(ba3[:, :nb, :], UTM[TM_GDN_BA * NT + t0:TM_GDN_BA * NT + t0 + w, 0:6].rearrange("(b p) c -> p b c", p=128), "UTM")
                for bi in range(nb):
                    tt(t1[:, bi * 3:bi * 3 + 3], ba3[:, bi, 3:6], DT4[:, bi * 3:bi * 3 + 3], ALU.add)
                    act(beta[:, c0 + bi * 3:c0 + bi * 3 + 3], ba3[:, bi, 0:3], AF.Exp, scale=-1.0)
                act(t1[:, :n3], t1[:, :n3], AF.Exp)
                act(t1[:, :n3], t1[:, :n3], AF.Ln, bias=ONE1)
                tt(gg[:, cs], t1[:, :n3], EA4[:, :n3], ALU.mult)
                ts(beta[:, cs], beta[:, cs], 1.0, ALU.add)
                recip(beta[:, cs], beta[:, cs])
                for bi in range(nb):
                    c3 = slice(c0 + bi * 3, c0 + bi * 3 + 3)
                    mm(PS[6][:, bi * 3:bi * 3 + 3], TRI1, gg[:, c3])
                    mm(PS[6][:, 16 + bi * 3:16 + bi * 3 + 3], ONES, gg[:, c3])
                cp(G[:, cs], PS[6][:, 0:n3])
                cp(GL[:, cs], PS[6][:, 16:16 + n3])
                act(eG[:, cs], G[:, cs], AF.Exp)
                act(eGL[:, cs], GL[:, cs], AF.Exp)
                tt(eGlG[:, cs], GL[:, cs], G[:, cs], ALU.subtract)
                act(eGlG[:, cs], eGlG[:, cs], AF.Exp)
                tt(bG[:, cs], beta[:, cs], eG[:, cs], ALU.mult)
            for h in range(3):
                for (t0, w) in tok_tiles(NT):
                    nb = w // 128
                    ys = [yq, yk, yv]
                    for j3, (fm0, y) in enumerate(((FM_GDN_Q, yq), (FM_GDN_K, yk), (FM_GDN_V, yv))):
                        x = xin[j3]
                        ct = fm0 + h
                        if t0 == 0:
                            memset(x[:, 0:3], 0.0)
                            ld(x[:, 3:3 + w], UFM[ct * 128:(ct + 1) * 128, 0:w], "UFM")
                        else:
                            ld(x[:, 0:3 + w], UFM[ct * 128:(ct + 1) * 128, t0 - 3:t0 + w], "UFM")
                        cw = lambda i: sm(l, 10 + (j3 * 3 + h) * 4 + i)
                        ts(acc[:, :w], x[:, 3:3 + w], cw(3), ALU.mult)
                        for i in (2, 1, 0):
                            stt(acc[:, :w], x[:, i:i + w], cw(i), acc[:, :w], ALU.mult, ALU.add)
                        act(y[:, :w], acc[:, :w], AF.Silu)
                        if j3 < 2:
                            act(sq[:, :w], y[:, :w], AF.Square)
                            mm(PS[5][:, :w], ONES, sq[:, :w])
                            rstd_from_ssq(rn, PS[5], 1.0, w)
                            if j3 == 0:
                                stt(y[:, :w], y[:, :w], sc, rn[:, :w], ALU.mult, ALU.mult)
                            else:
                                tt(y[:, :w], y[:, :w], rn[:, :w], ALU.mult)
                    ld(zt[:, :w], UFM[(FM_GDN_Z + h) * 128:(FM_GDN_Z + h + 1) * 128, t0:t0 + w], "UFM")
                    S = Sst[h]
                    for bi in range(nb):
                        blk = slice(bi * 128, (bi + 1) * 128)
                        col = (t0 // 128 + bi) * 3 + h
                        c1 = lambda t: t[:, col:col + 1]
                        pa, pb, pc, pd, pe = [PS[i][:, :128] for i in range(5)]
                        tr(pa, yk[:, blk])
                        ts(B["kbg"], pa, c1(bG), ALU.mult)
                        ts(B["Kd"], pa, c1(eGlG), ALU.mult)
                        tr(pb, yv[:, blk])
                        ts(B["betaV"], pb, c1(beta), ALU.mult)
                        mm(pc, yk[:, blk], yk[:, blk])
                        mm(pd, yk[:, blk], yq[:, blk])
                        ts(B["gbc"], ONES, c1(gg), ALU.mult)
                        mm(pe, B["gbc"], TRI1)
                        ts(B["d1"], pe, c1(G), ALU.subtract, 0.0, ALU.max)
                        act(B["d1"], B["d1"], AF.Exp, scale=-1.0)
                        ts(B["d2"], pe, c1(G), ALU.subtract, 0.0, ALU.min)
                        act(B["d2"], B["d2"], AF.Exp)
                        act(B["EGb"], pe, AF.Exp)
                        tt(B["M"], pc, B["d1"], ALU.mult)
                        stt(B["M"], B["M"], c1(beta), MSL, ALU.mult, ALU.mult)
                        tt(B["Aqk"], pd, B["d2"], ALU.mult)
                        tt(B["Aqk"], B["Aqk"], MUT, ALU.mult)
                        tt(B["qg"], yq[:, blk], B["EGb"], ALU.mult)
                        tr(pa, B["M"])
                        act(B["MT"], pa, AF.Copy)
                        tt(B["Xa"], IDENT, pa, ALU.subtract)
                        mm(pb, B["MT"], B["M"])
                        mm(pc, B["M"], B["MT"])
                        cp(B["Na"], pb)
                        act(B["NTa"], pc, AF.Copy)
                        X, X2 = B["Xa"], B["Xb"]
                        N, N2, NTt, NT2 = B["Na"], B["Nb"], B["NTa"], B["NTb"]
                        for it in range(6):
                            mm(pd, N, X)
                            tt(X2, pd, X, ALU.add)
                            X, X2 = X2, X
                            if it < 5:
                                mm(pb, NTt, N)
                                mm(pc, N, NTt)
                                cp(N2, pb)
                                act(NT2, pc, AF.Copy)
                                N, N2, NTt, NT2 = N2, N, NT2, NTt
                        mm(pa, B["kbg"], X)
                        ts(B["nWT"], pa, -1.0, ALU.mult)
                        mm(pb, X, B["betaV"], start=True, stop=False)
                        mm(pb, B["nWT"], S, start=False, stop=True)
                        cp(B["Vnew"], pb)
                        mm(pc, S, B["qg"], start=True, stop=False)
                        mm(pc, B["Vnew"], B["Aqk"], start=False, stop=True)
                        act(oraw[:, blk], pc, AF.Copy)
                        mm(pd, B["Kd"], B["Vnew"])
                        stt(S, S, c1(eGL), pd, ALU.mult, ALU.add)
                    act(sq[:, :w], oraw[:, :w], AF.Square)
                    mm(PS[7][:, :w], ONES, sq[:, :w])
                    rstd_from_ssq(rn, PS[7], 1.0 / 128, w)
                    act(sq[:, :w], zt[:, :w], AF.Silu)
                    stt(o[:, :w], oraw[:, :w], sm(l, 2), rn[:, :w], ALU.mult, ALU.mult)
                    tt(o[:, :w], o[:, :w], sq[:, :w], ALU.mult)
                    st_OL(256 + h * 128, t0, w, o)

        YP = dram("YP", [4 * D, SL])
        YS = dram("YS", [D, SL])

        def outproj_phase(l):
            phase()
            otb = [alloc("ot%d" % i, 8 * 512) for i in range(2)]
            wt = [alloc("wt%d" % i, 8 * 128) for i in range(2)]
            ev = [alloc("ev%d" % i, 512) for i in range(2)]
            nw = 0
            for ti, (t0, w) in enumerate(tok_tiles(NT)):
                ot = otb[ti % 2]
                ot3 = ot[:, :8 * w].re("p (k t) -> p k t", t=w)
                ld(ot3, OL[:, t0:t0 + w].rearrange("(k p) t -> p k t", p=128), "OL")
                for j in range(KC):
                    w3 = wt[nw % 2].re("p (k c) -> p k c", c=128)
                    for half in range(2):
                        T = l * KC + j
                        base = (T // 4) * 1024 + half * 512 + (T % 4) * 128
                        ld(w3[:, half * 4:(half + 1) * 4, :],
                           WOUT[base:base + 128, :].rearrange("p (k c) -> p k c", c=128), "WOUT#%d" % (T // 4))
                    e = ev[nw % 2]
                    nw += 1
                    ps = PS[j % 2]
                    for lc in range(8):
                        mm(ps[:, :w], w3[:, lc, :], ot3[:, lc, :], start=(lc == 0), stop=(lc == 7))
                    if nw % 2:
                        cp(e[:, :w], ps[:, :w])
                    else:
                        act(e[:, :w], ps[:, :w], AF.Copy)
                    t = t0
                    while t < t0 + w:
                        s = t // SL
                        te = min(t0 + w, (s + 1) * SL)
                        st(YP[j * 512 + s * 128:j * 512 + (s + 1) * 128, t - s * SL:te - s * SL], "YP#%d" % j, e[:, t - t0:te - t0])
                        t = te
            for j in range(KC):
                P.dma("pool", lambda e, j=j: e.collective_compute(
                    "ReduceScatter", ALU.add, replica_groups=quads, ins=[YP[j * 512:(j + 1) * 512, :].opt()],
                    outs=[YS[j * 128:(j + 1) * 128, :].opt()]),
                    reads=["YP#%d" % j], writes=["YS#%d" % j], chan="cc_rs_y")

        hsrc_ap, hsrc_key = hT_in, None
        for l in range(L + 1):
            if cfg.stop == 0:
                break
            last = (l == L)
            hd = HB[l % 2]
            norm_phase(l, hsrc_ap, hsrc_key, l > 0, hd, "HB%d" % (l % 2),
                       out_ext if last else XNS, "OUT" if last else "XNS")
            if l > 0:
                hsrc_ap, hsrc_key = hd, "HB%d" % (l % 2)
            if last or cfg.stop == 1:
                break
            for i in range(2 * KC):
                allgather(XN[i * 256:(i + 1) * 256, :], "XN#%d" % i, XNS[i * 64:(i + 1) * 64, :], "XNS#%d" % (i // 2),
                          quads, "cc_xn")
            inproj_phase(l)
            if cfg.debug and l == 0:
                d2d(dbg["ufm"], "dbgufm", UFM, "UFM", "c_dbg1")
                d2d(dbg["utm"], "dbgutm", UTM, "UTM", "c_dbg2")
            if cfg.stop == 2:
                break
            if "sb" in cfg.mixers:
                sb_phase(l)
            if "gla" in cfg.mixers:
                gla_phase(l)
            if "gdn" in cfg.mixers:
                gdn_phase(l)
            if cfg.debug and l == 0:
                d2d(dbg["ol"], "dbgol", OL, "OL", "c_dbg3")
            outproj_phase(l)
        P.barrier()
        P.emit()
    return nc


def prep_inputs(cfg, x, meta, norm_g, w_in, gla_w_gate, gla_b_gate, gla_norm_g, gdn_conv_w,
                gdn_a_log, gdn_dt_bias, gdn_norm_g, sb_norm_g, w_out, final_g):
    D, KC, NT, SL, L = cfg.D, cfg.KC, cfg.NT, cfg.SL, cfg.depth
    KH = KC // 2
    NSM = 64
    f32 = np.float32
    consts = make_consts()
    normg = np.zeros((128, L + 1, KC), f32)
    for l in range(L):
        normg[:, l, :] = np.asarray(norm_g[l], f32).reshape(KC, 128).T
    normg[:, L, :] = np.asarray(final_g, f32).reshape(KC, 128).T
    in_maps = []
    wg_cache = {}
    for c in range(8):
        b, g = c // 4, c % 4
        h0 = np.zeros((NT, D), f32)
        h0[112:128] = meta
        h0[128:] = x[b]
        hT = np.ascontiguousarray(h0[g * SL:(g + 1) * SL].T)
        if g not in wg_cache:
            tiles = col_tiles(g)
            cols = np.concatenate(tiles)
            wl = []
            for l in range(L):
                wsel = np.where(cols[None, :] >= 0, np.asarray(w_in[l])[:, np.maximum(cols, 0)], 0.0).astype(f32)
                w4 = wsel.reshape(KC, 128, NCT, 128).transpose(2, 1, 0, 3)
                wl.append(w4)
            wg_cache[g] = np.stack(wl)
        wfull = wg_cache[g]
        win = np.ascontiguousarray(wfull[:, :, :, b * KH:(b + 1) * KH, :]).reshape(L * NCT * 128, KH * 128)
        rows = np.concatenate([g * 256 + np.arange(256), 1024 + 3 * g * 128 + np.arange(384),
                               2560 + 3 * g * 128 + np.arange(384)])
        wo = []
        for l in range(L):
            wp = np.asarray(w_out[l], f32)[rows]
            w4 = wp.reshape(8, 128, KC, 128).transpose(2, 1, 0, 3)
            wo.append(w4[:, :, b * 4:(b + 1) * 4, :])
        wout = np.ascontiguousarray(np.stack(wo)).reshape(L * KC * 128, 4 * 128)
        small = np.zeros((128, L, NSM), f32)
        for l in range(L):
            small[:, l, 0] = gla_norm_g[l][0:128]
            small[:, l, 1] = gla_norm_g[l][128:256]
            small[:, l, 2] = gdn_norm_g[l]
            small[:, l, 3] = sb_norm_g[l]
            for h in range(3):
                small[:, l, 4 + h] = gdn_a_log[l][3 * g + h]
                small[:, l, 7 + h] = gdn_dt_bias[l][3 * g + h]
            for j3 in range(3):
                for h in range(3):
                    ch = j3 * 1536 + (3 * g + h) * 128 + np.arange(128)
                    for i in range(4):
                        small[:, l, 10 + (j3 * 3 + h) * 4 + i] = gdn_conv_w[l][i, ch]
        glawg = np.ascontiguousarray(np.asarray(gla_w_gate, f32)[:L, :, g * 128:(g + 1) * 128].transpose(1, 0, 2))
        glabg = np.ascontiguousarray(np.asarray(gla_b_gate, f32)[:L, g * 128:(g + 1) * 128][None])
        in_maps.append({"hT": hT, "consts": consts, "normg": normg, "win": win, "wout": wout,
                        "small": small, "glawg": glawg, "glabg": glabg})
    return in_maps


def run(cfg, **inputs):
    inputs = {k: np.asarray(v) for k, v in inputs.items()}
    in_maps = prep_inputs(cfg, **inputs)
    nc = build(cfg)
    res = run_bass_kernel_spmd(nc, in_maps, core_ids=list(range(8)))
    D, NT, SL = cfg.D, cfg.NT, cfg.SL
    out = np.zeros((2, NT, D), np.float32)
    for c in range(8):
        b, g = c // 4, c % 4
        out[b, g * SL:(g + 1) * SL, :] = res.results[c]["outT"].T
    return np.ascontiguousarray(out[:, 128:, :]), res


def kernel(**inputs):
    out, _ = run(Cfg(), **inputs)
    return out
```

```python
import numpy as np
import concourse.bass as bass
import concourse.mybir as mybir
from concourse.bass_utils import run_bass_kernel_spmd
from contextlib import ExitStack

F32 = mybir.dt.float32
AF = mybir.ActivationFunctionType
ALU = mybir.AluOpType

EPOCH = 20000
DEPOCH = 1000
SAME_ENGINE_SYNC = True
EPS = 1e-6


class V:
    def __init__(self, ap, key):
        self.ap = ap
        self.key = key

    def __getitem__(self, idx):
        return V(self.ap[idx], self.key)

    def re(self, pat, **kw):
        return V(self.ap.rearrange(pat, **kw), self.key)


class Prog:
    ENGS = ("pe", "act", "dve", "pool", "sp")

    def __init__(self, nc, es):
        self.nc = nc
        self.es = es
        self.ops = []
        self.cnt = {e: 0 for e in self.ENGS}
        self.last_w = {}
        self.readers = {}
        self.chan_cnt = {}
        self.sems = {}

    def _deps(self, reads, writes):
        deps = []
        for k in reads:
            t = self.last_w.get(k)
            if t is not None:
                deps.append((t, True))
        for k in writes:
            t = self.last_w.get(k)
            if t is not None:
                deps.append((t, False))
            deps.extend((r, False) for r in self.readers.get(k, []))
        return deps

    def _commit(self, tok, reads, writes):
        for k in reads:
            self.readers.setdefault(k, []).append(tok)
        for k in writes:
            self.last_w[k] = tok
            self.readers[k] = []

    def op(self, eng, fn, reads=(), writes=()):
        reads = [k for k in reads if k is not None]
        writes = list(writes) + [k for k in reads if k.startswith("ps") and k not in writes]
        deps = self._deps(reads, writes)
        self.cnt[eng] += 1
        tok = ("c", eng, self.cnt[eng])
        self.ops.append((eng, fn, deps, tok))
        self._commit(tok, reads, writes)
        return tok

    def dma(self, eng, fn, reads=(), writes=(), chan=None):
        deps = self._deps(reads, writes)
        c = self.chan_cnt.get(chan, 0) + 1
        self.chan_cnt[chan] = c
        tok = ("d", chan, c)
        self.ops.append((eng, fn, deps, tok))
        self._commit(tok, reads, writes)
        return tok

    def barrier(self):
        keep = lambda k: k.startswith("WIN#") or k.startswith("WOUT#")
        deps = [(("c", e, n), True) for e, n in self.cnt.items() if n > 0]
        deps += [(("d", ch, n), True) for ch, n in self.chan_cnt.items() if not ch.startswith("cc_w")]
        for e in self.ENGS:
            self.ops.append((e, None, list(deps), None))
        self.last_w = {k: v for k, v in self.last_w.items() if keep(k)}
        self.readers = {k: v for k, v in self.readers.items() if keep(k)}

    def _tok_sem(self, tok):
        kind, who, n = tok
        if kind == "c":
            return ("c_%s_%d" % (who, (n - 1) // EPOCH), (n - 1) % EPOCH + 1)
        return ("d_%s_%d" % (who, (n - 1) // DEPOCH), (1 if who.startswith("cc") else 16) * ((n - 1) % DEPOCH + 1))

    def emit(self):
        nc = self.nc
        for eng, fn, deps, tok in self.ops:
            if tok is not None:
                s = self._tok_sem(tok)[0]
                if s not in self.sems:
                    self.sems[s] = self.es.enter_context(nc.semaphore(s))
        EI = {e: i for i, e in enumerate(self.ENGS)}
        know_c = {e: [0] * len(self.ENGS) for e in self.ENGS}
        know_d = {e: {} for e in self.ENGS}
        clock = {}
        plan = []
        for eng, fn, deps, tok in self.ops:
            kc, kd = know_c[eng], know_d[eng]
            need = []
            for d, is_raw in deps:
                if d[0] == "c":
                    if d[1] == eng:
                        if eng == "pe" or not SAME_ENGINE_SYNC or not is_raw:
                            continue
                        need.append(d)
                        continue
                    if kc[EI[d[1]]] >= d[2]:
                        continue
                else:
                    if kd.get(d[1], 0) >= d[2]:
                        continue
                need.append(d)
            best = {}
            for d in need:
                key = (d[0], d[1])
                if key not in best or best[key][2] < d[2]:
                    best[key] = d
            waits = []
            newd = None
            for d in best.values():
                waits.append(self._tok_sem(d))
                if d[0] == "c" and d[1] == eng:
                    continue
                ck = clock.get(d)
                if d[0] == "c":
                    if kc[EI[d[1]]] < d[2]:
                        kc[EI[d[1]]] = d[2]
                else:
                    if kd.get(d[1], 0) < d[2]:
                        if newd is None:
                            newd = dict(kd)
                        newd[d[1]] = d[2]
                if ck is not None:
                    cc, cd = ck
                    for i in range(len(cc)):
                        if kc[i] < cc[i]:
                            kc[i] = cc[i]
                    for ch, n in cd.items():
                        if (newd if newd is not None else kd).get(ch, 0) < n:
                            if newd is None:
                                newd = dict(kd)
                            newd[ch] = n
            if newd is not None:
                know_d[eng] = newd
            plan.append(waits)
            if tok is not None:
                clock[tok] = (tuple(kc), know_d[eng])
        per_eng = {e: [] for e in self.ENGS}
        for o, waits in zip(self.ops, plan):
            per_eng[o[0]].append((o, waits))
        with nc.Block() as block:
            def make(e):
                def body(engh):
                    for (eng, fn, deps, tok), waits in per_eng[e]:
                        for s, v in waits:
                            engh.wait_ge(self.sems[s], v)
                        if fn is None:
                            continue
                        ins = fn(engh)
                        s, v = self._tok_sem(tok)
                        ins.then_inc(self.sems[s], 16 if (tok[0] == "d" and not tok[1].startswith("cc")) else 1)
                return body
            for e, deco in (("pe", block.tensor), ("act", block.scalar), ("dve", block.vector),
                            ("pool", block.gpsimd), ("sp", block.sync)):
                if per_eng[e]:
                    deco(make(e))


class Cfg:
    def __init__(self, D=4096, NB=65, depth=2, debug=False, mixers=("sb", "gla", "gdn")):
        self.mixers = mixers
        self.stop = 99
        self.D = D
        self.KC = D // 128
        self.NB = NB
        self.NT = NB * 128
        assert self.NT % 4 == 0
        self.SL = self.NT // 4
        self.depth = depth
        self.debug = debug
        self.DMIX = 4096
        self.CC = 32


FM_GLA_Q, FM_GLA_K, FM_GLA_R0, FM_GLA_R1, FM_GLA_LR = 0, 1, 2, 3, 4
FM_GDN_Q, FM_GDN_K, FM_GDN_V, FM_GDN_Z = 5, 8, 11, 14
FM_SB_Q, FM_SB_K, FM_SB_G = 17, 20, 23
NFM = 26
TM_GLA_K, TM_GLA_V0, TM_GLA_V1, TM_GDN_BA, TM_SB_V = 0, 1, 2, 3, 4
NTM = 7
NCT = NFM + NTM

_OFF = {}
_o = 0
for _n, _w in (("gq", 512), ("gk", 512), ("gv", 1024), ("gr", 1024), ("glr", 16),
               ("dq", 1536), ("dk", 1536), ("dv", 1536), ("dz", 1536), ("db", 12), ("da", 12),
               ("sq", 1536), ("sk", 1536), ("sv", 1536), ("sg", 1536)):
    _OFF[_n] = _o
    _o += _w
D_IN = _o


def col_tiles(g):
    def rng(name, start, n):
        a = np.full(128, -1, np.int64)
        a[:n] = _OFF[name] + start + np.arange(n)
        return a
    fm = [None] * NFM
    fm[FM_GLA_Q] = rng("gq", g * 128, 128)
    fm[FM_GLA_K] = rng("gk", g * 128, 128)
    fm[FM_GLA_R0] = rng("gr", g * 256, 128)
    fm[FM_GLA_R1] = rng("gr", g * 256 + 128, 128)
    fm[FM_GLA_LR] = rng("glr", 0, 16)
    for h in range(3):
        hh = 3 * g + h
        fm[FM_GDN_Q + h] = rng("dq", hh * 128, 128)
        fm[FM_GDN_K + h] = rng("dk", hh * 128, 128)
        fm[FM_GDN_V + h] = rng("dv", hh * 128, 128)
        fm[FM_GDN_Z + h] = rng("dz", hh * 128, 128)
        fm[FM_SB_Q + h] = rng("sq", hh * 128, 128)
        fm[FM_SB_K + h] = rng("sk", hh * 128, 128)
        fm[FM_SB_G + h] = rng("sg", hh * 128, 128)
    tm = [None] * NTM
    tm[TM_GLA_K] = rng("gk", g * 128, 128)
    tm[TM_GLA_V0] = rng("gv", g * 256, 128)
    tm[TM_GLA_V1] = rng("gv", g * 256 + 128, 128)
    ba = np.full(128, -1, np.int64)
    ba[0:3] = _OFF["db"] + 3 * g + np.arange(3)
    ba[3:6] = _OFF["da"] + 3 * g + np.arange(3)
    tm[TM_GDN_BA] = ba
    for h in range(3):
        tm[TM_SB_V + h] = rng("sv", (3 * g + h) * 128, 128)
    return fm + tm


C_ID, C_ONES, C_UI, C_TRI, C_SFX, C_MUT, C_MSL, C_TRI1 = 0, 128, 256, 384, 512, 640, 768, 896
C_SBM = 1024
C_VALID = C_SBM + 4 * 512
C_ONE = C_VALID + 1
C_EPS = C_ONE + 1
C_LNS = C_EPS + 1
NCONST = C_LNS + 1


def make_consts():
    c = np.zeros((128, NCONST), np.float32)
    j = np.arange(128)[:, None]
    f = np.arange(128)[None, :]
    c[:, C_ID:C_ID + 128] = (j == f)
    c[:, C_ONES:C_ONES + 128] = 1.0
    c[:, C_UI:C_UI + 128] = (j >= f)
    c[:, C_TRI:C_TRI + 128] = (j <= f) * (-1.0 / 16.0)
    c[:, C_SFX:C_SFX + 128] = (j > f) * (-1.0 / 16.0)
    c[:, C_MUT:C_MUT + 128] = (j <= f)
    c[:, C_MSL:C_MSL + 128] = (f < j)
    c[:, C_TRI1:C_TRI1 + 128] = (j <= f)
    t = np.arange(512)[None, :]
    for m in range(4):
        c[:, C_SBM + m * 512:C_SBM + (m + 1) * 512] = ((m * 128 + j) < t)
    c[:, C_VALID] = (np.arange(128) >= 112)
    c[:, C_ONE] = 1.0
    c[:, C_EPS] = EPS
    c[:, C_LNS] = np.log(128.0 ** -0.5)
    return c


ARENA = 47000


def build(cfg):
    nc = bass.Bass("TRN2", target_bir_lowering=False)
    D, KC, NB, NT, SL, L = cfg.D, cfg.KC, cfg.NB, cfg.NT, cfg.SL, cfg.depth
    KH = KC // 2
    dram_in = lambda name, shape: nc.dram_tensor(name, shape, F32, kind="ExternalInput").ap()
    dram_out = lambda name, shape: nc.dram_tensor(name, shape, F32, kind="ExternalOutput").ap()
    dram = lambda name, shape: nc.dram_tensor(name, shape, F32).ap()

    hT_in = dram_in("hT", [D, SL])
    consts_in = dram_in("consts", [128, NCONST])
    normg_in = dram_in("normg", [128, L + 1, KC])
    win_in = dram_in("win", [L * NCT * 128, KH * 128])
    wout_in = dram_in("wout", [L * KC * 128, 4 * 128])
    NSM = 64
    small_in = dram_in("small", [128, L, NSM])
    glawg_in = dram_in("glawg", [16, L, 128])
    glabg_in = dram_in("glabg", [1, L, 128])
    out_ext = dram_out("outT", [D, SL])

    win_b = dram("win_b", [L * NCT * 128, KH * 128])
    WIN = dram("WIN", [2 * L * NCT * 128, KH * 128])
    wout_b = dram("wout_b", [L * KC * 128, 4 * 128])
    WOUT = dram("WOUT", [2 * L * KC * 128, 4 * 128])
    XNS = dram("XNS", [D, SL])
    XN = dram("XN", [4 * D, SL])
    UFM = dram("UFM", [NFM * 128, NT])
    UTM = dram("UTM", [NTM * NT, 128])
    OL = dram("OL", [1024, NT])
    HB = [dram("HB%d" % i, [D, SL]) for i in range(2)]
    dbg = {}
    if cfg.debug:
        dbg["ufm"] = dram_out("dbg_ufm", [NFM * 128, NT])
        dbg["utm"] = dram_out("dbg_utm", [NTM * NT, 128])
        dbg["ol"] = dram_out("dbg_ol", [1024, NT])

    with ExitStack() as es:
        P = Prog(nc, es)
        arena_t = es.enter_context(nc.sbuf_tensor("arena", [128, ARENA], F32))
        const_t = es.enter_context(nc.sbuf_tensor("constt", [128, NCONST], F32))
        ng_t = es.enter_context(nc.sbuf_tensor("ngt", [128, (L + 1) * KC], F32))
        small_t = es.enter_context(nc.sbuf_tensor("smallt", [128, L * NSM], F32))
        PS = [V(es.enter_context(nc.psum_tensor("ps%d" % i, [128, 512], F32))[:], "ps%d" % i)
              for i in range(8)]
        CONST = V(const_t[:], "const")
        NG = V(ng_t[:], "ng")
        SMALL = V(small_t[:], "small")
        state = {"off": 0, "phase": 0}

        def phase():
            P.barrier()
            state["off"] = 0
            state["phase"] += 1

        def alloc(name, n):
            off = state["off"]
            assert off + n <= ARENA, (name, off, n)
            state["off"] = off + n
            return V(arena_t[:, off:off + n], "%s@%d" % (name, state["phase"]))

        cst = lambda c0, n=128: CONST[:, c0:c0 + n]
        IDENT, ONES = cst(C_ID), cst(C_ONES)
        ONE1, EPS1, LNS1 = cst(C_ONE, 1), cst(C_EPS, 1), cst(C_LNS, 1)

        def ld(out, in_ap, rkey=None, eng="sp"):
            rk = [] if rkey is None else (list(rkey) if isinstance(rkey, (list, tuple)) else [rkey])
            P.dma(eng, lambda e: e.dma_start(out=out.ap, in_=in_ap), reads=rk,
                  writes=[out.key], chan="L" + out.key)

        def st(out_ap, wkey, in_, eng="pool"):
            P.dma(eng, lambda e: e.dma_start(out=out_ap, in_=in_.ap), reads=[in_.key],
                  writes=[wkey] if wkey else [], chan="S" + in_.key)

        def d2d(out_ap, wkey, in_ap, rkey, chan, eng="sp"):
            nrows = out_ap.shape[0]
            for r0 in range(0, nrows, 1024):
                r1 = min(nrows, r0 + 1024)
                P.dma(eng, lambda e, r0=r0, r1=r1: e.dma_start(out=out_ap[r0:r1, :], in_=in_ap[r0:r1, :]),
                      reads=[rkey] if rkey else [], writes=[wkey], chan=chan)

        def allgather(out_ap, wkey, in_ap, rkey, groups, chan):
            P.dma("pool", lambda e: e.collective_compute(
                "AllGather", ALU.bypass, replica_groups=groups, ins=[in_ap.opt()], outs=[out_ap.opt()]),
                reads=[rkey], writes=[wkey], chan=chan)

        def mm(out, lhsT, rhs, start=True, stop=True):
            P.op("pe", lambda e: e.matmul(out.ap, lhsT=lhsT.ap, rhs=rhs.ap, start=start, stop=stop),
                 reads=[lhsT.key, rhs.key], writes=[out.key])

        def tr(out, in_):
            P.op("pe", lambda e: e.transpose(out.ap, in_.ap, IDENT.ap),
                 reads=[in_.key, "const"], writes=[out.key])

        def act(out, in_, func, bias=None, scale=None, eng="act"):
            kw = {}
            rk = [in_.key]
            if bias is not None:
                if isinstance(bias, V):
                    kw["bias"] = bias.ap
                    rk.append(bias.key)
                else:
                    kw["bias"] = bias
            if scale is not None:
                if isinstance(scale, V):
                    kw["scale"] = scale.ap
                    rk.append(scale.key)
                else:
                    kw["scale"] = scale
            P.op("act", lambda e: e.activation(out=out.ap, in_=in_.ap, func=func, **kw),
                 reads=rk, writes=[out.key])

        def tt(out, a, b, op, eng="dve"):
            P.op(eng, lambda e: e.tensor_tensor(out=out.ap, in0=a.ap, in1=b.ap, op=op),
                 reads=[a.key, b.key], writes=[out.key])

        def ts(out, a, s1, op0, s2=None, op1=None, eng="dve"):
            rk = [a.key]
            v1 = s1
            if isinstance(s1, V):
                v1 = s1.ap
                rk.append(s1.key)
            v2 = s2
            if isinstance(s2, V):
                v2 = s2.ap
                rk.append(s2.key)
            if op1 is None:
                P.op(eng, lambda e: e.tensor_scalar(out=out.ap, in0=a.ap, scalar1=v1, scalar2=None, op0=op0),
                     reads=rk, writes=[out.key])
            else:
                P.op(eng, lambda e: e.tensor_scalar(out=out.ap, in0=a.ap, scalar1=v1, scalar2=v2,
                                                    op0=op0, op1=op1), reads=rk, writes=[out.key])

        def stt(out, a, s, b, op0, op1):
            rk = [a.key, b.key]
            sv = s
            if isinstance(s, V):
                sv = s.ap
                rk.append(s.key)
            P.op("dve", lambda e: e.scalar_tensor_tensor(out=out.ap, in0=a.ap, scalar=sv, in1=b.ap,
                                                         op0=op0, op1=op1), reads=rk, writes=[out.key])

        def cp(out, in_, eng="dve"):
            P.op(eng, lambda e: e.tensor_copy(out=out.ap, in_=in_.ap), reads=[in_.key], writes=[out.key])

        def memset(out, val, eng="dve"):
            P.op(eng, lambda e: e.memset(out.ap, val), writes=[out.key])

        def recip(out, in_):
            P.op("dve", lambda e: e.reciprocal(out=out.ap, in_=in_.ap), reads=[in_.key], writes=[out.key])

        def rstd_from_ssq(dst, ssq_ps, inv_n, w):
            act(dst[:, :w], ssq_ps[:, :w], AF.Sqrt, bias=EPS1, scale=inv_n)
            recip(dst[:, :w], dst[:, :w])

        ld(CONST, consts_in)
        ld(NG, normg_in.rearrange("p l k -> p (l k)"))
        ld(SMALL, small_in.rearrange("p l k -> p (l k)"))
        d2d(win_b, "win_b", win_in, None, "c_win")
        d2d(wout_b, "wout_b", wout_in, None, "c_wout")
        pairs = [[0, 4], [1, 5], [2, 6], [3, 7]]
        quads = [[0, 1, 2, 3], [4, 5, 6, 7]]
        for T in range(L * NCT):
            allgather(WIN[T * 256:(T + 1) * 256, :], "WIN#%d" % T, win_b[T * 128:(T + 1) * 128, :], "win_b", pairs, "cc_win")
        for q in range(L * KC // 4):
            allgather(WOUT[q * 1024:(q + 1) * 1024, :], "WOUT#%d" % q, wout_b[q * 512:(q + 1) * 512, :], "wout_b", pairs, "cc_wout")

        def tok_tiles(n, w=512):
            return [(t0, min(w, n - t0)) for t0 in range(0, n, w)]

        def norm_phase(l, src_ap, src_key, add_y, hdst_ap, hdst_key, dst_ap, dst_key):
            phase()
            W_ = 256
            h = alloc("h", KC * W_)
            y = alloc("y", KC * W_)
            sq = [alloc("sq%d" % i, W_) for i in range(2)]
            r = alloc("r", W_)
            xo = [alloc("xo%d" % i, W_) for i in range(2)]
            for (t0, w) in tok_tiles(SL, W_):
                h3 = h[:, :KC * w].re("p (k t) -> p k t", t=w)
                ld_chunks(h3, src_ap[:, t0:t0 + w].rearrange("(k p) t -> p k t", p=128), src_key, KC, 8)
                if add_y:
                    y3 = y[:, :KC * w].re("p (k t) -> p k t", t=w)
                    ysrc = YS[:, t0:t0 + w].rearrange("(k p) t -> p k t", p=128)
                    for k0 in range(0, KC, 8):
                        k1 = min(KC, k0 + 8)
                        ld(y3[:, k0:k1, :], ysrc[:, k0:k1, :], ["YS#%d" % k for k in range(k0, k1)])
                    tt(h[:, :KC * w], h[:, :KC * w], y[:, :KC * w], ALU.add)
                    for k0 in range(0, KC, 8):
                        k1 = min(KC, k0 + 8)
                        st(hdst_ap[k0 * 128:k1 * 128, t0:t0 + w].rearrange("(k p) t -> p k t", p=128), hdst_key,
                           h3[:, k0:k1, :])
                for kc in range(KC):
                    s = sq[kc % 2]
                    act(s[:, :w], h3[:, kc, :], AF.Square)
                    mm(PS[0][:, :w], ONES, s[:, :w], start=(kc == 0), stop=(kc == KC - 1))
                rstd_from_ssq(r, PS[0], 1.0 / D, w)
                for kc in range(KC):
                    x = xo[kc % 2]
                    stt(x[:, :w], h3[:, kc, :], NG[:, l * KC + kc:l * KC + kc + 1], r[:, :w], ALU.mult, ALU.mult)
                    st(dst_ap[kc * 128:(kc + 1) * 128, t0:t0 + w], "%s#%d" % (dst_key, kc), x[:, :w])

        def inproj_phase(l):
            phase()
            xnb = [alloc("xn%d" % i, KC * 512) for i in range(2)]
            wt = [alloc("wt%d" % i, KC * 128) for i in range(2)]
            ev = [alloc("ev%d" % i, 512) for i in range(2)]
            ev2 = [alloc("ev2%d" % i, 512) for i in range(2)]
            nw = 0
            ne = 0
            tiles = tok_tiles(NT)

            def load_xn(ti):
                t0, w = tiles[ti]
                xn3 = xnb[ti % 2][:, :KC * w].re("p (k t) -> p k t", t=w)
                t = t0
                while t < t0 + w:
                    rk = t // SL
                    te = min(t0 + w, (rk + 1) * SL)
                    for ph in range(2):
                        srcv = XN.rearrange("(k ph r pl) t -> ph r pl k t", ph=2, r=4, pl=64)[ph, rk]
                        for k0 in range(0, KC, 8):
                            k1 = min(KC, k0 + 8)
                            ld(xn3[ph * 64:(ph + 1) * 64, k0:k1, t - t0:te - t0],
                               srcv[:, k0:k1, t - rk * SL:te - rk * SL],
                               ["XN#%d" % (2 * k + ph) for k in range(k0, k1)])
                    t = te

            load_xn(0)
            for ti, (t0, w) in enumerate(tiles):
                xn = xnb[ti % 2]
                xn3 = xn[:, :KC * w].re("p (k t) -> p k t", t=w)
                for ct in range(NCT):
                    if ct == 3 and ti + 1 < len(tiles):
                        load_xn(ti + 1)
                    if ct == NFM + TM_GLA_K:
                        continue
                    wtile = wt[nw % 2]
                    nw += 1
                    w3 = wtile.re("p (k c) -> p k c", c=128)
                    for half in range(2):
                        T = l * NCT + ct
                        base = (T * 2 + half) * 128
                        ld(w3[:, half * KH:(half + 1) * KH, :],
                           WIN[base:base + 128, :].rearrange("p (k c) -> p k c", c=128), "WIN#%d" % T)
                    if ct < NFM:
                        ps = PS[ct % 2]
                        for kc in range(KC):
                            mm(ps[:, :w], w3[:, kc, :], xn3[:, kc, :], start=(kc == 0), stop=(kc == KC - 1))
                        e = ev[ne % 2]
                        ne += 1
                        if ne % 2:
                            cp(e[:, :w], ps[:, :w], eng="dve")
                        else:
                            act(e[:, :w], ps[:, :w], AF.Copy)
                        st(UFM[ct * 128:(ct + 1) * 128, t0:t0 + w], "UFM", e[:, :w])
                        if ct == FM_GLA_K:
                            ps2 = PS[2]
                            for bi in range(w // 128):
                                tr(ps2[:, bi * 128:(bi + 1) * 128], e[:, bi * 128:(bi + 1) * 128])
                            e2 = ev2[0]
                            cp(e2[:, :w], ps2[:, :w], eng="dve")
                            st(UTM[TM_GLA_K * NT + t0:TM_GLA_K * NT + t0 + w, :].rearrange("(b p) c -> p b c", p=128),
                               "UTM", e2[:, :w].re("p (b c) -> p b c", c=128))
                    else:
                        tmi = ct - NFM
                        nblk = w // 128
                        ps = PS[ct % 2]
                        for kc in range(KC):
                            mm(ps[:, :w], w3[:, kc, :], xn3[:, kc, :], start=(kc == 0), stop=(kc == KC - 1))
                        e = ev[ne % 2]
                        ne += 1
                        act(e[:, :w], ps[:, :w], AF.Copy)
                        ps2 = PS[2 + (tmi % 2)]
                        for bi in range(nblk):
                            tr(ps2[:, bi * 128:(bi + 1) * 128], e[:, bi * 128:(bi + 1) * 128])
                        e2 = ev2[tmi % 2]
                        cp(e2[:, :w], ps2[:, :w], eng="dve")
                        st(UTM[tmi * NT + t0:tmi * NT + t0 + w, :].rearrange("(b p) c -> p b c", p=128),
                           "UTM", e2[:, :w].re("p (b c) -> p b c", c=128))

        def ld_chunks(dst3, src3, key, n, step):
            for a in range(0, n, step):
                b = min(n, a + step)
                ld(dst3[:, a:b, :], src3[:, a:b, :], key)

        def st_OL(c0, t0, w, tile):
            st(OL[c0:c0 + 128, t0:t0 + w], "OL", tile[:, :w])

        SBM = [CONST[:, C_SBM + m * 512:C_SBM + (m + 1) * 512] for m in range(4)]
        VALID1 = cst(C_VALID, 1)
        UI, TRI, SFX, MUT, MSL, TRI1 = cst(C_UI), cst(C_TRI), cst(C_SFX), cst(C_MUT), cst(C_MSL), cst(C_TRI1)
        NSM_ = NSM

        def sm(l, idx):
            return SMALL[:, l * NSM_ + idx:l * NSM_ + idx + 1]

        def sb_phase(l):
            import os
            CUT = int(os.environ.get("CUT", "99"))
            phase()
            sc = 128.0 ** -0.5
            kT = alloc("kT", NT)
            vtm = alloc("vtm", NB * 128)
            vtm3 = vtm.re("p (b d) -> p b d", d=128)
            qs, nqs, S = alloc("qs", 512), alloc("nqs", 512), alloc("S", 512)
            E = [alloc("E%d" % i, 512) for i in range(2)]
            SP = [alloc("SP%d" % i, 512) for i in range(2)]
            A = [alloc("A%d" % i, 512) for i in range(2)]
            osb, sq, r, gt = alloc("osb", 512), alloc("sq", 512), alloc("r", 512), alloc("gt", 512)
            for h in range(3):
                ld(kT, UFM[(FM_SB_K + h) * 128:(FM_SB_K + h + 1) * 128, :], "UFM")
                ld_chunks(vtm3, UTM[(TM_SB_V + h) * NT:(TM_SB_V + h + 1) * NT, :].rearrange("(b p) d -> p b d", p=128),
                          "UTM", NB, 4)
                for (q0, w) in tok_tiles(NT):
                    qb0, nqb = q0 // 128, w // 128
                    ld(qs[:, :w], UFM[(FM_SB_Q + h) * 128:(FM_SB_Q + h + 1) * 128, q0:q0 + w], "UFM")
                    ld(gt[:, :w], UFM[(FM_SB_G + h) * 128:(FM_SB_G + h + 1) * 128, q0:q0 + w], "UFM")
                    ts(nqs[:, :w], qs[:, :w], -sc, ALU.mult)
                    ts(qs[:, :w], qs[:, :w], sc, ALU.mult)
                    kbs = list(range(qb0 + nqb - 1, -1, -1))
                    nu = len(kbs)

                    def stageA(i):
                        kb = kbs[i]
                        Zb, e, sp = PS[i % 2], E[i % 2], SP[i % 2]
                        kblk = kT[:, kb * 128:(kb + 1) * 128]
                        m = kb - qb0
                        mm(Zb[:, :w], kblk, qs[:, :w])
                        act(e[:, :w], Zb[:, :w], AF.Exp)
                        act(sp[:, :w], e[:, :w], AF.Ln, bias=ONE1)
                        if m >= 0:
                            tt(sp[:, :w], sp[:, :w], SBM[m][:, :w], ALU.mult)
                        if kb == 0:
                            ts(sp[:, :w], sp[:, :w], VALID1, ALU.mult)

                    def stageB(i):
                        kb = kbs[i]
                        Tb, sp, a, e = PS[2 + i % 2], SP[i % 2], A[i % 2], E[i % 2]
                        m = kb - qb0
                        first = (i == 0)
                        mm(Tb[:, :w], UI, sp[:, :w], start=True, stop=first)
                        if not first:
                            mm(Tb[:, :w], ONES, S[:, :w], start=False, stop=True)
                        act(a[:, :w], Tb[:, :w], AF.Exp, scale=-1.0)
                        tt(a[:, :w], a[:, :w], e[:, :w], ALU.mult)
                        if m >= 0:
                            tt(a[:, :w], a[:, :w], SBM[m][:, :w], ALU.mult)
                        if kb == 0:
                            ts(a[:, :w], a[:, :w], VALID1, ALU.mult)
                        if first:
                            cp(S[:, :w], sp[:, :w], eng="pool")
                        elif i < nu - 1:
                            tt(S[:, :w], S[:, :w], sp[:, :w], ALU.add, eng="pool")

                    def stageC(i):
                        kb = kbs[i]
                        mm(PS[4][:, :w], vtm3[:, kb, :], A[i % 2][:, :w], start=(i == 0), stop=(i == nu - 1))

                    stageA(0)
                    for i in range(nu):
                        if i + 1 < nu:
                            stageA(i + 1)
                        stageB(i)
                        if i >= 1:
                            stageC(i - 1)
                    stageC(nu - 1)
                    if CUT <= 4:
                        continue
                    cp(osb[:, :w], PS[4][:, :w])
                    act(sq[:, :w], PS[4][:, :w], AF.Square)
                    mm(PS[5][:, :w], ONES, sq[:, :w])
                    rstd_from_ssq(r, PS[5], 1.0 / 128, w)
                    act(sq[:, :w], gt[:, :w], AF.Silu)
                    stt(osb[:, :w], osb[:, :w], sm(l, 3), r[:, :w], ALU.mult, ALU.mult)
                    tt(osb[:, :w], osb[:, :w], sq[:, :w], ALU.mult)
                    if CUT <= 5:
                        continue
                    st_OL(5 * 128 + h * 128, q0, w, osb)

        def gla_phase(l):
            phase()
            S = alloc("S", 256)
            WG, BG = alloc("WG", 128), alloc("BG", 128)
            qT, kT, lr = alloc("qT", 512), alloc("kT", 512), alloc("lr", 512)
            rT = [alloc("r0", 512), alloc("r1", 512)]
            ktm, vtm = alloc("ktm", 512), alloc("vtm", 1024)
            ktm3 = ktm.re("p (b d) -> p b d", d=128)
            vtm3 = vtm.re("p (b d) -> p b d", d=256)
            e1, sp, bl, eBl = alloc("e1", 128), alloc("sp", 128), alloc("bl", 1), alloc("eBl", 1)
            E1, E2, E3 = alloc("E1", 128), alloc("E2", 128), alloc("E3", 128)
            qg, kg, kd, att = alloc("qg", 128), alloc("kg", 128), alloc("kd", 128), alloc("att", 128)
            oraw = [alloc("oraw0", 512), alloc("oraw1", 512)]
            sq, r, sg, o = alloc("sq", 512), alloc("r", 512), alloc("sg", 512), alloc("o", 512)
            memset(S, 0.0)
            memset(WG, 0.0)
            memset(BG, 0.0)
            ld(WG[0:16, :], glawg_in[:, l, :])
            ld(BG[0:1, :], glabg_in[:, l, :])
            fmrow = lambda ct, t0, w: UFM[ct * 128:(ct + 1) * 128, t0:t0 + w]
            for (t0, w) in tok_tiles(NT):
                nb = w // 128
                ld(qT[:, :w], fmrow(FM_GLA_Q, t0, w), "UFM")
                ld(kT[:, :w], fmrow(FM_GLA_K, t0, w), "UFM")
                ld(lr[:, :w], fmrow(FM_GLA_LR, t0, w), "UFM")
                ld(rT[0][:, :w], fmrow(FM_GLA_R0, t0, w), "UFM")
                ld(rT[1][:, :w], fmrow(FM_GLA_R1, t0, w), "UFM")
                ld(ktm3[:, :nb, :], UTM[TM_GLA_K * NT + t0:TM_GLA_K * NT + t0 + w, :].rearrange("(b p) d -> p b d", p=128), "UTM")
                for vc in range(2):
                    ld(vtm3[:, :nb, vc * 128:(vc + 1) * 128],
                       UTM[(TM_GLA_V0 + vc) * NT + t0:(TM_GLA_V0 + vc) * NT + t0 + w, :].rearrange("(b p) d -> p b d", p=128), "UTM")
                for bi in range(nb):
                    blk = slice(bi * 128, (bi + 1) * 128)
                    mm(PS[0][:, :128], lr[:, blk], WG, start=True, stop=False)
                    mm(PS[0][:, :128], ONES, BG, start=False, stop=True)
                    act(e1, PS[0][:, :128], AF.Exp, scale=-1.0)
                    act(sp, e1, AF.Ln, bias=ONE1)
                    mm(PS[1][:, :128], sp, TRI)
                    mm(PS[2][:, :128], SFX, sp)
                    cp(bl, PS[1][:, 127:128])
                    act(E1, PS[1][:, :128], AF.Exp, bias=LNS1)
                    act(E2, PS[1][:, :128], AF.Exp, scale=-1.0)
                    act(E3, PS[2][:, :128], AF.Exp)
                    act(eBl, bl, AF.Exp)
                    tt(qg, qT[:, blk], E1, ALU.mult)
                    tt(kg, kT[:, blk], E2, ALU.mult)
                    tt(kd, ktm3[:, bi, :], E3, ALU.mult)
                    mm(PS[3][:, :128], kg, qg)
                    tt(att, PS[3][:, :128], MUT, ALU.mult)
                    for vc in range(2):
                        mm(PS[4 + vc][:, :128], vtm3[:, bi, vc * 128:(vc + 1) * 128], att, start=True, stop=False)
                        mm(PS[4 + vc][:, :128], S[:, vc * 128:(vc + 1) * 128], qg, start=False, stop=True)
                        act(oraw[vc][:, blk], PS[4 + vc][:, :128], AF.Copy)
                    mm(PS[6][:, :256], kd, vtm3[:, bi, :])
                    stt(S, S, eBl, PS[6][:, :256], ALU.mult, ALU.add)
                    import os
                    if os.environ.get("GLARAW") == "2":
                        cp(oraw[0][:, blk], PS[1][:, :128])
                        tr(PS[3][:, :128], sp)
                        cp(oraw[1][:, blk], PS[3][:, :128])
                for vc in range(2):
                    act(sq[:, :w], oraw[vc][:, :w], AF.Square)
                    mm(PS[7][:, :w], ONES, sq[:, :w], start=(vc == 0), stop=(vc == 1))
                rstd_from_ssq(r, PS[7], 1.0 / 256, w)
                for vc in range(2):
                    act(sg[:, :w], rT[vc][:, :w], AF.Silu)
                    stt(o[:, :w], oraw[vc][:, :w], sm(l, vc), r[:, :w], ALU.mult, ALU.mult)
                    tt(o[:, :w], o[:, :w], sg[:, :w], ALU.mult)
                    import os
                    if os.environ.get("GLARAW"):
                        cp(o[:, :w], oraw[vc][:, :w])
                    st_OL(vc * 128, t0, w, o)

        def gdn_phase(l):
            phase()
            sc = 128.0 ** -0.5
            Sst = [alloc("S%d" % h, 128) for h in range(3)]
            ba = alloc("ba", 4 * 6)
            ba3 = ba.re("p (b c) -> p b c", c=6)
            DT4, EA4 = alloc("DT4", 12), alloc("EA4", 12)
            t1, gg, beta, G, GL, eG, eGlG, eGL, bG = [alloc("sc%d" % i, NB * 3) for i in range(9)]
            xin = [alloc("xin%d" % i, 515) for i in range(3)]
            acc = alloc("acc", 512)
            yq, yk, yv = alloc("yq", 512), alloc("yk", 512), alloc("yv", 512)
            sq, rn, zt = alloc("sq", 512), alloc("rn", 512), alloc("zt", 512)
            oraw, o = alloc("oraw", 512), alloc("o", 512)
            names = ["kbg", "Kd", "betaV", "gbc", "d1", "d2", "EGb", "M", "Aqk", "qg", "MT", "Xa", "Xb",
                     "Na", "Nb", "NTa", "NTb", "nWT", "Vnew"]
            B = {n: alloc(n, 128) for n in names}
            for h in range(3):
                memset(Sst[h], 0.0)
            for bi in range(4):
                for h in range(3):
                    cp(DT4[:, bi * 3 + h:bi * 3 + h + 1], sm(l, 7 + h))
                    act(EA4[:, bi * 3 + h:bi * 3 + h + 1], sm(l, 4 + h), AF.Exp)
            ts(EA4, EA4, -1.0, ALU.mult)
            for (t0, w) in tok_tiles(NT):
                nb = w // 128
                n3 = nb * 3
                c0 = (t0 // 128) * 3
                cs = slice(c0, c0 + n3)
                ld(ba3[:, :nb, :], UTM[TM_GDN_BA * NT + t0:TM_GDN_BA * NT + t0 + w, 0:6].rearrange("(b p) c -> p b c", p=128), "UTM")
                for bi in range(nb):
                    tt(t1[:, bi * 3:bi * 3 + 3], ba3[:, bi, 3:6], DT4[:, bi * 3:bi * 3 + 3], ALU.add)
                    act(beta[:, c0 + bi * 3:c0 + bi * 3 + 3], ba3[:, bi, 0:3], AF.Exp, scale=-1.0)
                act(t1[:, :n3], t1[:, :n3], AF.Exp)
                act(t1[:, :n3], t1[:, :n3], AF.Ln, bias=ONE1)
                tt(gg[:, cs], t1[:, :n3], EA4[:, :n3], ALU.mult)
                ts(beta[:, cs], beta[:, cs], 1.0, ALU.add)
                recip(beta[:, cs], beta[:, cs])
                for bi in range(nb):
                    c3 = slice(c0 + bi * 3, c0 + bi * 3 + 3)
                    mm(PS[6][:, bi * 3:bi * 3 + 3], TRI1, gg[:, c3])
                    mm(PS[6][:, 16 + bi * 3:16 + bi * 3 + 3], ONES, gg[:, c3])
                cp(G[:, cs], PS[6][:, 0:n3])
                cp(GL[:, cs], PS[6][:, 16:16 + n3])
                act(eG[:, cs], G[:, cs], AF.Exp)
                act(eGL[:, cs], GL[:, cs], AF.Exp)
                tt(eGlG[:, cs], GL[:, cs], G[:, cs], ALU.subtract)
                act(eGlG[:, cs], eGlG[:, cs], AF.Exp)
                tt(bG[:, cs], beta[:, cs], eG[:, cs], ALU.mult)
            for h in range(3):
                for (t0, w) in tok_tiles(NT):
                    nb = w // 128
                    ys = [yq, yk, yv]
                    for j3, (fm0, y) in enumerate(((FM_GDN_Q, yq), (FM_GDN_K, yk), (FM_GDN_V, yv))):
                        x = xin[j3]
                        ct = fm0 + h
                        if t0 == 0:
                            memset(x[:, 0:3], 0.0)
                            ld(x[:, 3:3 + w], UFM[ct * 128:(ct + 1) * 128, 0:w], "UFM")
                        else:
                            ld(x[:, 0:3 + w], UFM[ct * 128:(ct + 1) * 128, t0 - 3:t0 + w], "UFM")
                        cw = lambda i: sm(l, 10 + (j3 * 3 + h) * 4 + i)
                        ts(acc[:, :w], x[:, 3:3 + w], cw(3), ALU.mult)
                        for i in (2, 1, 0):
                            stt(acc[:, :w], x[:, i:i + w], cw(i), acc[:, :w], ALU.mult, ALU.add)
                        act(y[:, :w], acc[:, :w], AF.Silu)
                        if j3 < 2:
                            act(sq[:, :w], y[:, :w], AF.Square)
                            mm(PS[5][:, :w], ONES, sq[:, :w])
                            rstd_from_ssq(rn, PS[5], 1.0, w)
                            if j3 == 0:
                                stt(y[:, :w], y[:, :w], sc, rn[:, :w], ALU.mult, ALU.mult)
                            else:
                                tt(y[:, :w], y[:, :w], rn[:, :w], ALU.mult)
                    ld(zt[:, :w], UFM[(FM_GDN_Z + h) * 128:(FM_GDN_Z + h + 1) * 128, t0:t0 + w], "UFM")
                    S = Sst[h]
                    for bi in range(nb):
                        blk = slice(bi * 128, (bi + 1) * 128)
                        col = (t0 // 128 + bi) * 3 + h
                        c1 = lambda t: t[:, col:col + 1]
                        pa, pb, pc, pd, pe = [PS[i][:, :128] for i in range(5)]
                        tr(pa, yk[:, blk])
                        ts(B["kbg"], pa, c1(bG), ALU.mult)
                        ts(B["Kd"], pa, c1(eGlG), ALU.mult)
                        tr(pb, yv[:, blk])
                        ts(B["betaV"], pb, c1(beta), ALU.mult)
                        mm(pc, yk[:, blk], yk[:, blk])
                        mm(pd, yk[:, blk], yq[:, blk])
                        ts(B["gbc"], ONES, c1(gg), ALU.mult)
                        mm(pe, B["gbc"], TRI1)
                        ts(B["d1"], pe, c1(G), ALU.subtract, 0.0, ALU.max)
                        act(B["d1"], B["d1"], AF.Exp, scale=-1.0)
                        ts(B["d2"], pe, c1(G), ALU.subtract, 0.0, ALU.min)
                        act(B["d2"], B["d2"], AF.Exp)
                        act(B["EGb"], pe, AF.Exp)
                        tt(B["M"], pc, B["d1"], ALU.mult)
                        stt(B["M"], B["M"], c1(beta), MSL, ALU.mult, ALU.mult)
                        tt(B["Aqk"], pd, B["d2"], ALU.mult)
                        tt(B["Aqk"], B["Aqk"], MUT, ALU.mult)
                        tt(B["qg"], yq[:, blk], B["EGb"], ALU.mult)
                        tr(pa, B["M"])
                        act(B["MT"], pa, AF.Copy)
                        tt(B["Xa"], IDENT, pa, ALU.subtract)
                        mm(pb, B["MT"], B["M"])
                        mm(pc, B["M"], B["MT"])
                        cp(B["Na"], pb)
                        act(B["NTa"], pc, AF.Copy)
                        X, X2 = B["Xa"], B["Xb"]
                        N, N2, NTt, NT2 = B["Na"], B["Nb"], B["NTa"], B["NTb"]
                        for it in range(6):
                            mm(pd, N, X)
                            tt(X2, pd, X, ALU.add)
                            X, X2 = X2, X
                            if it < 5:
                                mm(pb, NTt, N)
                                mm(pc, N, NTt)
                                cp(N2, pb)
                                act(NT2, pc, AF.Copy)
                                N, N2, NTt, NT2 = N2, N, NT2, NTt
                        mm(pa, B["kbg"], X)
                        ts(B["nWT"], pa, -1.0, ALU.mult)
                        mm(pb, X, B["betaV"], start=True, stop=False)
                        mm(pb, B["nWT"], S, start=False, stop=True)
                        cp(B["Vnew"], pb)
                        mm(pc, S, B["qg"], start=True, stop=False)
                        mm(pc, B["Vnew"], B["Aqk"], start=False, stop=True)
                        act(oraw[:, blk], pc, AF.Copy)
                        mm(pd, B["Kd"], B["Vnew"])
                        stt(S, S, c1(eGL), pd, ALU.mult, ALU.add)
                    act(sq[:, :w], oraw[:, :w], AF.Square)
                    mm(PS[7][:, :w], ONES, sq[:, :w])
                    rstd_from_ssq(rn, PS[7], 1.0 / 128, w)
                    act(sq[:, :w], zt[:, :w], AF.Silu)
                    stt(o[:, :w], oraw[:, :w], sm(l, 2), rn[:, :w], ALU.mult, ALU.mult)
                    tt(o[:, :w], o[:, :w], sq[:, :w], ALU.mult)
                    st_OL(256 + h * 128, t0, w, o)

        YP = dram("YP", [4 * D, SL])
        YS = dram("YS", [D, SL])

        def outproj_phase(l):
            phase()
            otb = [alloc("ot%d" % i, 8 * 512) for i in range(2)]
            wt = [alloc("wt%d" % i, 8 * 128) for i in range(2)]
            ev = [alloc("ev%d" % i, 512) for i in range(2)]
            nw = 0
            for ti, (t0, w) in enumerate(tok_tiles(NT)):
                ot = otb[ti % 2]
                ot3 = ot[:, :8 * w].re("p (k t) -> p k t", t=w)
                ld(ot3, OL[:, t0:t0 + w].rearrange("(k p) t -> p k t", p=128), "OL")
                for j in range(KC):
                    w3 = wt[nw % 2].re("p (k c) -> p k c", c=128)
                    for half in range(2):
                        T = l * KC + j
                        base = (T // 4) * 1024 + half * 512 + (T % 4) * 128
                        ld(w3[:, half * 4:(half + 1) * 4, :],
                           WOUT[base:base + 128, :].rearrange("p (k c) -> p k c", c=128), "WOUT#%d" % (T // 4))
                    e = ev[nw % 2]
                    nw += 1
                    ps = PS[j % 2]
                    for lc in range(8):
                        mm(ps[:, :w], w3[:, lc, :], ot3[:, lc, :], start=(lc == 0), stop=(lc == 7))
                    if nw % 2:
                        cp(e[:, :w], ps[:, :w])
                    else:
                        act(e[:, :w], ps[:, :w], AF.Copy)
                    t = t0
                    while t < t0 + w:
                        s = t // SL
                        te = min(t0 + w, (s + 1) * SL)
                        st(YP[j * 512 + s * 128:j * 512 + (s + 1) * 128, t - s * SL:te - s * SL], "YP#%d" % j, e[:, t - t0:te - t0])
                        t = te
            for j in range(KC):
                P.dma("pool", lambda e, j=j: e.collective_compute(
                    "ReduceScatter", ALU.add, replica_groups=quads, ins=[YP[j * 512:(j + 1) * 512, :].opt()],
                    outs=[YS[j * 128:(j + 1) * 128, :].opt()]),
                    reads=["YP#%d" % j], writes=["YS#%d" % j], chan="cc_rs_y")

        hsrc_ap, hsrc_key = hT_in, None
        for l in range(L + 1):
            if cfg.stop == 0:
                break
            last = (l == L)
            hd = HB[l % 2]
            norm_phase(l, hsrc_ap, hsrc_key, l > 0, hd, "HB%d" % (l % 2),
                       out_ext if last else XNS, "OUT" if last else "XNS")
            if l > 0:
                hsrc_ap, hsrc_key = hd, "HB%d" % (l % 2)
            if last or cfg.stop == 1:
                break
            for i in range(2 * KC):
                allgather(XN[i * 256:(i + 1) * 256, :], "XN#%d" % i, XNS[i * 64:(i + 1) * 64, :], "XNS#%d" % (i // 2),
                          quads, "cc_xn")
            inproj_phase(l)
            if cfg.debug and l == 0:
                d2d(dbg["ufm"], "dbgufm", UFM, "UFM", "c_dbg1")
                d2d(dbg["utm"], "dbgutm", UTM, "UTM", "c_dbg2")
            if cfg.stop == 2:
                break
            if "sb" in cfg.mixers:
                sb_phase(l)
            if "gla" in cfg.mixers:
                gla_phase(l)
            if "gdn" in cfg.mixers:
                gdn_phase(l)
            if cfg.debug and l == 0:
                d2d(dbg["ol"], "dbgol", OL, "OL", "c_dbg3")
            outproj_phase(l)
        P.barrier()
        P.emit()
    return nc


def prep_inputs(cfg, x, meta, norm_g, w_in, gla_w_gate, gla_b_gate, gla_norm_g, gdn_conv_w,
                gdn_a_log, gdn_dt_bias, gdn_norm_g, sb_norm_g, w_out, final_g):
    D, KC, NT, SL, L = cfg.D, cfg.KC, cfg.NT, cfg.SL, cfg.depth
    KH = KC // 2
    NSM = 64
    f32 = np.float32
    consts = make_consts()
    normg = np.zeros((128, L + 1, KC), f32)
    for l in range(L):
        normg[:, l, :] = np.asarray(norm_g[l], f32).reshape(KC, 128).T
    normg[:, L, :] = np.asarray(final_g, f32).reshape(KC, 128).T
    in_maps = []
    wg_cache = {}
    for c in range(8):
        b, g = c // 4, c % 4
        h0 = np.zeros((NT, D), f32)
        h0[112:128] = meta
        h0[128:] = x[b]
        hT = np.ascontiguousarray(h0[g * SL:(g + 1) * SL].T)
        if g not in wg_cache:
            tiles = col_tiles(g)
            cols = np.concatenate(tiles)
            wl = []
            for l in range(L):
                wsel = np.where(cols[None, :] >= 0, np.asarray(w_in[l])[:, np.maximum(cols, 0)], 0.0).astype(f32)
                w4 = wsel.reshape(KC, 128, NCT, 128).transpose(2, 1, 0, 3)
                wl.append(w4)
            wg_cache[g] = np.stack(wl)
        wfull = wg_cache[g]
        win = np.ascontiguousarray(wfull[:, :, :, b * KH:(b + 1) * KH, :]).reshape(L * NCT * 128, KH * 128)
        rows = np.concatenate([g * 256 + np.arange(256), 1024 + 3 * g * 128 + np.arange(384),
                               2560 + 3 * g * 128 + np.arange(384)])
        wo = []
        for l in range(L):
            wp = np.asarray(w_out[l], f32)[rows]
            w4 = wp.reshape(8, 128, KC, 128).transpose(2, 1, 0, 3)
            wo.append(w4[:, :, b * 4:(b + 1) * 4, :])
        wout = np.ascontiguousarray(np.stack(wo)).reshape(L * KC * 128, 4 * 128)
        small = np.zeros((128, L, NSM), f32)
        for l in range(L):
            small[:, l, 0] = gla_norm_g[l][0:128]
            small[:, l, 1] = gla_norm_g[l][128:256]
            small[:, l, 2] = gdn_norm_g[l]
            small[:, l, 3] = sb_norm_g[l]
            for h in range(3):
                small[:, l, 4 + h] = gdn_a_log[l][3 * g + h]
                small[:, l, 7 + h] = gdn_dt_bias[l][3 * g + h]
            for j3 in range(3):
                for h in range(3):
                    ch = j3 * 1536 + (3 * g + h) * 128 + np.arange(128)
                    for i in range(4):
                        small[:, l, 10 + (j3 * 3 + h) * 4 + i] = gdn_conv_w[l][i, ch]
        glawg = np.ascontiguousarray(np.asarray(gla_w_gate, f32)[:L, :, g * 128:(g + 1) * 128].transpose(1, 0, 2))
        glabg = np.ascontiguousarray(np.asarray(gla_b_gate, f32)[:L, g * 128:(g + 1) * 128][None])
        in_maps.append({"hT": hT, "consts": consts, "normg": normg, "win": win, "wout": wout,
                        "small": small, "glawg": glawg, "glabg": glabg})
    return in_maps


def run(cfg, **inputs):
    inputs = {k: np.asarray(v) for k, v in inputs.items()}
    in_maps = prep_inputs(cfg, **inputs)
    nc = build(cfg)
    res = run_bass_kernel_spmd(nc, in_maps, core_ids=list(range(8)))
    D, NT, SL = cfg.D, cfg.NT, cfg.SL
    out = np.zeros((2, NT, D), np.float32)
    for c in range(8):
        b, g = c // 4, c % 4
        out[b, g * SL:(g + 1) * SL, :] = res.results[c]["outT"].T
    return np.ascontiguousarray(out[:, 128:, :]), res


def kernel(**inputs):
    out, _ = run(Cfg(), **inputs)
    return out
```
